# Optimizing a Trainium2 kernel written in Bass

```python
import jax, jax.numpy as jnp
from jax import lax
import numpy as np

D_MODEL = 1024
BATCH = 32
SEQ = 2048
DEPTH = 4

M_HEADS = 4
M_HEAD_DIM = 512
M_WIDTH = M_HEADS * M_HEAD_DIM
M_CHUNK = 64
CONV_WIDTH = 4
M_FORGET_BIAS_LO = 3.0
M_FORGET_BIAS_HI = 6.0
H_HEADS = 8
H_EXPAND = 128
H_VDIM = D_MODEL // H_HEADS
H_KWIDTH = H_HEADS * H_EXPAND
H_VWIDTH = H_HEADS * H_VDIM
H_CHUNK = 16
FFN_HIDDEN = -(-(8 * D_MODEL) // (3 * 256)) * 256
ALPHA = (2 * DEPTH) ** 0.25
BETA = (8 * DEPTH) ** -0.25
LN_EPS = 1e-5
HEAD_NORM_EPS = 1e-6
IN_SIZES = (M_WIDTH, M_WIDTH, M_WIDTH, M_WIDTH, M_HEADS, M_HEADS,
            H_KWIDTH, H_KWIDTH, H_VWIDTH, H_VWIDTH, D_MODEL, D_MODEL)
IN_WIDTH = sum(IN_SIZES)

kernel_name = 'hybrid_mlstm_hgrn2_deepnorm'


def _in_offsets():
    return [int(o) for o in np.cumsum((0,) + IN_SIZES)]


def _split_heads(a, n_heads):
    return a.reshape(a.shape[:2] + (n_heads, -1))


def _to_chunks(a, chunk):
    b, s, h = a.shape[:3]
    a = a.reshape((b, s // chunk, chunk, h) + a.shape[3:])
    return jnp.moveaxis(a, (1, 3), (0, 2))


def _from_chunks(a):
    a = jnp.moveaxis(a, (0, 2), (1, 3))
    return a.reshape((a.shape[0], a.shape[1] * a.shape[2]) + a.shape[3:])


def _layernorm(x, g, b):
    xf = x.astype(jnp.float32)
    mu = jnp.mean(xf, axis=-1, keepdims=True)
    var = jnp.mean(jnp.square(xf - mu), axis=-1, keepdims=True)
    return ((xf - mu) * lax.rsqrt(var + LN_EPS) * g + b).astype(x.dtype)


def _head_layernorm(h, gain):
    mu = jnp.mean(h, axis=-1, keepdims=True)
    var = jnp.mean(jnp.square(h - mu), axis=-1, keepdims=True)
    hn = (h - mu) * lax.rsqrt(var + HEAD_NORM_EPS)
    return hn.reshape(h.shape[:2] + (-1,)) * gain


def _head_rmsnorm(h, gain):
    hn = h * lax.rsqrt(jnp.mean(jnp.square(h), axis=-1, keepdims=True) + HEAD_NORM_EPS)
    return hn.reshape(h.shape[:2] + (-1,)) * gain


def _causal_depthwise_conv(u, w, b):
    c = u.shape[-1]
    y = lax.conv_general_dilated(u, w[:, None, :].astype(u.dtype), window_strides=(1,),
                                 padding=[(CONV_WIDTH - 1, 0)],
                                 dimension_numbers=('NWC', 'WIO', 'NWC'),
                                 feature_group_count=c)
    return y + b


def _hgrn_lower_bounds(lb_logits):
    p = jax.nn.softmax(lb_logits.astype(jnp.float32), axis=0)
    c = jnp.cumsum(p, axis=0)
    return c - c[0:1]


def _mlstm_chunkwise(q, k, v, i_pre, f_pre):
    f32 = jnp.float32
    bsz, _, n_heads, dh = q.shape
    q = q.astype(f32)
    k = k.astype(f32) * (dh ** -0.5)
    v = v.astype(f32)
    log_f = jax.nn.log_sigmoid(f_pre.astype(f32))
    log_i = i_pre.astype(f32)
    causal = jnp.tril(jnp.ones((M_CHUNK, M_CHUNK), dtype=bool))

    def step(carry, xs):
        c_mat, n_vec, m_prev = carry
        qc, kc, vc, ic, lfc = xs
        b = jnp.cumsum(lfc, axis=-1)
        log_w = jnp.where(causal, b[..., :, None] - b[..., None, :] + ic[..., None, :], -jnp.inf)
        log_inter = b + m_prev[..., None]
        m_t = jnp.maximum(log_inter, jnp.max(log_w, axis=-1))
        w = jnp.exp(log_w - m_t[..., None]) * jnp.einsum('bhtd,bhsd->bhts', qc, kc)
        s_inter = jnp.exp(log_inter - m_t)
        num = (jnp.einsum('bhts,bhsd->bhtd', w, vc)
               + s_inter[..., None] * jnp.einsum('bhtk,bhkv->bhtv', qc, c_mat))
        den = jnp.sum(w, axis=-1) + s_inter * jnp.einsum('bhtk,bhk->bht', qc, n_vec)
        h = num / jnp.maximum(jnp.abs(den), jnp.exp(-m_t))[..., None]
        b_end = b[..., -1]
        log_g = b_end[..., None] - b + ic
        m_new = jnp.maximum(b_end + m_prev, jnp.max(log_g, axis=-1))
        decay = jnp.exp(b_end + m_prev - m_new)
        g = jnp.exp(log_g - m_new[..., None])
        c_mat = decay[..., None, None] * c_mat + jnp.einsum('bhs,bhsk,bhsv->bhkv', g, kc, vc)
        n_vec = decay[..., None] * n_vec + jnp.einsum('bhs,bhsk->bhk', g, kc)
        return (c_mat, n_vec, m_new), h

    init = (jnp.zeros((bsz, n_heads, dh, dh), f32),
            jnp.zeros((bsz, n_heads, dh), f32),
            jnp.zeros((bsz, n_heads), f32))
    xs = (_to_chunks(q, M_CHUNK), _to_chunks(k, M_CHUNK), _to_chunks(v, M_CHUNK),
          _to_chunks(log_i, M_CHUNK), _to_chunks(log_f, M_CHUNK))
    _, h = lax.scan(step, init, xs)
    return _from_chunks(h)


def _hgrn2_chunkwise(q, k, v, log_f):
    f32 = jnp.float32
    bsz, _, n_heads, kd = q.shape
    vd = v.shape[-1]
    causal = jnp.tril(jnp.ones((H_CHUNK, H_CHUNK), dtype=bool))[:, :, None]

    def step(state, xs):
        qc, kc, vc, lfc = xs
        a = jnp.cumsum(lfc, axis=2)
        rel = jnp.where(causal, a[:, :, :, None, :] - a[:, :, None, :, :], -jnp.inf)
        scores = jnp.einsum('bhtk,bhtsk,bhsk->bhts', qc, jnp.exp(rel), kc)
        o = (jnp.einsum('bhts,bhsv->bhtv', scores, vc)
             + jnp.einsum('bhtk,bhkv->bhtv', qc * jnp.exp(a), state))
        a_end = a[:, :, -1:, :]
        state = (jnp.exp(a_end[:, :, 0, :])[..., None] * state
                 + jnp.einsum('bhsk,bhsv->bhkv', kc * jnp.exp(a_end - a), vc))
        return state, o

    init = jnp.zeros((bsz, n_heads, kd, vd), f32)
    xs = tuple(_to_chunks(t.astype(f32), H_CHUNK) for t in (q, k, v, log_f))
    _, o = lax.scan(step, init, xs)
    return _from_chunks(o)


def setup_inputs(seed: int = 0) -> dict:
    key = jax.random.key(seed)
    ks = jax.random.split(key, 18)
    f32 = jnp.float32
    off = _in_offsets()

    def nrm(k, shape, scale):
        return jax.random.normal(k, shape, f32) * scale

    x = nrm(ks[0], (BATCH, SEQ, D_MODEL), 1.0)
    w_in = nrm(ks[1], (DEPTH, D_MODEL, IN_WIDTH), D_MODEL ** -0.5)
    w_in = w_in.at[:, :, off[2]:off[3]].multiply(BETA)
    w_in = w_in.at[:, :, off[8]:off[9]].multiply(BETA)
    b_in = nrm(ks[2], (DEPTH, IN_WIDTH), 0.01)
    b_in = b_in.at[:, off[5]:off[6]].add(
        jnp.linspace(M_FORGET_BIAS_LO, M_FORGET_BIAS_HI, M_HEADS, dtype=f32))
    conv_w = nrm(ks[3], (DEPTH, CONV_WIDTH, 2 * M_WIDTH), CONV_WIDTH ** -0.5)
    conv_b = nrm(ks[4], (DEPTH, 2 * M_WIDTH), 0.01)
    m_norm_g = 1.0 + nrm(ks[5], (DEPTH, M_WIDTH), 0.02)
    lb_logits = nrm(ks[6], (DEPTH, H_KWIDTH), 0.1)
    h_norm_g = 1.0 + nrm(ks[7], (DEPTH, H_VWIDTH), 0.02)
    w_proj_a = nrm(ks[8], (DEPTH, M_WIDTH, D_MODEL), BETA * M_WIDTH ** -0.5)
    w_proj_b = nrm(ks[9], (DEPTH, H_VWIDTH, D_MODEL), BETA * H_VWIDTH ** -0.5)
    w_out = nrm(ks[10], (DEPTH, D_MODEL, D_MODEL), BETA * D_MODEL ** -0.5)
    ln1_g = 1.0 + nrm(ks[11], (DEPTH, D_MODEL), 0.02)
    ln1_b = nrm(ks[12], (DEPTH, D_MODEL), 0.02)
    w_ffn_gate = nrm(ks[13], (DEPTH, D_MODEL, FFN_HIDDEN), D_MODEL ** -0.5)
    w_ffn_up = nrm(ks[14], (DEPTH, D_MODEL, FFN_HIDDEN), BETA * D_MODEL ** -0.5)
    w_ffn_down = nrm(ks[15], (DEPTH, FFN_HIDDEN, D_MODEL), BETA * FFN_HIDDEN ** -0.5)
    ln2_g = 1.0 + nrm(ks[16], (DEPTH, D_MODEL), 0.02)
    ln2_b = nrm(ks[17], (DEPTH, D_MODEL), 0.02)
    return {'x': x, 'w_in': w_in, 'b_in': b_in, 'conv_w': conv_w, 'conv_b': conv_b,
            'm_norm_g': m_norm_g, 'lb_logits': lb_logits, 'h_norm_g': h_norm_g,
            'w_proj_a': w_proj_a, 'w_proj_b': w_proj_b, 'w_out': w_out,
            'ln1_g': ln1_g, 'ln1_b': ln1_b, 'w_ffn_gate': w_ffn_gate, 'w_ffn_up': w_ffn_up,
            'w_ffn_down': w_ffn_down, 'ln2_g': ln2_g, 'ln2_b': ln2_b}


def reference(x, w_in, b_in, conv_w, conv_b, m_norm_g, lb_logits, h_norm_g, w_proj_a, w_proj_b,
              w_out, ln1_g, ln1_b, w_ffn_gate, w_ffn_up, w_ffn_down, ln2_g, ln2_b):
    split_points = _in_offsets()[1:-1]
    lower_bounds = _hgrn_lower_bounds(lb_logits)
    for layer in range(DEPTH):
        u = x @ w_in[layer] + b_in[layer]
        (mq, mk, mv, mo, mi, mf, hq, hf, hi, hg, ga, gb) = jnp.split(u, split_points, axis=-1)

        qk = jax.nn.silu(_causal_depthwise_conv(jnp.concatenate([mq, mk], axis=-1),
                                                conv_w[layer], conv_b[layer]))
        mq, mk = jnp.split(qk, 2, axis=-1)
        h_a = _mlstm_chunkwise(_split_heads(mq, M_HEADS), _split_heads(mk, M_HEADS),
                               _split_heads(mv, M_HEADS), mi, mf)
        h_a = (_head_layernorm(h_a, m_norm_g[layer]) * jax.nn.sigmoid(mo)).astype(x.dtype)

        lb = lower_bounds[layer]
        log_f = jnp.logaddexp(jnp.log(lb), jnp.log1p(-lb) + jax.nn.log_sigmoid(hf.astype(jnp.float32)))
        h_b = _hgrn2_chunkwise(_split_heads(jax.nn.silu(hq), H_HEADS),
                               _split_heads(-jnp.expm1(log_f), H_HEADS),
                               _split_heads(hi, H_HEADS),
                               _split_heads(log_f, H_HEADS))
        h_b = (_head_rmsnorm(h_b, h_norm_g[layer]) * jax.nn.sigmoid(hg)).astype(x.dtype)

        mixed = (jax.nn.sigmoid(ga) * (h_a @ w_proj_a[layer])
                 + jax.nn.sigmoid(gb) * (h_b @ w_proj_b[layer]))
        x = _layernorm(ALPHA * x + mixed @ w_out[layer], ln1_g[layer], ln1_b[layer])

        ffn = (jax.nn.silu(x @ w_ffn_gate[layer]) * (x @ w_ffn_up[layer])) @ w_ffn_down[layer]
        x = _layernorm(ALPHA * x + ffn, ln2_g[layer], ln2_b[layer])
    return x
```

```python
import numpy as np
from contextlib import ExitStack
import concourse.bass as bass
import concourse.mybir as mybir
from concourse.bass_utils import run_bass_kernel_spmd

F32 = mybir.dt.float32
BF16 = mybir.dt.bfloat16
AF = mybir.ActivationFunctionType
ALU = mybir.AluOpType
AX = mybir.AxisListType

D = 1024
SEQ = 2048
DEPTH = 4
T = 1024
NT = T // 128
NB = T // 512
MW = 2048
FFN = 2816
NHC = FFN // 128
ALPHA = float((2 * DEPTH) ** 0.25)
OFF = [0, 2048, 4096, 6144, 8192, 8196, 8200, 9224, 10248, 11272, 12296, 13320, 14344]
O_MQ, O_MK, O_MV, O_MO, O_MI, O_MF, O_HQ, O_HF, O_HI, O_HG, O_GA, O_GB = OFF[:12]
LN_EPS = 1e-5
HN_EPS = 1e-6
SAME_SYNC = True
DBG_HALF = 0


class Reg:
    __slots__ = ("w", "r")

    def __init__(self):
        self.w = None
        self.r = {}


class Prog:
    ENG = {"pe": "tensor", "act": "scalar", "dve": "vector", "pool": "gpsimd", "sp": "sync"}

    def __init__(self, nc, stack):
        self.nc = nc
        self.stack = stack
        self.ops = {e: [] for e in self.ENG}
        self.sem = {}
        self.cnt = {}
        self.seen = {e: {} for e in self.ENG}
        for e in ("pe", "act", "dve", "pool"):
            self.newsem("c_" + e)

    def newsem(self, name):
        self.sem[name] = self.stack.enter_context(self.nc.semaphore(name))
        self.cnt[name] = 0

    def _deps(self, eng, reads, writes):
        need = {}
        for r in reads:
            if r.w is not None and need.get(r.w[0], 0) < r.w[1]:
                need[r.w[0]] = r.w[1]
        for w in writes:
            if w.w is not None and need.get(w.w[0], 0) < w.w[1]:
                need[w.w[0]] = w.w[1]
            for s, v in w.r.items():
                if need.get(s, 0) < v:
                    need[s] = v
        waits = []
        own = "c_" + eng
        seen = self.seen[eng]
        for s, v in need.items():
            if s == own and (eng == "pe" or not SAME_SYNC):
                continue
            if seen.get(s, 0) >= v:
                continue
            seen[s] = v
            waits.append((s, v))
        return waits

    def _commit(self, tok, reads, writes):
        s, v = tok
        for r in reads:
            if r.r.get(s, 0) < v:
                r.r[s] = v
        for w in writes:
            w.w = tok
            w.r = {}

    def op(self, eng, fn, reads=(), writes=()):
        waits = self._deps(eng, reads, writes)
        s = "c_" + eng
        self.cnt[s] += 1
        self.ops[eng].append((waits, fn, (s, 1)))
        self._commit((s, self.cnt[s]), reads, writes)

    def dma(self, eng, fn, reads, writes, sem):
        if sem not in self.sem:
            self.newsem(sem)
        waits = self._deps(eng, reads, writes)
        self.cnt[sem] += 16
        self.ops[eng].append((waits, fn, (sem, 16)))
        self._commit((sem, self.cnt[sem]), reads, writes)

    def barrier(self, final=False):
        for e in self.ENG:
            waits = []
            for s, c in self.cnt.items():
                if c == 0 or (s.startswith("w") and not final):
                    continue
                if s == "c_" + e:
                    continue
                if self.seen[e].get(s, 0) < c:
                    self.seen[e][s] = c
                    waits.append((s, c))
            if waits:
                self.ops[e].append((waits, None, None))

    def emit(self):
        nc = self.nc
        with nc.Block() as block:
            for e, attr in self.ENG.items():
                def mk(e):
                    def body(engine):
                        for waits, fn, inc in self.ops[e]:
                            for s, v in waits:
                                engine.wait_ge(self.sem[s], v)
                            if fn is not None:
                                ins = fn(engine)
                                ins.then_inc(self.sem[inc[0]], inc[1])
                    return body
                getattr(block, attr)(mk(e))


class Arena:
    def __init__(self, tensor, nbytes):
        self.t = tensor
        self.nbytes = nbytes
        self.off = 0

    def reset(self, off=0):
        self.off = off

    def alloc(self, shape, dtype):
        esz = 4 if dtype == F32 else 2
        n = 1
        for s in shape[1:]:
            n *= s
        nb = (n * esz + 31) // 32 * 32
        assert self.off + nb <= self.nbytes, (self.off, nb, self.nbytes)
        a = self.off // 2
        v = self.t[0:shape[0], a:a + nb // 2]
        self.off += nb
        if dtype == F32:
            v = v.bitcast(F32)
        v = v[:, 0:n]
        if len(shape) == 3:
            v = v.rearrange("p (a b) -> p a b", b=shape[2])
        return v


def build(n_seq, n_layers, debug=None):
    nc = bass.Bass("TRN2", target_bir_lowering=False)
    dr = {}

    def din(name, shape):
        dr[name] = nc.dram_tensor(name, list(shape), F32, kind="ExternalInput").ap()
    din("x", [n_seq, SEQ, D])
    din("w_in", [DEPTH, D, 14344]); din("b_in", [DEPTH, 14344])
    din("conv_w", [DEPTH, 4, 4096]); din("conv_b", [DEPTH, 4096])
    din("m_norm_g", [DEPTH, MW]); din("lb_logits", [DEPTH, 1024]); din("h_norm_g", [DEPTH, 1024])
    din("w_proj_a", [DEPTH, MW, D]); din("w_proj_b", [DEPTH, D, D]); din("w_out", [DEPTH, D, D])
    din("ln1_g", [DEPTH, D]); din("ln1_b", [DEPTH, D])
    din("w_ffn_gate", [DEPTH, D, FFN]); din("w_ffn_up", [DEPTH, D, FFN]); din("w_ffn_down", [DEPTH, FFN, D])
    din("ln2_g", [DEPTH, D]); din("ln2_b", [DEPTH, D])
    out = nc.dram_tensor("out", [n_seq, SEQ, D], F32, kind="ExternalOutput").ap()
    res1 = nc.dram_tensor("res1", [T, D], F32, kind="Internal").ap()
    res2 = nc.dram_tensor("res2", [T, D], F32, kind="Internal").ap()
    cst = nc.dram_tensor("cst", [n_layers, 4, 128, 2048], F32, kind="Internal").ap()
    sst = nc.dram_tensor("sst", [n_layers, 4, 128, 256], F32, kind="Internal").ap()
    dbg = None
    if debug is not None:
        dbg = nc.dram_tensor("dbg", list(debug), F32, kind="ExternalOutput").ap()

    with ExitStack() as st:
        P = Prog(nc, st)

        def sb(name, shape, dt):
            return st.enter_context(nc.sbuf_tensor(name, list(shape), dt))
        XT = sb("XT", [128, 8, T], BF16); XTr = [Reg() for _ in range(NT)]
        MIXb = sb("MIX", [128, 8 * T], BF16)
        MIXT = MIXb[:, :].rearrange("p (a b) -> p a b", b=T); MIXr = [Reg() for _ in range(NB)]
        BIGb = sb("BIG", [128, 24 * T], BF16)
        hAT = BIGb[:, 0:16 * T].rearrange("p (a b) -> p a b", b=T); hATr = [Reg() for _ in range(NB)]
        hBT = BIGb[:, 16 * T:24 * T].rearrange("p (a b) -> p a b", b=T); hBTr = [Reg() for _ in range(NB)]
        HIDT = BIGb[:, 0:NHC * T].rearrange("p (a b) -> p a b", b=T); HIDr = [Reg() for _ in range(NB)]
        WORKB = 75 * 1024 + 512
        WORKt = sb("WORK", [128, WORKB // 2], BF16)
        WA = Arena(WORKt, WORKB)
        MA = Arena(MIXb, 16 * 1024)
        NSLOT = 3
        wslot = [sb("wslot%d" % i, [128, 4096], BF16) for i in range(NSLOT)]
        wreg = [Reg() for _ in range(NSLOT)]
        bbc = [sb("bbc%d" % i, [128, 512], F32) for i in range(NSLOT)]
        bbr = [Reg() for _ in range(NSLOT)]
        GAM = sb("GAM", [128, T + 2], F32); GAMr = Reg()
        gprow = sb("gprow", [4, T + 2], F32); gprr = Reg()
        ident = sb("ident", [128, 128], BF16)
        identf = sb("identf", [128, 128], F32)
        maskbig = sb("maskbig", [128, 128], F32)
        mask01 = sb("mask01", [128, 128], F32)
        ones_row = sb("ones_row", [4, 8], F32)
        ones_bf = sb("ones_bf", [128, 2], BF16)
        cmk = sb("cmk", [128, T], BF16)
        sel = sb("sel", [4, 4, 128], F32)
        i4 = sb("i4", [4, 4], F32)
        bcol = sb("bcol", [128, DEPTH, 64], F32)
        cw = sb("cw", [128, DEPTH, 32, 4], F32)
        cb = sb("cb", [128, DEPTH, 32], F32)
        mg = sb("mg", [128, DEPTH, 16], F32)
        hgn = sb("hgn", [128, DEPTH, 8], F32)
        lbl = sb("lbl", [128, 8, DEPTH], F32)
        lbp = sb("lbp", [128, 8, DEPTH], F32)
        lb = sb("lb", [128, DEPTH, 8], F32)
        oml = sb("oml", [128, DEPTH, 8], F32)
        gbias = sb("gbias", [4, DEPTH, 2], F32)
        car = sb("car", [4, DEPTH, 4], F32); carr = Reg()
        ccar = sb("ccar", [128, DEPTH, 32, 4], BF16); ccr = Reg()
        nst = sb("nst", [128, DEPTH, 4, 4], F32); nsr = Reg()
        acolA = sb("acolA", [128, NT, 4], F32)
        fcolA = sb("fcolA", [128, NT, 4], F32); colr = Reg()
        small = sb("small", [128, 64], F32)
        cbh = sb("cbh", [128, DEPTH, 32], F32)
        bcolh = sb("bcolh", [128, DEPTH, 64], F32)
        lbc0 = sb("lbc0", [128, DEPTH, 8], F32)
        lbc1 = sb("lbc1", [128, DEPTH, 8], F32)
        mhalf = sb("mhalf", [128, 8], F32)
        CONST = Reg()
        banks = [st.enter_context(nc.psum_tensor("bank%d" % i, [128, 512], F32)) for i in range(8)]
        bankr = [Reg() for _ in range(8)]
        bstate = [0]

        dstate = {"on": False, "names": []}

        def dump(name, ap, reg, p=128):
            if dbg is None or not dstate["on"] or name in dstate["names"] or len(dstate["names"]) >= dbg.shape[0]:
                return
            i = len(dstate["names"])
            dstate["names"].append(name)
            n = ap.shape[-1] if len(ap.shape) == 2 else None
            regs = reg if isinstance(reg, list) else [reg]
            P.dma("pool", lambda e: e.dma_start(out=dbg[i, 0:p, 0:n], in_=ap), regs, [], "dbg")

        def bank():
            i = bstate[0]
            bstate[0] = (i + 1) % 8
            return banks[i], bankr[i]

        def bfv(bk, a, b):
            return bk[:, :].bitcast(BF16)[:, 0:a * b].rearrange("p (a b) -> p a b", b=b)

        def setup():
            def c1(e):
                e.memset(ident[:, :], 0.0)
                e.memset(identf[:, :], 0.0)
                e.memset(maskbig[:, :], 0.0)
                e.memset(mask01[:, :], 1.0)
                e.memset(ones_row[:, :], 1.0)
                e.memset(ones_bf[:, :], 1.0)
                e.memset(cmk[:, :], 1.0)
                e.memset(sel[:, :, :], 0.0)
                e.memset(i4[:, :], 0.0)
                e.memset(car[:, :, :], 0.0)
                e.memset(ccar[:, :, :, :], 0.0)
                e.memset(nst[:, :, :, :], 0.0)
                e.memset(gprow[:, :], 0.0)
                e.memset(mhalf[:, :], -0.5)
                return e.memset(lb[:, :, :], 0.0)
            P.op("pool", c1, [], [CONST])

            def c2(e):
                e.affine_select(out=ident[:, :], in_=ident[:, :], pattern=[[-1, 128]], compare_op=ALU.not_equal,
                                fill=1.0, base=0, channel_multiplier=1)
                e.affine_select(out=identf[:, :], in_=identf[:, :], pattern=[[-1, 128]], compare_op=ALU.not_equal,
                                fill=1.0, base=0, channel_multiplier=1)
                e.affine_select(out=maskbig[:, :], in_=maskbig[:, :], pattern=[[1, 128]], compare_op=ALU.is_ge,
                                fill=30000.0, base=0, channel_multiplier=-1)
                e.affine_select(out=mask01[:, :], in_=mask01[:, :], pattern=[[1, 128]], compare_op=ALU.is_ge,
                                fill=0.0, base=0, channel_multiplier=-1)
                e.affine_select(out=sel[:, :, :], in_=sel[:, :, :], pattern=[[1, 4], [0, 128]],
                                compare_op=ALU.not_equal, fill=1.0, base=0, channel_multiplier=-1)
                e.affine_select(out=i4[:, :], in_=i4[:, :], pattern=[[1, 4]], compare_op=ALU.not_equal,
                                fill=1.0, base=0, channel_multiplier=-1)
                return e.memset(cmk[:, :].rearrange("p (c l) -> p c l", l=128)[:, :, 0:1], 0.0)
            P.op("pool", c2, [CONST], [CONST])

            segs = [(O_MQ, 16, 0), (O_MK, 16, 16), (O_HQ, 8, 32), (O_HF, 8, 40), (O_GA, 8, 48), (O_GB, 8, 56)]
            for l in range(n_layers):
                for (o, n, c0) in segs:
                    P.dma("sp", lambda e, l=l, o=o, n=n, c0=c0: e.dma_start(
                        out=bcol[:, l, c0:c0 + n], in_=dr["b_in"][l, o:o + n * 128].rearrange("(c p) -> p c", p=128),
                        allow_slow_non_contiguous=True), [], [CONST], "cst")
                for j in range(4):
                    for c8 in range(4):
                        P.dma("sp", lambda e, l=l, j=j, c8=c8: e.dma_start(
                            out=cw[:, l, c8 * 8:(c8 + 1) * 8, j],
                            in_=dr["conv_w"][l, j, c8 * 1024:(c8 + 1) * 1024].rearrange("(c p) -> p c", p=128),
                            allow_slow_non_contiguous=True), [], [CONST], "cst")
                for c8 in range(2):
                    P.dma("sp", lambda e, l=l, c8=c8: e.dma_start(
                        out=cb[:, l, c8 * 16:(c8 + 1) * 16],
                        in_=dr["conv_b"][l, c8 * 2048:(c8 + 1) * 2048].rearrange("(c p) -> p c", p=128),
                        allow_slow_non_contiguous=True), [], [CONST], "cst")
                P.dma("sp", lambda e, l=l: e.dma_start(
                    out=mg[:, l, :], in_=dr["m_norm_g"][l, :].rearrange("(c p) -> p c", p=128),
                    allow_slow_non_contiguous=True), [], [CONST], "cst")
                P.dma("sp", lambda e, l=l: e.dma_start(
                    out=hgn[:, l, :], in_=dr["h_norm_g"][l, :].rearrange("(c p) -> p c", p=128),
                    allow_slow_non_contiguous=True), [], [CONST], "cst")
                P.dma("sp", lambda e, l=l: e.dma_start(
                    out=gbias[:, l, :], in_=dr["b_in"][l, O_MI:O_MI + 8].rearrange("(g h) -> h g", h=4),
                    allow_slow_non_contiguous=True), [], [CONST], "cst")
            for l in range(DEPTH):
                P.dma("sp", lambda e, l=l: e.dma_start(
                    out=lbl[:, :, l], in_=dr["lb_logits"][l, :].rearrange("(c p) -> p c", p=128),
                    allow_slow_non_contiguous=True), [], [CONST], "cst")
            P.op("act", lambda e: e.activation(out=lbp[:, :, :], in_=lbl[:, :, :], func=AF.Exp), [CONST], [CONST])
            P.op("dve", lambda e: e.tensor_reduce(out=small[:, 0:8], in_=lbp[:, :, :], axis=AX.X, op=ALU.add),
                 [CONST], [CONST])
            P.op("dve", lambda e: e.reciprocal(out=small[:, 8:16], in_=small[:, 0:8]), [CONST], [CONST])
            P.op("dve", lambda e: e.tensor_tensor(out=lbp[:, :, :], in0=lbp[:, :, :],
                                                  in1=small[:, 8:16].unsqueeze(2).broadcast_to([128, 8, DEPTH]),
                                                  op=ALU.mult), [CONST], [CONST])
            for l in range(1, DEPTH):
                P.op("dve", lambda e, l=l: e.tensor_tensor(out=lb[:, l, :], in0=lb[:, l - 1, :], in1=lbp[:, :, l],
                                                           op=ALU.add), [CONST], [CONST])
            P.op("dve", lambda e: e.tensor_scalar(out=oml[:, :, :], in0=lb[:, :, :], scalar1=-1.0, scalar2=1.0,
                                                  op0=ALU.mult, op1=ALU.add), [CONST], [CONST])
            P.op("dve", lambda e: e.tensor_scalar(out=lbc1[:, :, :], in0=oml[:, :, :], scalar1=0.5, scalar2=None,
                                                  op0=ALU.mult), [CONST], [CONST])
            P.op("dve", lambda e: e.tensor_tensor(out=lbc0[:, :, :], in0=lb[:, :, :], in1=lbc1[:, :, :], op=ALU.add),
                 [CONST], [CONST])
            P.op("dve", lambda e: e.tensor_scalar(out=cbh[:, :, :], in0=cb[:, :, :], scalar1=0.5, scalar2=None,
                                                  op0=ALU.mult), [CONST], [CONST])
            P.op("dve", lambda e: e.tensor_scalar(out=bcolh[:, :, :], in0=bcol[:, :, :], scalar1=0.5, scalar2=None,
                                                  op0=ALU.mult), [CONST], [CONST])
            P.op("dve", lambda e: e.tensor_scalar(out=mg[:, :, :], in0=mg[:, :, :], scalar1=0.5, scalar2=None,
                                                  op0=ALU.mult), [CONST], [CONST])
            P.op("dve", lambda e: e.tensor_scalar(out=hgn[:, :, :], in0=hgn[:, :, :], scalar1=0.5, scalar2=None,
                                                  op0=ALU.mult), [CONST], [CONST])

        jobs = []

        def wdma(slot_i, view, src):
            P.dma("pool", lambda e: e.dma_start(out=view, in_=src), [], [wreg[slot_i]], "w%d" % slot_i)

        def wsrc(w2d, k0, nk, c0, ncol):
            return w2d[k0 * 128:(k0 + nk) * 128, c0:c0 + ncol].rearrange("(k p) n -> p k n", p=128)

        def sview(slot_i, nk, ncol, col0=0, tot=None):
            tot = tot or ncol
            return wslot[slot_i][:, 0:nk * tot].rearrange("p (k c) -> p k c", c=tot)[:, :, col0:col0 + ncol]

        def run_jobs():
            issued = 0
            bg = [None, 0.0, 0.0]

            def step_bg():
                if bg[0] is None:
                    return
                try:
                    next(bg[0])
                except StopIteration:
                    bg[0] = None

            def drain_bg():
                while bg[0] is not None:
                    step_bg()
            for idx in range(len(jobs)):
                while issued < len(jobs) and issued <= idx + 2:
                    jobs[issued][0](issued % NSLOT)
                    issued += 1
                job = jobs[idx]
                g = job[1](idx % NSLOT)
                if len(job) > 2:
                    drain_bg()
                    bg[0] = g
                    bg[1] = job[2]
                    bg[2] = 0.0
                    step_bg()
                    continue
                if g is None:
                    continue
                for _ in g:
                    bg[2] += bg[1]
                    while bg[2] >= 1.0:
                        bg[2] -= 1.0
                        step_bg()
            drain_bg()

        def mm_group(out_ap, pairs, rd, wr):
            n = len(pairs)

            def fn(e):
                ins = None
                for i, (a, b) in enumerate(pairs):
                    ins = e.matmul(out_ap, lhsT=a, rhs=b, start=(i == 0), stop=(i == n - 1))
                return ins
            P.op("pe", fn, rd, [wr])

        def tr_group(dst_views, src_views, idn, rd, wr):
            def fn(e):
                ins = None
                for d_, s_ in zip(dst_views, src_views):
                    ins = e.transpose(out=d_, in_=s_, identity=idn)
                return ins
            P.op("pe", fn, rd, [wr])

        def to_xt(tile_f32, treg, tt, xb, xbr):
            P.op("act", lambda e: e.activation(out=xb, in_=tile_f32, func=AF.Copy), [treg], [xbr])
            bk, br = bank()
            pv = bfv(bk, 8, 128)
            tr_group([pv[:, k, :] for k in range(8)], [xb[:, k * 128:(k + 1) * 128] for k in range(8)],
                     ident[:, :], [xbr, CONST], br)
            P.op("act", lambda e: e.activation(out=XT[:, :, tt * 128:(tt + 1) * 128], in_=pv, func=AF.Copy),
                 [br], [XTr[tt]])

        XTB = lambda tb: [XTr[4 * tb + i] for i in range(4)]

        def do_pass(seq, half):
            tok0 = half * T
            first = (half == 0)
            dstate["on"] = (seq == 0 and half == DBG_HALF)
            P.barrier()
            WA.reset()
            xin = [WA.alloc([128, D], F32) for _ in range(2)]
            xinr = [Reg(), Reg()]
            xb = [WA.alloc([128, D], BF16) for _ in range(2)]
            xbr = [Reg(), Reg()]
            for tt in range(NT):
                i = tt % 2
                P.dma("sp", lambda e, tt=tt, i=i: e.dma_start(
                    out=xin[i], in_=dr["x"][seq, tok0 + tt * 128: tok0 + (tt + 1) * 128, :]),
                    [], [xinr[i]], "xi%d" % i)
                to_xt(xin[i], xinr[i], tt, xb[i], xbr[i])
            dump("XT0", XT[:, 0, :], list(XTr))
            for l in range(n_layers):
                last = (l == n_layers - 1)
                res_in = (lambda tt: dr["x"][seq, tok0 + tt * 128: tok0 + (tt + 1) * 128, :]) if l == 0 else \
                    (lambda tt: res2[tt * 128:(tt + 1) * 128, :])
                res_out = (lambda tt: out[seq, tok0 + tt * 128: tok0 + (tt + 1) * 128, :]) if last else \
                    (lambda tt: res2[tt * 128:(tt + 1) * 128, :])
                layer(l, first, res_in, res_out, last)

        R1 = [Reg() for _ in range(NT)]
        R2 = [Reg() for _ in range(NT)]

        def layer(l, first, res_in, res_out, last):
            win = dr["w_in"][l]
            bin_ = dr["b_in"][l]
            del jobs[:]
            P.barrier()
            WA.reset(); MA.reset()
            rowr = Reg()
            hbv = BIGb[:, 16 * T:24 * T]
            ASET = [
                dict(qT=WA.alloc([128, 4, T], BF16), kT=WA.alloc([128, 4, T], BF16),
                     V=WA.alloc([128, NT, 512], BF16), sO=WA.alloc([128, NT, 512], BF16),
                     qTr=Reg(), kTr=Reg(), Vr=Reg(), sOr=Reg(), xw=[]),
                dict(qT=MIXb[:, 0:4 * T].rearrange("p (a b) -> p a b", b=T),
                     kT=MIXb[:, 4 * T:8 * T].rearrange("p (a b) -> p a b", b=T),
                     V=hbv[:, 0:NT * 512].rearrange("p (a b) -> p a b", b=512),
                     sO=hbv[:, NT * 512:2 * NT * 512].rearrange("p (a b) -> p a b", b=512),
                     qTr=Reg(), kTr=Reg(), Vr=Reg(), sOr=Reg(), xw=[rowr]),
            ]
            ub = [WA.alloc([128, T + 4], BF16) for _ in range(2)]; ubr = [Reg(), Reg()]
            C = WA.alloc([128, 4, 512], F32); Cr = Reg()
            Cbf2 = [WA.alloc([128, 4, 512], BF16) for _ in range(2)]; Cbf2r = [Reg(), Reg()]
            nbf2 = [WA.alloc([128, 4], BF16) for _ in range(2)]; nbf2r = [Reg(), Reg()]
            dg = WA.alloc([128, 4, 128], BF16); dgr = Reg()
            tE = WA.alloc([128, 128], F32); tEr = Reg()
            tD = WA.alloc([128, 128], F32); tDr = Reg()
            tS2 = [WA.alloc([128, 128], F32) for _ in range(3)]; tS2r = [Reg() for _ in range(3)]
            tW2 = [WA.alloc([128, 128], BF16) for _ in range(3)]; tW2r = [Reg() for _ in range(3)]
            qTp2 = [WA.alloc([128, 4, 128], BF16) for _ in range(3)]; qTp2r = [Reg() for _ in range(3)]
            hb2 = [WA.alloc([128, 512], F32) for _ in range(2)]; hb2r = [Reg(), Reg()]
            hg2 = [WA.alloc([128, 512], BF16) for _ in range(2)]; hg2r = [Reg(), Reg()]
            Kp2 = [WA.alloc([128, 512], BF16) for _ in range(3)]; Kp2r = [Reg() for _ in range(3)]
            smM = [WA.alloc([128, 8], F32) for _ in range(2)]; smMr = [Reg(), Reg()]
            tmpo = WA.alloc([128, 512], F32); tmpor = Reg()
            tht = [WA.alloc([128, 512], BF16) for _ in range(2)]; thtr = [Reg(), Reg()]
            thx = [WA.alloc([128, 512], BF16) for _ in range(2)]; thxr = [Reg(), Reg()]
            sm = WA.alloc([128, 16], F32); smr = Reg()
            r_i = MA.alloc([4, T], F32); r_f = MA.alloc([4, T], F32)
            r_B = MA.alloc([4, T], F32); r_m = MA.alloc([4, T], F32)

            def g_load(si):
                wdma(si, sview(si, 8, 8), wsrc(win, 0, 8, O_MI, 8))

            def g_comp(si):
                wv = sview(si, 8, 8)
                for g, dst in ((0, r_i), (1, r_f)):
                    for tb in range(NB):
                        bk, br = bank()
                        mm_group(bk[0:4, :], [(wv[:, k, g * 4:(g + 1) * 4], XT[:, k, tb * 512:(tb + 1) * 512])
                                              for k in range(8)], [wreg[si]] + XTB(tb), br)
                        P.op("act", lambda e, bk=bk, dst=dst, tb=tb, g=g: e.activation(
                            out=dst[:, tb * 512:(tb + 1) * 512], in_=bk[0:4, :], func=AF.Identity,
                            bias=gbias[:, l, g:g + 1]), [br, CONST], [rowr])
                P.op("act", lambda e: e.activation(out=r_f, in_=r_f, func=AF.Exp, scale=-1.0), [rowr], [rowr])
                P.op("act", lambda e: e.activation(out=r_f, in_=r_f, func=AF.Ln, bias=1.0), [rowr], [rowr])
                P.op("dve", lambda e: e.tensor_scalar(out=r_f, in0=r_f, scalar1=-1.0, scalar2=None, op0=ALU.mult),
                     [rowr], [rowr])
                if first:
                    P.op("pool", lambda e: e.memset(car[:, l, :], 0.0), [carr], [carr])
                P.op("dve", lambda e: e.tensor_tensor_scan(
                    out=r_B, data0=ones_row[:, 0:1].broadcast_to([4, T]), data1=r_f, initial=car[:, l, 0:1],
                    op0=ALU.mult, op1=ALU.add), [rowr, carr, CONST], [rowr])
                P.op("dve", lambda e: e.tensor_tensor_scan(
                    out=r_m, data0=r_f, data1=r_i, initial=car[:, l, 1:2], op0=ALU.add, op1=ALU.max),
                    [rowr, carr], [rowr])
                P.op("dve", lambda e: e.tensor_copy(out=gprow[:, 1:2], in_=car[:, l, 2:3]), [carr, gprr], [gprr])
                P.op("dve", lambda e: e.tensor_tensor(out=gprow[:, 2:T + 2], in0=r_m, in1=r_B, op=ALU.subtract),
                     [rowr, gprr], [gprr])
                P.op("dve", lambda e: e.tensor_tensor(out=r_i, in0=r_i, in1=r_B, op=ALU.subtract), [rowr], [rowr])
                P.op("dve", lambda e: e.tensor_copy(out=car[:, l, 0:1], in_=r_B[:, T - 1:T]), [rowr, carr], [carr])
                P.op("dve", lambda e: e.tensor_copy(out=car[:, l, 1:2], in_=r_m[:, T - 1:T]), [rowr, carr], [carr])
                P.op("dve", lambda e: e.tensor_copy(out=car[:, l, 2:3], in_=gprow[:, T + 1:T + 2]),
                     [gprr, carr], [carr])
                for src, dstc, isf in ((r_i, acolA, False), (r_m, fcolA, True)):
                    bk, br = bank()

                    def fn(e, src=src, bk=bk):
                        ins = None
                        for c in range(NT):
                            ins = e.matmul(bk[:, c * 4:(c + 1) * 4], lhsT=src[:, c * 128:(c + 1) * 128],
                                           rhs=i4[:, :], start=True, stop=True)
                        return ins
                    P.op("pe", fn, [rowr, CONST], [br])
                    if isf:
                        P.op("act", lambda e, bk=bk, dstc=dstc: e.activation(
                            out=dstc[:, :, :], in_=bk[:, 0:NT * 4].rearrange("p (c h) -> p c h", h=4),
                            func=AF.Exp, scale=-1.0, bias=float(np.log(4.0 * np.sqrt(512.0)))), [br], [colr])
                    else:
                        P.op("act", lambda e, bk=bk, dstc=dstc: e.activation(
                            out=dstc[:, :, :], in_=bk[:, 0:NT * 4].rearrange("p (c h) -> p c h", h=4),
                            func=AF.Identity), [br], [colr])
            jobs.append((g_load, g_comp))

            for h in range(4):
                BS = ASET[h % 2]
                for which, (o_seg, dstT, dstr, bc0) in enumerate(((O_MQ, BS["qT"], BS["qTr"], 0),
                                                                  (O_MK, BS["kT"], BS["kTr"], 16))):
                    def qk_load(si, o_seg=o_seg, h=h):
                        wdma(si, sview(si, 8, 512), wsrc(win, 0, 8, o_seg + h * 512, 512))

                    def qk_comp(si, o_seg=o_seg, h=h, dstT=dstT, dstr=dstr, bc0=bc0, which=which, xw=BS["xw"]):
                        wv = sview(si, 8, 512)

                        def proj(dc):
                            ch = which * 16 + h * 4 + dc
                            u = ub[dc % 2]; ur = ubr[dc % 2]
                            if first:
                                P.op("pool", lambda e: e.memset(u[:, 0:4], 0.0), [], [ur])
                            else:
                                P.op("act", lambda e: e.activation(out=u[:, 0:4], in_=ccar[:, l, ch, :],
                                                                   func=AF.Copy), [ccr], [ur])
                            for tb in range(NB):
                                bk, br = bank()
                                mm_group(bk[:, :], [(wv[:, k, dc * 128:(dc + 1) * 128],
                                                     XT[:, k, tb * 512:(tb + 1) * 512]) for k in range(8)],
                                         [wreg[si]] + XTB(tb), br)
                                P.op("act", lambda e, bk=bk, tb=tb: e.activation(
                                    out=u[:, 4 + tb * 512: 4 + (tb + 1) * 512], in_=bk[:, :], func=AF.Identity,
                                    bias=bcol[:, l, bc0 + h * 4 + dc: bc0 + h * 4 + dc + 1]), [br, CONST], [ur])
                            P.op("act", lambda e: e.activation(out=ccar[:, l, ch, :], in_=u[:, T:T + 4],
                                                               func=AF.Copy), [ur], [ccr])

                        def conv(dc, tb):
                            ch = which * 16 + h * 4 + dc
                            u = ub[dc % 2]; ur = ubr[dc % 2]
                            if tb == 0:
                                for j in range(4):
                                    P.op("dve", lambda e, j=j: e.tensor_scalar(
                                        out=dg[:, j, :], in0=ident[:, :], scalar1=cw[:, l, ch, j:j + 1], scalar2=None,
                                        op0=ALU.mult), [CONST], [dgr])
                            bk, br = bank()
                            mm_group(bk[:, :], [(dg[:, j, :], u[:, 1 + j + tb * 512: 1 + j + (tb + 1) * 512])
                                                for j in range(4)], [dgr, ur], br)
                            th = tht[tb]; thr = thtr[tb]
                            P.op("act", lambda e: e.activation(
                                out=th, in_=bk[:, :], func=AF.Tanh, scale=0.5, bias=cbh[:, l, ch:ch + 1]),
                                [br, CONST], [thr])
                            xp = thx[tb]; xpr = thxr[tb]
                            P.op("act", lambda e: e.activation(
                                out=xp, in_=bk[:, :], func=AF.Identity, bias=cb[:, l, ch:ch + 1]),
                                [br, CONST], [xpr])
                            P.op("dve", lambda e: e.scalar_tensor_tensor(
                                out=dstT[:, dc, tb * 512:(tb + 1) * 512], in0=th, scalar=1.0,
                                in1=xp, op0=ALU.add, op1=ALU.mult), [thr, xpr], [dstr] + xw)

                        proj(0)
                        for dc in range(4):
                            if dc + 1 < 4:
                                proj(dc + 1)
                            conv(dc, 0)
                            yield
                            conv(dc, 1)
                            yield
                    jobs.append((qk_load, qk_comp))
                for which, o_seg in enumerate((O_MV, O_MO)):
                    def vo_load(si, o_seg=o_seg, h=h, which=which):
                        wdma(si, sview(si, 8, 512), wsrc(win, 0, 8, o_seg + h * 512, 512))
                        P.dma("sp", lambda e: e.dma_start(
                            out=bbc[si][:, :], in_=bin_[o_seg + h * 512: o_seg + (h + 1) * 512].partition_broadcast(128)),
                            [], [bbr[si]], "bb%d" % si)

                    def vo_comp(si, which=which, V=BS["V"], Vr=BS["Vr"], sO=BS["sO"], sOr=BS["sOr"]):
                        wv = sview(si, 8, 512)
                        for tt in range(NT):
                            bk, br = bank()
                            mm_group(bk[:, :], [(XT[:, k, tt * 128:(tt + 1) * 128], wv[:, k, :]) for k in range(8)],
                                     [wreg[si], XTr[tt]], br)
                            if which == 0:
                                P.op("dve", lambda e, bk=bk, tt=tt: e.tensor_tensor(
                                    out=V[:, tt, :], in0=bk[:, :], in1=bbc[si][:, :], op=ALU.add), [br, bbr[si]], [Vr])
                            else:
                                P.op("dve", lambda e, bk=bk: e.tensor_tensor(
                                    out=tmpo, in0=bk[:, :], in1=bbc[si][:, :], op=ALU.add), [br, bbr[si]], [tmpor])
                                P.op("act", lambda e, tt=tt: e.activation(out=sO[:, tt, :], in_=tmpo, func=AF.Tanh,
                                                                          scale=0.5), [tmpor], [sOr])
                            yield
                    jobs.append((vo_load, vo_comp))
                def ch_load(si):
                    pass

                def ch_comp(si, h=h, BS=BS):
                    qT = BS["qT"]; kT = BS["kT"]; V = BS["V"]; sO = BS["sO"]
                    qTr = BS["qTr"]; kTr = BS["kTr"]; Vr = BS["Vr"]; sOr = BS["sOr"]
                    if first:
                        P.op("pool", lambda e: e.memset(C, 0.0), [], [Cr])
                        P.op("pool", lambda e: e.memset(Cbf2[1], 0.0), [], [Cbf2r[1]])
                        P.op("pool", lambda e: e.memset(nst[:, l, h, :], 0.0), [], [nsr])
                    else:
                        P.dma("sp", lambda e: e.dma_start(
                            out=C, in_=cst[l, h].rearrange("p (a b) -> p a b", b=512)), [], [Cr], "cs")
                        P.op("act", lambda e: e.activation(out=Cbf2[1], in_=C, func=AF.Copy), [Cr], [Cbf2r[1]])
                    P.op("act", lambda e: e.activation(out=nbf2[1], in_=nst[:, l, h, :], func=AF.Copy), [nsr], [nbf2r[1]])
                    for (a, b) in ((0, 512), (512, 1024), (1024, 1026)):
                        bk, br = bank()
                        mm_group(bk[:, 0:b - a], [(sel[:, h, :], gprow[:, a:b])], [gprr, CONST], br)
                        P.op("act", lambda e, bk=bk, a=a, b=b: e.activation(
                            out=GAM[:, a:b], in_=bk[:, 0:b - a], func=AF.Identity), [br], [GAMr])
                    if l == 0 and h == 0:
                        dump("qT0", qT[:, 0, :], qTr); dump("kT0", kT[:, 0, :], kTr)
                        dump("V0", V[:, 0, :], Vr); dump("sO0", sO[:, 0, :], sOr)
                        dump("GAM", GAM[:, 0:1024], GAMr); dump("acol", acolA[:, :, :].rearrange("p a b -> p (a b)"), colr)
                        dump("fcol", fcolA[:, :, :].rearrange("p a b -> p (a b)"), colr)
                        dump("gprow", gprow[:, 0:1024], gprr, p=4)

                    def F(c):
                        b = c % 3
                        t0 = c * 128
                        gs = GAM[:, 2 + t0: 2 + t0 + 128]
                        bS, bSr = bank()
                        mm_group(bS[:, 0:128], [(kT[:, dc, t0:t0 + 128], qT[:, dc, t0:t0 + 128]) for dc in range(4)],
                                 [kTr, qTr], bSr)
                        P.op("dve", lambda e: e.scalar_tensor_tensor(
                            out=tE, in0=gs, scalar=acolA[:, c, h:h + 1], in1=maskbig[:, :], op0=ALU.subtract,
                            op1=ALU.max), [GAMr, colr, CONST], [tEr])
                        P.op("act", lambda e: e.activation(out=tD, in_=tE, func=AF.Exp, scale=-1.0), [tEr], [tDr])
                        P.op("dve", lambda e: e.tensor_tensor(out=tW2[b], in0=bS[:, 0:128], in1=tD, op=ALU.mult),
                             [bSr, tDr], [tW2r[b]])
                        P.op("act", lambda e: e.activation(
                            out=tS2[b], in_=gs, func=AF.Exp, scale=-1.0, bias=GAM[:, 1 + t0: 2 + t0]), [GAMr], [tS2r[b]])
                        P.op("dve", lambda e: e.tensor_tensor(
                            out=qTp2[b], in0=qT[:, :, t0:t0 + 128], in1=tS2[b].unsqueeze(1).broadcast_to([128, 4, 128]),
                            op=ALU.mult), [qTr, tS2r[b]], [qTp2r[b]])
                        bK, bKr = bank()
                        pk = bfv(bK, 4, 128)
                        tr_group([pk[:, dc, :] for dc in range(4)], [kT[:, dc, t0:t0 + 128] for dc in range(4)],
                                 ident[:, :], [kTr, CONST], bKr)
                        P.op("act", lambda e: e.activation(
                            out=Kp2[b], in_=bK[:, :].bitcast(BF16)[:, 0:512], func=AF.Identity, scale=tD[:, 127:128]),
                            [bKr, tDr], [Kp2r[b]])

                    def U(c):
                        b = c % 2
                        kb = c % 3
                        for dc in range(4):
                            bC, bCr = bank()
                            mm_group(bC[:, :], [(Kp2[kb][:, dc * 128:(dc + 1) * 128], V[:, c, :])], [Kp2r[kb], Vr], bCr)
                            P.op("dve", lambda e, bC=bC, dc=dc: e.scalar_tensor_tensor(
                                out=C[:, dc, :], in0=C[:, dc, :], scalar=tS2[kb][:, 127:128], in1=bC[:, :], op0=ALU.mult,
                                op1=ALU.add), [bCr, tS2r[kb], Cr], [Cr])
                        P.op("act", lambda e: e.activation(out=Cbf2[b], in_=C, func=AF.Copy), [Cr], [Cbf2r[b]])
                        bn_, bnr = bank()

                        def fn(e):
                            ins = None
                            for dc in range(4):
                                ins = e.matmul(bn_[:, 2 * dc:2 * dc + 1], lhsT=Kp2[kb][:, dc * 128:(dc + 1) * 128],
                                               rhs=ones_bf[:, 0:1], start=True, stop=True)
                            return ins
                        P.op("pe", fn, [Kp2r[kb], CONST], [bnr])
                        P.op("dve", lambda e: e.scalar_tensor_tensor(
                            out=nst[:, l, h, :], in0=nst[:, l, h, :], scalar=tS2[kb][:, 127:128],
                            in1=bn_[:, 0:8].rearrange("p (a b) -> p a b", b=2)[:, :, 0], op0=ALU.mult, op1=ALU.add),
                            [bnr, tS2r[kb], nsr], [nsr])
                        P.op("act", lambda e: e.activation(out=nbf2[b], in_=nst[:, l, h, :], func=AF.Copy),
                             [nsr], [nbf2r[b]])

                    def M(c):
                        b = c % 2
                        pb = (c - 1) % 2
                        kb = c % 3
                        bN, bNr = bank()
                        mm_group(bN[:, :], [(tW2[kb], V[:, c, :])] + [(qTp2[kb][:, dc, :], Cbf2[pb][:, dc, :])
                                                                     for dc in range(4)],
                                 [tW2r[kb], Vr, qTp2r[kb], Cbf2r[pb]], bNr)
                        bD, bDr = bank()
                        mm_group(bD[:, 0:1], [(tW2[kb], ones_bf[:, 0:1])] + [(qTp2[kb][:, dc, :], nbf2[pb][:, dc:dc + 1])
                                                                           for dc in range(4)],
                                 [tW2r[kb], CONST, qTp2r[kb], nbf2r[pb]], bDr)
                        smm = smM[b]; smmr = smMr[b]
                        P.op("act", lambda e: e.activation(out=smm[:, 0:1], in_=bD[:, 0:1], func=AF.Abs),
                             [bDr, smmr], [smmr])
                        P.op("dve", lambda e: e.tensor_scalar(
                            out=smm[:, 1:2], in0=smm[:, 0:1], scalar1=fcolA[:, c, h:h + 1], scalar2=None,
                            op0=ALU.max), [colr, smmr], [smmr])
                        P.op("dve", lambda e: e.reciprocal(out=smm[:, 2:3], in_=smm[:, 1:2]), [smmr], [smmr])
                        P.op("act", lambda e: e.activation(out=hb2[b], in_=bN[:, :], func=AF.Identity,
                                                           scale=smm[:, 2:3]), [bNr, smmr], [hb2r[b]])

                    def G1(c):
                        b = c % 2
                        hbb = hb2[b]; hbbr = hb2r[b]
                        hg = hg2[b]; hgr = hg2r[b]
                        P.op("dve", lambda e: e.bn_stats(out=sm[:, 2:8], in_=hbb), [hbbr, smr], [smr])
                        P.op("dve", lambda e: e.bn_aggr(out=sm[:, 8:10], in_=sm[:, 2:8]), [smr], [smr])
                        P.op("pool", lambda e: e.tensor_scalar(out=sm[:, 10:11], in0=sm[:, 9:10], scalar1=1.0, scalar2=HN_EPS, op0=ALU.mult, op1=ALU.add), [smr], [smr])
                        P.op("pool", lambda e: e.tensor_tensor(out=sm[:, 11:12], in0=sm[:, 10:11], in1=mhalf[:, 0:1],
                                                               op=ALU.pow), [smr, CONST], [smr])
                        P.op("dve", lambda e: e.tensor_scalar(out=sm[:, 12:13], in0=sm[:, 8:9], scalar1=sm[:, 11:12],
                                                              scalar2=-1.0, op0=ALU.mult, op1=ALU.mult), [smr], [smr])
                        P.op("act", lambda e: e.activation(out=hbb, in_=hbb, func=AF.Identity, scale=sm[:, 11:12],
                                                           bias=sm[:, 12:13]), [hbbr, smr], [hbbr])
                        if l == 0 and h == 0 and c == 1:
                            dump("hb", hbb, hbbr)
                        P.op("dve", lambda e: e.scalar_tensor_tensor(out=hg, in0=sO[:, c, :], scalar=1.0, in1=hbb,
                                                                     op0=ALU.add, op1=ALU.mult), [hbbr, sOr], [hgr])

                    def G2(c):
                        b = c % 2
                        t0 = c * 128
                        hg = hg2[b]; hgr = hg2r[b]
                        bT, bTr = bank()
                        pv = bfv(bT, 4, 128)
                        tr_group([pv[:, dc, :] for dc in range(4)], [hg[:, dc * 128:(dc + 1) * 128] for dc in range(4)],
                                 ident[:, :], [hgr, CONST], bTr)
                        P.op("dve", lambda e: e.tensor_tensor(
                            out=hAT[:, h * 4:(h + 1) * 4, t0:t0 + 128], in0=pv,
                            in1=mg[:, l, h * 4:(h + 1) * 4].unsqueeze(2).broadcast_to([128, 4, 128]), op=ALU.mult),
                            [bTr, CONST], [hATr[c // 4]])

                    F(0)
                    F(1)
                    yield
                    for i in range(NT):
                        if i + 2 < NT:
                            F(i + 2)
                        U(i)
                        M(i)
                        if i >= 1:
                            G1(i - 1)
                        if i >= 2:
                            G2(i - 2)
                        yield
                    G1(NT - 1)
                    G2(NT - 2)
                    yield
                    G2(NT - 1)
                    P.dma("sp", lambda e: e.dma_start(out=cst[l, h].rearrange("p (a b) -> p a b", b=512), in_=C),
                          [Cr], [], "cs")
                    if l == 0 and h == 0:
                        dump("hAT0", hAT[:, 0, :], hATr); dump("C0", C[:, 0, :], Cr)
                jobs.append((ch_load, ch_comp, 0.5))
            run_jobs()
            del jobs[:]

            P.barrier()
            WA.reset()
            BSET = [dict(sgq=WA.alloc([128, 2, T], BF16), kk=WA.alloc([128, 2, T], BF16),
                         aa=WA.alloc([128, 2, T], F32), V2=WA.alloc([128, NT, 256], BF16),
                         sG=WA.alloc([128, NT, 256], BF16), sgqr=Reg(), kkr=Reg(), aar=Reg(), V2r=Reg(), sGr=Reg())
                    for _ in range(2)]
            lga = WA.alloc([128, T], F32); lgar = Reg()
            S = WA.alloc([128, 2, 128], F32); Sr = Reg()
            Sbf2 = [WA.alloc([128, 2, 128], BF16) for _ in range(2)]; Sbf2r = [Reg(), Reg()]
            t1 = WA.alloc([128, 512], F32); t1r = Reg()
            t2 = WA.alloc([128, 512], F32); t2r = Reg()
            t3 = WA.alloc([128, 512], BF16); t3r = Reg()
            t4 = WA.alloc([128, 512], BF16); t4r = Reg()
            d1 = WA.alloc([128, 2, 128], F32); d1r = Reg()
            d2 = WA.alloc([128, 2, 128], F32); d2r = Reg()
            e0 = WA.alloc([128, 2, 128], BF16); e0r = Reg()
            e1 = WA.alloc([128, 2, 128], BF16); e1r = Reg()
            e1n = WA.alloc([128, 2, 128], BF16); e1nr = Reg()
            e2 = WA.alloc([128, 2, 128], BF16); e2r = Reg()
            q02 = [WA.alloc([128, 2, 128], BF16) for _ in range(3)]; q02r = [Reg() for _ in range(3)]
            qm2 = [WA.alloc([128, 2, 128], BF16) for _ in range(2)]; qm2r = [Reg(), Reg()]
            km2 = [WA.alloc([128, 2, 128], BF16) for _ in range(2)]; km2r = [Reg(), Reg()]
            Kh2 = [WA.alloc([128, 2, 128], BF16) for _ in range(2)]; Kh2r = [Reg(), Reg()]
            KhT2 = [WA.alloc([128, 2, 128], BF16) for _ in range(2)]; KhT2r = [Reg(), Reg()]
            scm2 = [WA.alloc([128, 2, 128], BF16) for _ in range(2)]; scm2r = [Reg(), Reg()]
            sq = WA.alloc([128, 256], F32); sqr = Reg()
            sqM = WA.alloc([128, 256], BF16); sqMr = Reg()
            hn2 = [WA.alloc([128, 2, 128], BF16) for _ in range(2)]; hn2r = [Reg(), Reg()]
            hgt2 = [WA.alloc([128, 256], BF16) for _ in range(2)]; hgt2r = [Reg(), Reg()]
            eae2 = [WA.alloc([128, 2], F32) for _ in range(3)]; eae2r = [Reg() for _ in range(3)]
            smB = [WA.alloc([128, 8], F32) for _ in range(2)]; smBr = [Reg(), Reg()]
            for g in range(4):
                def b1_load(si, g=g):
                    wdma(si, sview(si, 8, 256, 0, 512), wsrc(win, 0, 8, O_HQ + g * 256, 256))
                    wdma(si, sview(si, 8, 256, 256, 512), wsrc(win, 0, 8, O_HF + g * 256, 256))

                QS = BSET[g % 2]

                def b1_comp(si, g=g, QS=QS):
                    sgq = QS["sgq"]; kk = QS["kk"]; aa = QS["aa"]
                    sgqr = QS["sgqr"]; kkr = QS["kkr"]; aar = QS["aar"]
                    wv = sview(si, 8, 512)
                    for j in range(2):
                        hd = 2 * g + j
                        for tb in range(NB):
                            bk, br = bank()
                            mm_group(bk[:, :], [(wv[:, k, j * 128:(j + 1) * 128], XT[:, k, tb * 512:(tb + 1) * 512])
                                                for k in range(8)], [wreg[si]] + XTB(tb), br)
                            P.op("act", lambda e, bk=bk, hd=hd: e.activation(
                                out=t3, in_=bk[:, :], func=AF.Tanh, scale=0.5, bias=bcolh[:, l, 32 + hd:33 + hd]),
                                [br, CONST], [t3r])
                            P.op("act", lambda e, bk=bk, hd=hd: e.activation(
                                out=t4, in_=bk[:, :], func=AF.Identity, bias=bcol[:, l, 32 + hd:33 + hd]),
                                [br, CONST], [t4r])
                            P.op("dve", lambda e, j=j, tb=tb: e.scalar_tensor_tensor(
                                out=sgq[:, j, tb * 512:(tb + 1) * 512], in0=t3, scalar=1.0,
                                in1=t4, op0=ALU.add, op1=ALU.mult), [t3r, t4r], [sgqr])
                            bk, br = bank()
                            mm_group(bk[:, :], [(wv[:, k, 256 + j * 128: 256 + (j + 1) * 128],
                                                 XT[:, k, tb * 512:(tb + 1) * 512]) for k in range(8)],
                                     [wreg[si]] + XTB(tb), br)
                            P.op("act", lambda e, bk=bk, hd=hd: e.activation(
                                out=t1, in_=bk[:, :], func=AF.Tanh, scale=0.5, bias=bcolh[:, l, 40 + hd:41 + hd]),
                                [br, CONST], [t1r])
                            P.op("dve", lambda e, hd=hd: e.tensor_scalar(
                                out=t2, in0=t1, scalar1=lbc1[:, l, hd:hd + 1], scalar2=lbc0[:, l, hd:hd + 1],
                                op0=ALU.mult, op1=ALU.add), [t1r, CONST], [t2r])
                            P.op("act", lambda e, j=j, tb=tb: e.activation(
                                out=lga[:, tb * 512:(tb + 1) * 512], in_=t2, func=AF.Ln), [t2r], [lgar])
                            P.op("dve", lambda e, j=j, tb=tb: e.tensor_scalar(
                                out=kk[:, j, tb * 512:(tb + 1) * 512], in0=t2, scalar1=-1.0, scalar2=1.0,
                                op0=ALU.mult, op1=ALU.add), [t2r], [kkr])
                            yield
                        P.op("dve", lambda e, j=j: e.tensor_tensor_scan(
                            out=aa[:, j, :], data0=cmk[:, :], data1=lga[:, :], initial=0.0, op0=ALU.mult,
                            op1=ALU.add), [lgar, CONST], [aar])
                    if l == 0 and g == 0:
                        dump("sgq0", sgq[:, 0, :], sgqr); dump("kk0", kk[:, 0, :], kkr); dump("aa0", aa[:, 0, :], aar)
                jobs.append((b1_load, b1_comp))

                def b2_load(si, g=g):
                    wdma(si, sview(si, 8, 256, 0, 512), wsrc(win, 0, 8, O_HI + g * 256, 256))
                    wdma(si, sview(si, 8, 256, 256, 512), wsrc(win, 0, 8, O_HG + g * 256, 256))
                    P.dma("sp", lambda e: e.dma_start(
                        out=bbc[si][:, 0:256], in_=bin_[O_HI + g * 256: O_HI + (g + 1) * 256].partition_broadcast(128)),
                        [], [bbr[si]], "bb%d" % si)
                    P.dma("sp", lambda e: e.dma_start(
                        out=bbc[si][:, 256:512], in_=bin_[O_HG + g * 256: O_HG + (g + 1) * 256].partition_broadcast(128)),
                        [], [bbr[si]], "bb%d" % si)

                def b2_comp(si, g=g, QS=QS):
                    V2 = QS["V2"]; sG = QS["sG"]; V2r = QS["V2r"]; sGr = QS["sGr"]
                    wv = sview(si, 8, 512)
                    for tt in range(NT):
                        bk, br = bank()
                        mm_group(bk[:, :], [(XT[:, k, tt * 128:(tt + 1) * 128], wv[:, k, :]) for k in range(8)],
                                 [wreg[si], XTr[tt]], br)
                        P.op("dve", lambda e, bk=bk: e.tensor_tensor(out=t1, in0=bk[:, :], in1=bbc[si][:, :], op=ALU.add),
                             [br, bbr[si]], [t1r])
                        P.op("act", lambda e, tt=tt: e.activation(out=V2[:, tt, :], in_=t1[:, 0:256], func=AF.Copy),
                             [t1r], [V2r])
                        P.op("act", lambda e, tt=tt: e.activation(out=sG[:, tt, :], in_=t1[:, 256:512], func=AF.Tanh,
                                                                  scale=0.5), [t1r], [sGr])
                        yield
                jobs.append((b2_load, b2_comp))

                def bch_comp(si, g=g, QS=QS):
                    sgq = QS["sgq"]; kk = QS["kk"]; aa = QS["aa"]; V2 = QS["V2"]; sG = QS["sG"]
                    sgqr = QS["sgqr"]; kkr = QS["kkr"]; aar = QS["aar"]; V2r = QS["V2r"]; sGr = QS["sGr"]
                    if first:
                        P.op("pool", lambda e: e.memset(S, 0.0), [], [Sr])
                        P.op("pool", lambda e: e.memset(Sbf2[1], 0.0), [], [Sbf2r[1]])
                    else:
                        P.dma("sp", lambda e: e.dma_start(out=S, in_=sst[l, g].rearrange("p (a b) -> p a b", b=128)),
                              [], [Sr], "ss")
                        P.op("act", lambda e: e.activation(out=Sbf2[1], in_=S, func=AF.Copy), [Sr], [Sbf2r[1]])

                    def F1(c):
                        b = c % 2
                        b3 = c % 3
                        t0 = c * 128
                        ac = aa[:, :, t0:t0 + 128]
                        amid = aa[:, :, t0 + 63:t0 + 64].broadcast_to([128, 2, 128])
                        aend = aa[:, :, t0 + 127:t0 + 128].broadcast_to([128, 2, 128])
                        P.op("dve", lambda e: e.tensor_tensor(out=d1, in0=ac, in1=amid, op=ALU.subtract), [aar], [d1r])
                        P.op("dve", lambda e: e.tensor_tensor(out=d2, in0=ac, in1=aend, op=ALU.subtract), [aar], [d2r])
                        P.op("act", lambda e: e.activation(out=e0, in_=ac, func=AF.Exp), [aar], [e0r])
                        P.op("act", lambda e: e.activation(out=e1, in_=d1, func=AF.Exp), [d1r], [e1r])
                        P.op("act", lambda e: e.activation(out=e1n, in_=d1, func=AF.Exp, scale=-1.0), [d1r], [e1nr])
                        P.op("act", lambda e: e.activation(out=e2, in_=d2, func=AF.Exp, scale=-1.0), [d2r], [e2r])
                        P.op("act", lambda e: e.activation(out=eae2[b3], in_=aa[:, :, t0 + 127], func=AF.Exp),
                             [aar], [eae2r[b3]])
                        P.op("dve", lambda e: e.tensor_tensor(out=q02[b3], in0=sgq[:, :, t0:t0 + 128], in1=e0,
                                                              op=ALU.mult), [sgqr, e0r], [q02r[b3]])
                        P.op("dve", lambda e: e.tensor_tensor(out=qm2[b], in0=sgq[:, :, t0:t0 + 128], in1=e1,
                                                              op=ALU.mult), [sgqr, e1r], [qm2r[b]])
                        P.op("dve", lambda e: e.tensor_tensor(out=km2[b], in0=kk[:, :, t0:t0 + 128], in1=e1n,
                                                              op=ALU.mult), [kkr, e1nr], [km2r[b]])
                        P.op("dve", lambda e: e.tensor_tensor(out=Kh2[b], in0=kk[:, :, t0:t0 + 128], in1=e2,
                                                              op=ALU.mult), [kkr, e2r], [Kh2r[b]])

                    def F2(c):
                        b = c % 2
                        bS, bSr = bank()

                        def fn(e):
                            ins = None
                            for j in range(2):
                                ins = e.matmul(bS[:, j * 128:(j + 1) * 128], lhsT=km2[b][:, j, :], rhs=qm2[b][:, j, :],
                                               start=True, stop=True)
                            return ins
                        P.op("pe", fn, [km2r[b], qm2r[b]], [bSr])
                        P.op("dve", lambda e: e.tensor_scalar(
                            out=sq, in0=bS[:, 0:256], scalar1=1e30, scalar2=-1e30, op0=ALU.min, op1=ALU.max),
                            [bSr, sqr], [sqr])
                        P.op("dve", lambda e: e.tensor_tensor(
                            out=scm2[b], in0=sq.rearrange("p (a b) -> p a b", b=128),
                            in1=mask01[:, :].unsqueeze(1).broadcast_to([128, 2, 128]), op=ALU.mult),
                            [sqr, CONST], [scm2r[b]])
                        bK, bKr = bank()
                        pk = bfv(bK, 2, 128)
                        tr_group([pk[:, j, :] for j in range(2)], [Kh2[b][:, j, :] for j in range(2)], ident[:, :],
                                 [Kh2r[b], CONST], bKr)
                        P.op("act", lambda e: e.activation(out=KhT2[b], in_=pk, func=AF.Copy), [bKr], [KhT2r[b]])

                    def U(c):
                        b = c % 2
                        bD, bDr = bank()

                        def fn(e):
                            ins = None
                            for j in range(2):
                                ins = e.matmul(bD[:, j * 128:(j + 1) * 128], lhsT=KhT2[b][:, j, :],
                                               rhs=V2[:, c, j * 128:(j + 1) * 128], start=True, stop=True)
                            return ins
                        P.op("pe", fn, [KhT2r[b], V2r], [bDr])
                        P.op("dve", lambda e: e.tensor_tensor(
                            out=S, in0=S, in1=eae2[c % 3].unsqueeze(2).broadcast_to([128, 2, 128]), op=ALU.mult),
                            [Sr, eae2r[c % 3]], [Sr])
                        P.op("dve", lambda e: e.tensor_tensor(
                            out=S, in0=S, in1=bD[:, 0:256].rearrange("p (a b) -> p a b", b=128), op=ALU.add),
                            [Sr, bDr], [Sr])
                        P.op("act", lambda e: e.activation(out=Sbf2[b], in_=S, func=AF.Copy), [Sr], [Sbf2r[b]])

                    def M(c):
                        b = c % 2
                        pb = (c - 1) % 2
                        bO, bOr = bank()

                        def fn(e):
                            ins = None
                            for j in range(2):
                                e.matmul(bO[:, j * 128:(j + 1) * 128], lhsT=scm2[b][:, j, :],
                                         rhs=V2[:, c, j * 128:(j + 1) * 128], start=True, stop=False)
                                ins = e.matmul(bO[:, j * 128:(j + 1) * 128], lhsT=q02[c % 3][:, j, :], rhs=Sbf2[pb][:, j, :],
                                               start=False, stop=True)
                            return ins
                        P.op("pe", fn, [scm2r[b], V2r, q02r[c % 3], Sbf2r[pb]], [bOr])
                        smb = smB[b]; smbr = smBr[b]
                        P.op("act", lambda e: e.activation(out=sqM, in_=bO[:, 0:256], func=AF.Square), [bOr], [sqMr])
                        P.op("dve", lambda e: e.tensor_reduce(
                            out=smb[:, 2:4], in_=sqM.rearrange("p (a b) -> p a b", b=128), axis=AX.X, op=ALU.add),
                            [sqMr, smbr], [smbr])
                        P.op("pool", lambda e: e.tensor_scalar(out=smb[:, 4:6], in0=smb[:, 2:4], scalar1=1.0 / 128.0,
                                                               scalar2=4.0 * HN_EPS, op0=ALU.mult, op1=ALU.add),
                             [smbr], [smbr])
                        P.op("pool", lambda e: e.tensor_tensor(out=smb[:, 6:8], in0=smb[:, 4:6], in1=mhalf[:, 0:2],
                                                               op=ALU.pow), [smbr, CONST], [smbr])
                        P.op("dve", lambda e: e.tensor_tensor(
                            out=hn2[b], in0=bO[:, 0:256].rearrange("p (a b) -> p a b", b=128),
                            in1=smb[:, 6:8].unsqueeze(2).broadcast_to([128, 2, 128]), op=ALU.mult),
                            [bOr, smbr], [hn2r[b]])

                    def G1(c):
                        b = c % 2
                        P.op("dve", lambda e: e.scalar_tensor_tensor(
                            out=hgt2[b], in0=sG[:, c, :], scalar=1.0, in1=hn2[b].rearrange("p a b -> p (a b)"),
                            op0=ALU.add, op1=ALU.mult), [hn2r[b], sGr], [hgt2r[b]])

                    def G2(c):
                        b = c % 2
                        t0 = c * 128
                        hgt = hgt2[b]; hgtr = hgt2r[b]
                        bT, bTr = bank()
                        pv = bfv(bT, 2, 128)
                        tr_group([pv[:, j, :] for j in range(2)], [hgt[:, j * 128:(j + 1) * 128] for j in range(2)],
                                 ident[:, :], [hgtr, CONST], bTr)
                        P.op("dve", lambda e: e.tensor_tensor(
                            out=hBT[:, 2 * g:2 * g + 2, t0:t0 + 128], in0=pv,
                            in1=hgn[:, l, 2 * g:2 * g + 2].unsqueeze(2).broadcast_to([128, 2, 128]), op=ALU.mult),
                            [bTr, CONST], [hBTr[c // 4]])

                    F1(0)
                    F1(1)
                    F2(0)
                    yield
                    for i in range(NT):
                        if i + 2 < NT:
                            F1(i + 2)
                        if i + 1 < NT:
                            F2(i + 1)
                        U(i)
                        M(i)
                        if i >= 1:
                            G1(i - 1)
                        if i >= 2:
                            G2(i - 2)
                        yield
                    G1(NT - 1)
                    G2(NT - 2)
                    yield
                    G2(NT - 1)
                    P.dma("sp", lambda e: e.dma_start(out=sst[l, g].rearrange("p (a b) -> p a b", b=128), in_=S),
                          [Sr], [], "ss")
                    if l == 0 and g == 0:
                        dump("V20", V2[:, 0, :], V2r); dump("hBT0", hBT[:, 0, :], hBTr); dump("S0", S[:, 0, :], Sr)
                jobs.append((lambda si: None, bch_comp, 1.0))
            run_jobs()
            del jobs[:]

            P.barrier()
            WA.reset()
            tmpA = WA.alloc([128, NB, 512], F32); tmpAr = Reg()
            sga = WA.alloc([128, 512], F32); sgar = Reg()
            sgb = WA.alloc([128, 512], F32); sgbr = Reg()
            for fc in range(8):
                def c1_load(si, fc=fc):
                    wdma(si, sview(si, 16, 128), wsrc(dr["w_proj_a"][l], 0, 16, fc * 128, 128))

                def c1_comp(si, fc=fc):
                    wv = sview(si, 16, 128)
                    for tb in range(NB):
                        bk, br = bank()
                        mm_group(bk[:, :], [(wv[:, kc, :], hAT[:, kc, tb * 512:(tb + 1) * 512]) for kc in range(16)],
                                 [wreg[si], hATr[tb]], br)
                        P.op("act", lambda e, bk=bk, tb=tb: e.activation(out=tmpA[:, tb, :], in_=bk[:, :], func=AF.Copy),
                             [br], [tmpAr])
                jobs.append((c1_load, c1_comp))

                def c2_load(si, fc=fc):
                    wdma(si, sview(si, 24, 128)[:, 0:8, :], wsrc(dr["w_proj_b"][l], 0, 8, fc * 128, 128))
                    wdma(si, sview(si, 24, 128)[:, 8:16, :], wsrc(win, 0, 8, O_GA + fc * 128, 128))
                    wdma(si, sview(si, 24, 128)[:, 16:24, :], wsrc(win, 0, 8, O_GB + fc * 128, 128))

                def c2_comp(si, fc=fc):
                    wv = sview(si, 24, 128)
                    for tb in range(NB):
                        xs = lambda k: XT[:, k, tb * 512:(tb + 1) * 512]
                        bk, br = bank()
                        mm_group(bk[:, :], [(wv[:, 8 + k, :], xs(k)) for k in range(8)], [wreg[si]] + XTB(tb), br)
                        P.op("act", lambda e, bk=bk: e.activation(out=sga, in_=bk[:, :], func=AF.Tanh, scale=0.5,
                                                                  bias=bcolh[:, l, 48 + fc:49 + fc]), [br, CONST], [sgar])
                        bk, br = bank()
                        mm_group(bk[:, :], [(wv[:, 16 + k, :], xs(k)) for k in range(8)], [wreg[si]] + XTB(tb), br)
                        P.op("act", lambda e, bk=bk: e.activation(out=sgb, in_=bk[:, :], func=AF.Tanh, scale=0.5,
                                                                  bias=bcolh[:, l, 56 + fc:57 + fc]), [br, CONST], [sgbr])
                        bk, br = bank()
                        mm_group(bk[:, :], [(wv[:, k, :], hBT[:, k, tb * 512:(tb + 1) * 512]) for k in range(8)],
                                 [wreg[si], hBTr[tb]], br)
                        P.op("dve", lambda e, tb=tb: e.scalar_tensor_tensor(
                            out=sga, in0=sga, scalar=1.0, in1=tmpA[:, tb, :], op0=ALU.add, op1=ALU.mult),
                            [sgar, tmpAr], [sgar])
                        P.op("dve", lambda e, bk=bk: e.scalar_tensor_tensor(
                            out=sgb, in0=sgb, scalar=1.0, in1=bk[:, :], op0=ALU.add, op1=ALU.mult),
                            [br, sgbr], [sgbr])
                        P.op("dve", lambda e, tb=tb: e.tensor_tensor(
                            out=MIXT[:, fc, tb * 512:(tb + 1) * 512], in0=sga, in1=sgb, op=ALU.add),
                            [sgar, sgbr], [MIXr[tb]])
                jobs.append((c2_load, c2_comp))
            run_jobs()
            del jobs[:]

            if l == 0:
                dump("MIXT0", MIXT[:, 0, :], MIXr)
            P.barrier()
            WA.reset()
            YT = WA.alloc([128, 8, T], F32); YTr = [Reg() for _ in range(NT)]
            lnb = WA.alloc([128, 2, D], F32); lnbr = Reg()
            xres = [WA.alloc([128, D], F32) for _ in range(2)]; xresr = [Reg(), Reg()]
            rb2 = [WA.alloc([128, D], F32) for _ in range(2)]; rb2r = [Reg(), Reg()]
            xb22 = [WA.alloc([128, D], BF16) for _ in range(2)]; xb22r = [Reg(), Reg()]
            sm32 = [WA.alloc([128, 32], F32) for _ in range(2)]; sm32r = [Reg(), Reg()]
            esg = [WA.alloc([128, 512], F32) for _ in range(2)]; esgr = [Reg(), Reg()]

            def ln_stage(gname, bname, res_src, res_srcr, res_dst, res_dstr, make_xt):
                P.dma("sp", lambda e: e.dma_start(out=lnb[:, 0, :], in_=dr[gname][l, :].partition_broadcast(128)),
                      [], [lnbr], "lnb")
                P.dma("sp", lambda e: e.dma_start(out=lnb[:, 1, :], in_=dr[bname][l, :].partition_broadcast(128)),
                      [], [lnbr], "lnb")

                def LA(tt):
                    i = tt % 2
                    rb = rb2[i]; rbr = rb2r[i]; sm3 = sm32[i]; sm3r = sm32r[i]
                    P.dma("sp", lambda e: e.dma_start(out=xres[i], in_=res_src(tt)),
                          [res_srcr[tt]] if res_srcr else [], [xresr[i]], "xr%d" % i)
                    for hf in range(2):
                        bk, br = bank()
                        tr_group([bk[:, j * 128:(j + 1) * 128] for j in range(4)],
                                 [YT[:, hf * 4 + j, tt * 128:(tt + 1) * 128] for j in range(4)], identf[:, :],
                                 [YTr[tt], CONST], br)
                        P.op("dve", lambda e, bk=bk, hf=hf: e.scalar_tensor_tensor(
                            out=rb[:, hf * 512:(hf + 1) * 512], in0=xres[i][:, hf * 512:(hf + 1) * 512], scalar=ALPHA,
                            in1=bk[:, :], op0=ALU.mult, op1=ALU.add), [br, xresr[i], rbr], [rbr])
                    for hf in range(2):
                        P.op("dve", lambda e, hf=hf: e.bn_stats(out=sm3[:, hf * 6:(hf + 1) * 6],
                                                                in_=rb[:, hf * 512:(hf + 1) * 512]), [rbr, sm3r], [sm3r])
                    P.op("dve", lambda e: e.bn_aggr(out=sm3[:, 12:14], in_=sm3[:, 0:12]), [sm3r], [sm3r])
                    P.op("pool", lambda e: e.tensor_scalar(out=sm3[:, 14:15], in0=sm3[:, 13:14], scalar1=1.0, scalar2=LN_EPS, op0=ALU.mult, op1=ALU.add), [sm3r], [sm3r])
                    P.op("pool", lambda e: e.tensor_tensor(out=sm3[:, 15:16], in0=sm3[:, 14:15], in1=mhalf[:, 0:1],
                                                           op=ALU.pow), [sm3r, CONST], [sm3r])
                    P.op("dve", lambda e: e.tensor_scalar(out=sm3[:, 16:17], in0=sm3[:, 12:13], scalar1=sm3[:, 15:16],
                                                          scalar2=-1.0, op0=ALU.mult, op1=ALU.mult), [sm3r], [sm3r])

                def LB(tt):
                    i = tt % 2
                    rb = rb2[i]; rbr = rb2r[i]; sm3 = sm32[i]; sm3r = sm32r[i]
                    P.op("act", lambda e: e.activation(out=rb, in_=rb, func=AF.Identity, scale=sm3[:, 15:16],
                                                       bias=sm3[:, 16:17]), [rbr, sm3r], [rbr])
                    P.op("dve", lambda e: e.tensor_tensor(out=rb, in0=rb, in1=lnb[:, 0, :], op=ALU.mult),
                         [rbr, lnbr], [rbr])
                    P.op("dve", lambda e: e.tensor_tensor(out=rb, in0=rb, in1=lnb[:, 1, :], op=ALU.add),
                         [rbr, lnbr], [rbr])
                    if l == 0 and tt == 0:
                        dump(gname, rb, rbr)
                    P.dma("sp", lambda e: e.dma_start(out=res_dst(tt), in_=rb), [rbr], [res_dstr[tt]], "ro%d" % i)
                    if make_xt:
                        to_xt(rb, rbr, tt, xb22[i], xb22r[i])
                LA(0)
                for tt in range(NT):
                    if tt + 1 < NT:
                        LA(tt + 1)
                    LB(tt)

            for fc in range(8):
                def d_load(si, fc=fc):
                    wdma(si, sview(si, 8, 128), wsrc(dr["w_out"][l], 0, 8, fc * 128, 128))

                def d_comp(si, fc=fc):
                    wv = sview(si, 8, 128)
                    for tb in range(NB):
                        bk, br = bank()
                        mm_group(bk[:, :], [(wv[:, k, :], MIXT[:, k, tb * 512:(tb + 1) * 512]) for k in range(8)],
                                 [wreg[si], MIXr[tb]], br)
                        P.op("act", lambda e, bk=bk, tb=tb: e.activation(
                            out=YT[:, fc, tb * 512:(tb + 1) * 512], in_=bk[:, :], func=AF.Identity, scale=0.5),
                            [br], [YTr[4 * tb + i] for i in range(4)])
                jobs.append((d_load, d_comp))
            jobs.append((lambda si: None, lambda si: ln_stage(
                "ln1_g", "ln1_b", res_in, (R2 if l > 0 else None), lambda tt: res1[tt * 128:(tt + 1) * 128, :], R1, True)))

            for jb in range(NHC // 2):
                def e_load(si, jb=jb):
                    wdma(si, sview(si, 8, 256, 0, 512), wsrc(dr["w_ffn_gate"][l], 0, 8, jb * 256, 256))
                    wdma(si, sview(si, 8, 256, 256, 512), wsrc(dr["w_ffn_up"][l], 0, 8, jb * 256, 256))

                def e_comp(si, jb=jb):
                    wv = sview(si, 8, 512)
                    for j in range(2):
                        hc = 2 * jb + j
                        for tb in range(NB):
                            bg, bgr = bank()
                            mm_group(bg[:, :], [(wv[:, k, j * 128:(j + 1) * 128], XT[:, k, tb * 512:(tb + 1) * 512])
                                                for k in range(8)], [wreg[si]] + XTB(tb), bgr)
                            bu, bur = bank()
                            mm_group(bu[:, :], [(wv[:, k, 256 + j * 128:256 + (j + 1) * 128],
                                                 XT[:, k, tb * 512:(tb + 1) * 512]) for k in range(8)],
                                     [wreg[si]] + XTB(tb), bur)
                            sg = esg[(2 * j + tb) % 2]
                            sgr = esgr[(2 * j + tb) % 2]
                            P.op("act", lambda e, bg=bg, sg=sg: e.activation(out=sg, in_=bg[:, :], func=AF.Tanh,
                                                                             scale=0.5), [bgr], [sgr])
                            P.op("dve", lambda e, bg=bg, sg=sg: e.scalar_tensor_tensor(
                                out=sg, in0=sg, scalar=1.0, in1=bg[:, :], op0=ALU.add, op1=ALU.mult), [bgr, sgr], [sgr])
                            P.op("dve", lambda e, bu=bu, sg=sg, hc=hc, tb=tb: e.tensor_tensor(
                                out=HIDT[:, hc, tb * 512:(tb + 1) * 512], in0=bu[:, :], in1=sg, op=ALU.mult),
                                [bur, sgr], [HIDr[tb]])
                jobs.append((e_load, e_comp))
            for fc in range(8):
                def f_load(si, fc=fc):
                    wdma(si, sview(si, NHC, 128), wsrc(dr["w_ffn_down"][l], 0, NHC, fc * 128, 128))

                def f_comp(si, fc=fc):
                    wv = sview(si, NHC, 128)
                    for tb in range(NB):
                        bk, br = bank()
                        mm_group(bk[:, :], [(wv[:, hc, :], HIDT[:, hc, tb * 512:(tb + 1) * 512]) for hc in range(NHC)],
                                 [wreg[si], HIDr[tb]], br)
                        P.op("act", lambda e, bk=bk, tb=tb: e.activation(
                            out=YT[:, fc, tb * 512:(tb + 1) * 512], in_=bk[:, :], func=AF.Identity, scale=0.5),
                            [br], [YTr[4 * tb + i] for i in range(4)])
                jobs.append((f_load, f_comp))
            jobs.append((lambda si: None, lambda si: ln_stage(
                "ln2_g", "ln2_b", lambda tt: res1[tt * 128:(tt + 1) * 128, :], R1, res_out, R2, not last)))
            run_jobs()
            del jobs[:]

        setup()
        for seq in range(n_seq):
            for half in range(SEQ // T):
                do_pass(seq, half)
        P.barrier(final=True)
        P.emit()
    nc._dbg_names = dstate["names"]
    return nc


_CACHE = {}


def kernel(**inputs):
    n = 8
    x = np.ascontiguousarray(inputs["x"], dtype=np.float32)
    nseq = x.shape[0] // n
    key = (nseq, DEPTH)
    if key not in _CACHE:
        _CACHE[key] = build(nseq, DEPTH)
    nc = _CACHE[key]
    shared = {k: np.ascontiguousarray(v, dtype=np.float32) for k, v in inputs.items() if k != "x"}
    in_maps = []
    for i in range(n):
        m = dict(shared)
        m["x"] = x[i * nseq:(i + 1) * nseq]
        in_maps.append(m)
    res = run_bass_kernel_spmd(nc, in_maps, core_ids=list(range(n)))
    return np.concatenate([r["out"] for r in res.results], axis=0)
```

```python
import numpy as np
from contextlib import ExitStack
import concourse.bass as bass
import concourse.mybir as mybir
from concourse.bass_utils import run_bass_kernel_spmd

F32 = mybir.dt.float32
BF16 = mybir.dt.bfloat16
AF = mybir.ActivationFunctionType
ALU = mybir.AluOpType
AX = mybir.AxisListType

D = 1024
SEQ = 2048
DEPTH = 4
T = 1024
NT = T // 128
NB = T // 512
MW = 2048
FFN = 2816
NHC = FFN // 128
ALPHA = float((2 * DEPTH) ** 0.25)
OFF = [0, 2048, 4096, 6144, 8192, 8196, 8200, 9224, 10248, 11272, 12296, 13320, 14344]
O_MQ, O_MK, O_MV, O_MO, O_MI, O_MF, O_HQ, O_HF, O_HI, O_HG, O_GA, O_GB = OFF[:12]
LN_EPS = 1e-5
HN_EPS = 1e-6
SAME_SYNC = True
DBG_HALF = 0


class Reg:
    __slots__ = ("w", "r")

    def __init__(self):
        self.w = None
        self.r = {}


class Prog:
    ENG = {"pe": "tensor", "act": "scalar", "dve": "vector", "pool": "gpsimd", "sp": "sync"}

    def __init__(self, nc, stack):
        self.nc = nc
        self.stack = stack
        self.ops = {e: [] for e in self.ENG}
        self.sem = {}
        self.cnt = {}
        self.seen = {e: {} for e in self.ENG}
        for e in ("pe", "act", "dve", "pool"):
            self.newsem("c_" + e)

    def newsem(self, name):
        self.sem[name] = self.stack.enter_context(self.nc.semaphore(name))
        self.cnt[name] = 0

    def _deps(self, eng, reads, writes):
        need = {}
        for r in reads:
            if r.w is not None and need.get(r.w[0], 0) < r.w[1]:
                need[r.w[0]] = r.w[1]
        for w in writes:
            if w.w is not None and need.get(w.w[0], 0) < w.w[1]:
                need[w.w[0]] = w.w[1]
            for s, v in w.r.items():
                if need.get(s, 0) < v:
                    need[s] = v
        waits = []
        own = "c_" + eng
        seen = self.seen[eng]
        for s, v in need.items():
            if s == own and (eng == "pe" or not SAME_SYNC):
                continue
            if seen.get(s, 0) >= v:
                continue
            seen[s] = v
            waits.append((s, v))
        return waits

    def _commit(self, tok, reads, writes):
        s, v = tok
        for r in reads:
            if r.r.get(s, 0) < v:
                r.r[s] = v
        for w in writes:
            w.w = tok
            w.r = {}

    def op(self, eng, fn, reads=(), writes=()):
        waits = self._deps(eng, reads, writes)
        s = "c_" + eng
        self.cnt[s] += 1
        self.ops[eng].append((waits, fn, (s, 1)))
        self._commit((s, self.cnt[s]), reads, writes)

    def dma(self, eng, fn, reads, writes, sem):
        if sem not in self.sem:
            self.newsem(sem)
        waits = self._deps(eng, reads, writes)
        self.cnt[sem] += 16
        self.ops[eng].append((waits, fn, (sem, 16)))
        self._commit((sem, self.cnt[sem]), reads, writes)

    def barrier(self, final=False):
        for e in self.ENG:
            waits = []
            for s, c in self.cnt.items():
                if c == 0 or (s.startswith("w") and not final):
                    continue
                if s == "c_" + e:
                    continue
                if self.seen[e].get(s, 0) < c:
                    self.seen[e][s] = c
                    waits.append((s, c))
            if waits:
                self.ops[e].append((waits, None, None))

    def emit(self):
        nc = self.nc
        with nc.Block() as block:
            for e, attr in self.ENG.items():
                def mk(e):
                    def body(engine):
                        for waits, fn, inc in self.ops[e]:
                            for s, v in waits:
                                engine.wait_ge(self.sem[s], v)
                            if fn is not None:
                                ins = fn(engine)
                                ins.then_inc(self.sem[inc[0]], inc[1])
                    return body
                getattr(block, attr)(mk(e))


class Arena:
    def __init__(self, tensor, nbytes):
        self.t = tensor
        self.nbytes = nbytes
        self.off = 0

    def reset(self, off=0):
        self.off = off

    def alloc(self, shape, dtype):
        esz = 4 if dtype == F32 else 2
        n = 1
        for s in shape[1:]:
            n *= s
        nb = (n * esz + 31) // 32 * 32
        assert self.off + nb <= self.nbytes, (self.off, nb, self.nbytes)
        a = self.off // 2
        v = self.t[0:shape[0], a:a + nb // 2]
        self.off += nb
        if dtype == F32:
            v = v.bitcast(F32)
        v = v[:, 0:n]
        if len(shape) == 3:
            v = v.rearrange("p (a b) -> p a b", b=shape[2])
        return v


def build(n_seq, n_layers, debug=None):
    nc = bass.Bass("TRN2", target_bir_lowering=False)
    dr = {}

    def din(name, shape):
        dr[name] = nc.dram_tensor(name, list(shape), F32, kind="ExternalInput").ap()
    din("x", [n_seq, SEQ, D])
    din("w_in", [DEPTH, D, 14344]); din("b_in", [DEPTH, 14344])
    din("conv_w", [DEPTH, 4, 4096]); din("conv_b", [DEPTH, 4096])
    din("m_norm_g", [DEPTH, MW]); din("lb_logits", [DEPTH, 1024]); din("h_norm_g", [DEPTH, 1024])
    din("w_proj_a", [DEPTH, MW, D]); din("w_proj_b", [DEPTH, D, D]); din("w_out", [DEPTH, D, D])
    din("ln1_g", [DEPTH, D]); din("ln1_b", [DEPTH, D])
    din("w_ffn_gate", [DEPTH, D, FFN]); din("w_ffn_up", [DEPTH, D, FFN]); din("w_ffn_down", [DEPTH, FFN, D])
    din("ln2_g", [DEPTH, D]); din("ln2_b", [DEPTH, D])
    out = nc.dram_tensor("out", [n_seq, SEQ, D], F32, kind="ExternalOutput").ap()
    res1 = nc.dram_tensor("res1", [T, D], F32, kind="Internal").ap()
    res2 = nc.dram_tensor("res2", [T, D], F32, kind="Internal").ap()
    cst = nc.dram_tensor("cst", [n_layers, 4, 128, 2048], F32, kind="Internal").ap()
    sst = nc.dram_tensor("sst", [n_layers, 4, 128, 256], F32, kind="Internal").ap()
    dbg = None
    if debug is not None:
        dbg = nc.dram_tensor("dbg", list(debug), F32, kind="ExternalOutput").ap()

    with ExitStack() as st:
        P = Prog(nc, st)

        def sb(name, shape, dt):
            return st.enter_context(nc.sbuf_tensor(name, list(shape), dt))
        XT = sb("XT", [128, 8, T], BF16); XTr = [Reg() for _ in range(NT)]
        MIXb = sb("MIX", [128, 8 * T], BF16)
        MIXT = MIXb[:, :].rearrange("p (a b) -> p a b", b=T); MIXr = [Reg() for _ in range(NB)]
        BIGb = sb("BIG", [128, 24 * T], BF16)
        hAT = BIGb[:, 0:16 * T].rearrange("p (a b) -> p a b", b=T); hATr = [Reg() for _ in range(NB)]
        hBT = BIGb[:, 16 * T:24 * T].rearrange("p (a b) -> p a b", b=T); hBTr = [Reg() for _ in range(NB)]
        HIDT = BIGb[:, 0:NHC * T].rearrange("p (a b) -> p a b", b=T); HIDr = [Reg() for _ in range(NB)]
        WORKB = 75 * 1024 + 512
        WORKt = sb("WORK", [128, WORKB // 2], BF16)
        WA = Arena(WORKt, WORKB)
        MA = Arena(MIXb, 16 * 1024)
        NSLOT = 3
        wslot = [sb("wslot%d" % i, [128, 4096], BF16) for i in range(NSLOT)]
        wreg = [Reg() for _ in range(NSLOT)]
        bbc = [sb("bbc%d" % i, [128, 512], F32) for i in range(NSLOT)]
        bbr = [Reg() for _ in range(NSLOT)]
        GAM = sb("GAM", [128, T + 2], F32); GAMr = Reg()
        gprow = sb("gprow", [4, T + 2], F32); gprr = Reg()
        ident = sb("ident", [128, 128], BF16)
        identf = sb("identf", [128, 128], F32)
        maskbig = sb("maskbig", [128, 128], F32)
        mask01 = sb("mask01", [128, 128], F32)
        ones_row = sb("ones_row", [4, 8], F32)
        ones_bf = sb("ones_bf", [128, 2], BF16)
        cmk = sb("cmk", [128, T], BF16)
        sel = sb("sel", [4, 4, 128], F32)
        i4 = sb("i4", [4, 4], F32)
        bcol = sb("bcol", [128, DEPTH, 64], F32)
        cw = sb("cw", [128, DEPTH, 32, 4], F32)
        cb = sb("cb", [128, DEPTH, 32], F32)
        mg = sb("mg", [128, DEPTH, 16], F32)
        hgn = sb("hgn", [128, DEPTH, 8], F32)
        lbl = sb("lbl", [128, 8, DEPTH], F32)
        lbp = sb("lbp", [128, 8, DEPTH], F32)
        lb = sb("lb", [128, DEPTH, 8], F32)
        oml = sb("oml", [128, DEPTH, 8], F32)
        gbias = sb("gbias", [4, DEPTH, 2], F32)
        car = sb("car", [4, DEPTH, 4], F32); carr = Reg()
        ccar = sb("ccar", [128, DEPTH, 32, 4], BF16); ccr = Reg()
        nst = sb("nst", [128, DEPTH, 4, 4], F32); nsr = Reg()
        acolA = sb("acolA", [128, NT, 4], F32)
        fcolA = sb("fcolA", [128, NT, 4], F32); colr = Reg()
        small = sb("small", [128, 64], F32)
        cbh = sb("cbh", [128, DEPTH, 32], F32)
        bcolh = sb("bcolh", [128, DEPTH, 64], F32)
        lbc0 = sb("lbc0", [128, DEPTH, 8], F32)
        lbc1 = sb("lbc1", [128, DEPTH, 8], F32)
        mhalf = sb("mhalf", [128, 8], F32)
        CONST = Reg()
        banks = [st.enter_context(nc.psum_tensor("bank%d" % i, [128, 512], F32)) for i in range(8)]
        bankr = [Reg() for _ in range(8)]
        bstate = [0]

        dstate = {"on": False, "names": []}

        def dump(name, ap, reg, p=128):
            if dbg is None or not dstate["on"] or name in dstate["names"] or len(dstate["names"]) >= dbg.shape[0]:
                return
            i = len(dstate["names"])
            dstate["names"].append(name)
            n = ap.shape[-1] if len(ap.shape) == 2 else None
            regs = reg if isinstance(reg, list) else [reg]
            P.dma("pool", lambda e: e.dma_start(out=dbg[i, 0:p, 0:n], in_=ap), regs, [], "dbg")

        def bank():
            i = bstate[0]
            bstate[0] = (i + 1) % 8
            return banks[i], bankr[i]

        def bfv(bk, a, b):
            return bk[:, :].bitcast(BF16)[:, 0:a * b].rearrange("p (a b) -> p a b", b=b)

        def setup():
            def c1(e):
                e.memset(ident[:, :], 0.0)
                e.memset(identf[:, :], 0.0)
                e.memset(maskbig[:, :], 0.0)
                e.memset(mask01[:, :], 1.0)
                e.memset(ones_row[:, :], 1.0)
                e.memset(ones_bf[:, :], 1.0)
                e.memset(cmk[:, :], 1.0)
                e.memset(sel[:, :, :], 0.0)
                e.memset(i4[:, :], 0.0)
                e.memset(car[:, :, :], 0.0)
                e.memset(ccar[:, :, :, :], 0.0)
                e.memset(nst[:, :, :, :], 0.0)
                e.memset(gprow[:, :], 0.0)
                e.memset(mhalf[:, :], -0.5)
                return e.memset(lb[:, :, :], 0.0)
            P.op("pool", c1, [], [CONST])

            def c2(e):
                e.affine_select(out=ident[:, :], in_=ident[:, :], pattern=[[-1, 128]], compare_op=ALU.not_equal,
                                fill=1.0, base=0, channel_multiplier=1)
                e.affine_select(out=identf[:, :], in_=identf[:, :], pattern=[[-1, 128]], compare_op=ALU.not_equal,
                                fill=1.0, base=0, channel_multiplier=1)
                e.affine_select(out=maskbig[:, :], in_=maskbig[:, :], pattern=[[1, 128]], compare_op=ALU.is_ge,
                                fill=30000.0, base=0, channel_multiplier=-1)
                e.affine_select(out=mask01[:, :], in_=mask01[:, :], pattern=[[1, 128]], compare_op=ALU.is_ge,
                                fill=0.0, base=0, channel_multiplier=-1)
                e.affine_select(out=sel[:, :, :], in_=sel[:, :, :], pattern=[[1, 4], [0, 128]],
                                compare_op=ALU.not_equal, fill=1.0, base=0, channel_multiplier=-1)
                e.affine_select(out=i4[:, :], in_=i4[:, :], pattern=[[1, 4]], compare_op=ALU.not_equal,
                                fill=1.0, base=0, channel_multiplier=-1)
                return e.memset(cmk[:, :].rearrange("p (c l) -> p c l", l=128)[:, :, 0:1], 0.0)
            P.op("pool", c2, [CONST], [CONST])

            segs = [(O_MQ, 16, 0), (O_MK, 16, 16), (O_HQ, 8, 32), (O_HF, 8, 40), (O_GA, 8, 48), (O_GB, 8, 56)]
            for l in range(n_layers):
                for (o, n, c0) in segs:
                    P.dma("sp", lambda e, l=l, o=o, n=n, c0=c0: e.dma_start(
                        out=bcol[:, l, c0:c0 + n], in_=dr["b_in"][l, o:o + n * 128].rearrange("(c p) -> p c", p=128),
                        allow_slow_non_contiguous=True), [], [CONST], "cst")
                for j in range(4):
                    for c8 in range(4):
                        P.dma("sp", lambda e, l=l, j=j, c8=c8: e.dma_start(
                            out=cw[:, l, c8 * 8:(c8 + 1) * 8, j],
                            in_=dr["conv_w"][l, j, c8 * 1024:(c8 + 1) * 1024].rearrange("(c p) -> p c", p=128),
                            allow_slow_non_contiguous=True), [], [CONST], "cst")
                for c8 in range(2):
                    P.dma("sp", lambda e, l=l, c8=c8: e.dma_start(
                        out=cb[:, l, c8 * 16:(c8 + 1) * 16],
                        in_=dr["conv_b"][l, c8 * 2048:(c8 + 1) * 2048].rearrange("(c p) -> p c", p=128),
                        allow_slow_non_contiguous=True), [], [CONST], "cst")
                P.dma("sp", lambda e, l=l: e.dma_start(
                    out=mg[:, l, :], in_=dr["m_norm_g"][l, :].rearrange("(c p) -> p c", p=128),
                    allow_slow_non_contiguous=True), [], [CONST], "cst")
                P.dma("sp", lambda e, l=l: e.dma_start(
                    out=hgn[:, l, :], in_=dr["h_norm_g"][l, :].rearrange("(c p) -> p c", p=128),
                    allow_slow_non_contiguous=True), [], [CONST], "cst")
                P.dma("sp", lambda e, l=l: e.dma_start(
                    out=gbias[:, l, :], in_=dr["b_in"][l, O_MI:O_MI + 8].rearrange("(g h) -> h g", h=4),
                    allow_slow_non_contiguous=True), [], [CONST], "cst")
            for l in range(DEPTH):
                P.dma("sp", lambda e, l=l: e.dma_start(
                    out=lbl[:, :, l], in_=dr["lb_logits"][l, :].rearrange("(c p) -> p c", p=128),
                    allow_slow_non_contiguous=True), [], [CONST], "cst")
            P.op("act", lambda e: e.activation(out=lbp[:, :, :], in_=lbl[:, :, :], func=AF.Exp), [CONST], [CONST])
            P.op("dve", lambda e: e.tensor_reduce(out=small[:, 0:8], in_=lbp[:, :, :], axis=AX.X, op=ALU.add),
                 [CONST], [CONST])
            P.op("dve", lambda e: e.reciprocal(out=small[:, 8:16], in_=small[:, 0:8]), [CONST], [CONST])
            P.op("dve", lambda e: e.tensor_tensor(out=lbp[:, :, :], in0=lbp[:, :, :],
                                                  in1=small[:, 8:16].unsqueeze(2).broadcast_to([128, 8, DEPTH]),
                                                  op=ALU.mult), [CONST], [CONST])
            for l in range(1, DEPTH):
                P.op("dve", lambda e, l=l: e.tensor_tensor(out=lb[:, l, :], in0=lb[:, l - 1, :], in1=lbp[:, :, l],
                                                           op=ALU.add), [CONST], [CONST])
            P.op("dve", lambda e: e.tensor_scalar(out=oml[:, :, :], in0=lb[:, :, :], scalar1=-1.0, scalar2=1.0,
                                                  op0=ALU.mult, op1=ALU.add), [CONST], [CONST])
            P.op("dve", lambda e: e.tensor_scalar(out=lbc1[:, :, :], in0=oml[:, :, :], scalar1=0.5, scalar2=None,
                                                  op0=ALU.mult), [CONST], [CONST])
            P.op("dve", lambda e: e.tensor_tensor(out=lbc0[:, :, :], in0=lb[:, :, :], in1=lbc1[:, :, :], op=ALU.add),
                 [CONST], [CONST])
            P.op("dve", lambda e: e.tensor_scalar(out=cbh[:, :, :], in0=cb[:, :, :], scalar1=0.5, scalar2=None,
                                                  op0=ALU.mult), [CONST], [CONST])
            P.op("dve", lambda e: e.tensor_scalar(out=bcolh[:, :, :], in0=bcol[:, :, :], scalar1=0.5, scalar2=None,
                                                  op0=ALU.mult), [CONST], [CONST])
            P.op("dve", lambda e: e.tensor_scalar(out=mg[:, :, :], in0=mg[:, :, :], scalar1=0.5, scalar2=None,
                                                  op0=ALU.mult), [CONST], [CONST])
            P.op("dve", lambda e: e.tensor_scalar(out=hgn[:, :, :], in0=hgn[:, :, :], scalar1=0.5, scalar2=None,
                                                  op0=ALU.mult), [CONST], [CONST])

        jobs = []

        def wdma(slot_i, view, src):
            P.dma("pool", lambda e: e.dma_start(out=view, in_=src), [], [wreg[slot_i]], "w%d" % slot_i)

        def wsrc(w2d, k0, nk, c0, ncol):
            return w2d[k0 * 128:(k0 + nk) * 128, c0:c0 + ncol].rearrange("(k p) n -> p k n", p=128)

        def sview(slot_i, nk, ncol, col0=0, tot=None):
            tot = tot or ncol
            return wslot[slot_i][:, 0:nk * tot].rearrange("p (k c) -> p k c", c=tot)[:, :, col0:col0 + ncol]

        def run_jobs():
            issued = 0
            bg = [None, 0.0, 0.0]

            def step_bg():
                if bg[0] is None:
                    return
                try:
                    next(bg[0])
                except StopIteration:
                    bg[0] = None

            def drain_bg():
                while bg[0] is not None:
                    step_bg()
            for idx in range(len(jobs)):
                while issued < len(jobs) and issued <= idx + 2:
                    jobs[issued][0](issued % NSLOT)
                    issued += 1
                job = jobs[idx]
                g = job[1](idx % NSLOT)
                if len(job) > 2:
                    drain_bg()
                    bg[0] = g
                    bg[1] = job[2]
                    bg[2] = 0.0
                    step_bg()
                    continue
                if g is None:
                    continue
                for _ in g:
                    bg[2] += bg[1]
                    while bg[2] >= 1.0:
                        bg[2] -= 1.0
                        step_bg()
            drain_bg()

        def mm_group(out_ap, pairs, rd, wr):
            n = len(pairs)

            def fn(e):
                ins = None
                for i, (a, b) in enumerate(pairs):
                    ins = e.matmul(out_ap, lhsT=a, rhs=b, start=(i == 0), stop=(i == n - 1))
                return ins
            P.op("pe", fn, rd, [wr])

        def tr_group(dst_views, src_views, idn, rd, wr):
            def fn(e):
                ins = None
                for d_, s_ in zip(dst_views, src_views):
                    ins = e.transpose(out=d_, in_=s_, identity=idn)
                return ins
            P.op("pe", fn, rd, [wr])

        def to_xt(tile_f32, treg, tt, xb, xbr):
            P.op("act", lambda e: e.activation(out=xb, in_=tile_f32, func=AF.Copy), [treg], [xbr])
            bk, br = bank()
            pv = bfv(bk, 8, 128)
            tr_group([pv[:, k, :] for k in range(8)], [xb[:, k * 128:(k + 1) * 128] for k in range(8)],
                     ident[:, :], [xbr, CONST], br)
            P.op("dve", lambda e: e.tensor_copy(out=XT[:, :, tt * 128:(tt + 1) * 128], in_=pv), [br], [XTr[tt]])

        XTB = lambda tb: [XTr[4 * tb + i] for i in range(4)]

        def do_pass(seq, half):
            tok0 = half * T
            first = (half == 0)
            dstate["on"] = (seq == 0 and half == DBG_HALF)
            P.barrier()
            WA.reset()
            xin = [WA.alloc([128, D], F32) for _ in range(2)]
            xinr = [Reg(), Reg()]
            xb = [WA.alloc([128, D], BF16) for _ in range(2)]
            xbr = [Reg(), Reg()]
            for tt in range(NT):
                i = tt % 2
                P.dma("sp", lambda e, tt=tt, i=i: e.dma_start(
                    out=xin[i], in_=dr["x"][seq, tok0 + tt * 128: tok0 + (tt + 1) * 128, :]),
                    [], [xinr[i]], "xi%d" % i)
                to_xt(xin[i], xinr[i], tt, xb[i], xbr[i])
            dump("XT0", XT[:, 0, :], list(XTr))
            for l in range(n_layers):
                last = (l == n_layers - 1)
                res_in = (lambda tt: dr["x"][seq, tok0 + tt * 128: tok0 + (tt + 1) * 128, :]) if l == 0 else \
                    (lambda tt: res2[tt * 128:(tt + 1) * 128, :])
                res_out = (lambda tt: out[seq, tok0 + tt * 128: tok0 + (tt + 1) * 128, :]) if last else \
                    (lambda tt: res2[tt * 128:(tt + 1) * 128, :])
                layer(l, first, res_in, res_out, last)

        R1 = [Reg() for _ in range(NT)]
        R2 = [Reg() for _ in range(NT)]

        def layer(l, first, res_in, res_out, last):
            win = dr["w_in"][l]
            bin_ = dr["b_in"][l]
            del jobs[:]
            P.barrier()
            WA.reset(); MA.reset()
            rowr = Reg()
            hbv = BIGb[:, 16 * T:24 * T]
            ASET = [
                dict(qT=WA.alloc([128, 4, T], BF16), kT=WA.alloc([128, 4, T], BF16),
                     V=WA.alloc([128, NT, 512], BF16), sO=WA.alloc([128, NT, 512], BF16),
                     qTr=Reg(), kTr=Reg(), Vr=Reg(), sOr=Reg(), xw=[]),
                dict(qT=MIXb[:, 0:4 * T].rearrange("p (a b) -> p a b", b=T),
                     kT=MIXb[:, 4 * T:8 * T].rearrange("p (a b) -> p a b", b=T),
                     V=hbv[:, 0:NT * 512].rearrange("p (a b) -> p a b", b=512),
                     sO=hbv[:, NT * 512:2 * NT * 512].rearrange("p (a b) -> p a b", b=512),
                     qTr=Reg(), kTr=Reg(), Vr=Reg(), sOr=Reg(), xw=[rowr]),
            ]
            ub = [WA.alloc([128, T + 4], BF16) for _ in range(2)]; ubr = [Reg(), Reg()]
            C = WA.alloc([128, 4, 512], F32); Cr = Reg()
            Cbf2 = [WA.alloc([128, 4, 512], BF16) for _ in range(2)]; Cbf2r = [Reg(), Reg()]
            nbf2 = [WA.alloc([128, 4], BF16) for _ in range(2)]; nbf2r = [Reg(), Reg()]
            dg = WA.alloc([128, 4, 128], BF16); dgr = Reg()
            tE = WA.alloc([128, 128], F32); tEr = Reg()
            tD = WA.alloc([128, 128], F32); tDr = Reg()
            tS2 = [WA.alloc([128, 128], F32) for _ in range(3)]; tS2r = [Reg() for _ in range(3)]
            tW2 = [WA.alloc([128, 128], BF16) for _ in range(3)]; tW2r = [Reg() for _ in range(3)]
            qTp2 = [WA.alloc([128, 4, 128], BF16) for _ in range(3)]; qTp2r = [Reg() for _ in range(3)]
            hb2 = [WA.alloc([128, 512], F32) for _ in range(2)]; hb2r = [Reg(), Reg()]
            hg2 = [WA.alloc([128, 512], BF16) for _ in range(2)]; hg2r = [Reg(), Reg()]
            Kp2 = [WA.alloc([128, 512], BF16) for _ in range(3)]; Kp2r = [Reg() for _ in range(3)]
            smM = [WA.alloc([128, 8], F32) for _ in range(2)]; smMr = [Reg(), Reg()]
            tmpo = WA.alloc([128, 512], F32); tmpor = Reg()
            tht = [WA.alloc([128, 512], BF16) for _ in range(2)]; thtr = [Reg(), Reg()]
            thx = [WA.alloc([128, 512], BF16) for _ in range(2)]; thxr = [Reg(), Reg()]
            sm = WA.alloc([128, 16], F32); smr = Reg()
            r_i = MA.alloc([4, T], F32); r_f = MA.alloc([4, T], F32)
            r_B = MA.alloc([4, T], F32); r_m = MA.alloc([4, T], F32)

            def g_load(si):
                wdma(si, sview(si, 8, 8), wsrc(win, 0, 8, O_MI, 8))

            def g_comp(si):
                wv = sview(si, 8, 8)
                for g, dst in ((0, r_i), (1, r_f)):
                    for tb in range(NB):
                        bk, br = bank()
                        mm_group(bk[0:4, :], [(wv[:, k, g * 4:(g + 1) * 4], XT[:, k, tb * 512:(tb + 1) * 512])
                                              for k in range(8)], [wreg[si]] + XTB(tb), br)
                        P.op("act", lambda e, bk=bk, dst=dst, tb=tb, g=g: e.activation(
                            out=dst[:, tb * 512:(tb + 1) * 512], in_=bk[0:4, :], func=AF.Identity,
                            bias=gbias[:, l, g:g + 1]), [br, CONST], [rowr])
                P.op("act", lambda e: e.activation(out=r_f, in_=r_f, func=AF.Exp, scale=-1.0), [rowr], [rowr])
                P.op("act", lambda e: e.activation(out=r_f, in_=r_f, func=AF.Ln, bias=1.0), [rowr], [rowr])
                P.op("dve", lambda e: e.tensor_scalar(out=r_f, in0=r_f, scalar1=-1.0, scalar2=None, op0=ALU.mult),
                     [rowr], [rowr])
                if first:
                    P.op("pool", lambda e: e.memset(car[:, l, :], 0.0), [carr], [carr])
                P.op("dve", lambda e: e.tensor_tensor_scan(
                    out=r_B, data0=ones_row[:, 0:1].broadcast_to([4, T]), data1=r_f, initial=car[:, l, 0:1],
                    op0=ALU.mult, op1=ALU.add), [rowr, carr, CONST], [rowr])
                P.op("dve", lambda e: e.tensor_tensor_scan(
                    out=r_m, data0=r_f, data1=r_i, initial=car[:, l, 1:2], op0=ALU.add, op1=ALU.max),
                    [rowr, carr], [rowr])
                P.op("dve", lambda e: e.tensor_copy(out=gprow[:, 1:2], in_=car[:, l, 2:3]), [carr, gprr], [gprr])
                P.op("dve", lambda e: e.tensor_tensor(out=gprow[:, 2:T + 2], in0=r_m, in1=r_B, op=ALU.subtract),
                     [rowr, gprr], [gprr])
                P.op("dve", lambda e: e.tensor_tensor(out=r_i, in0=r_i, in1=r_B, op=ALU.subtract), [rowr], [rowr])
                P.op("dve", lambda e: e.tensor_copy(out=car[:, l, 0:1], in_=r_B[:, T - 1:T]), [rowr, carr], [carr])
                P.op("dve", lambda e: e.tensor_copy(out=car[:, l, 1:2], in_=r_m[:, T - 1:T]), [rowr, carr], [carr])
                P.op("dve", lambda e: e.tensor_copy(out=car[:, l, 2:3], in_=gprow[:, T + 1:T + 2]),
                     [gprr, carr], [carr])
                for src, dstc, isf in ((r_i, acolA, False), (r_m, fcolA, True)):
                    bk, br = bank()

                    def fn(e, src=src, bk=bk):
                        ins = None
                        for c in range(NT):
                            ins = e.matmul(bk[:, c * 4:(c + 1) * 4], lhsT=src[:, c * 128:(c + 1) * 128],
                                           rhs=i4[:, :], start=True, stop=True)
                        return ins
                    P.op("pe", fn, [rowr, CONST], [br])
                    if isf:
                        P.op("act", lambda e, bk=bk, dstc=dstc: e.activation(
                            out=dstc[:, :, :], in_=bk[:, 0:NT * 4].rearrange("p (c h) -> p c h", h=4),
                            func=AF.Exp, scale=-1.0, bias=float(np.log(4.0 * np.sqrt(512.0)))), [br], [colr])
                    else:
                        P.op("act", lambda e, bk=bk, dstc=dstc: e.activation(
                            out=dstc[:, :, :], in_=bk[:, 0:NT * 4].rearrange("p (c h) -> p c h", h=4),
                            func=AF.Identity), [br], [colr])
            jobs.append((g_load, g_comp))

            for h in range(4):
                BS = ASET[h % 2]
                for which, (o_seg, dstT, dstr, bc0) in enumerate(((O_MQ, BS["qT"], BS["qTr"], 0),
                                                                  (O_MK, BS["kT"], BS["kTr"], 16))):
                    def qk_load(si, o_seg=o_seg, h=h):
                        wdma(si, sview(si, 8, 512), wsrc(win, 0, 8, o_seg + h * 512, 512))

                    def qk_comp(si, o_seg=o_seg, h=h, dstT=dstT, dstr=dstr, bc0=bc0, which=which, xw=BS["xw"]):
                        wv = sview(si, 8, 512)
                        for dc in range(4):
                            ch = which * 16 + h * 4 + dc
                            u = ub[dc % 2]; ur = ubr[dc % 2]
                            if first:
                                P.op("pool", lambda e, u=u: e.memset(u[:, 0:4], 0.0), [], [ur])
                            else:
                                P.op("act", lambda e, u=u, ch=ch: e.activation(out=u[:, 0:4], in_=ccar[:, l, ch, :],
                                                                               func=AF.Copy), [ccr], [ur])
                            for tb in range(NB):
                                bk, br = bank()
                                mm_group(bk[:, :], [(wv[:, k, dc * 128:(dc + 1) * 128],
                                                     XT[:, k, tb * 512:(tb + 1) * 512]) for k in range(8)],
                                         [wreg[si]] + XTB(tb), br)
                                P.op("act", lambda e, bk=bk, u=u, tb=tb, dc=dc: e.activation(
                                    out=u[:, 4 + tb * 512: 4 + (tb + 1) * 512], in_=bk[:, :], func=AF.Identity,
                                    bias=bcol[:, l, bc0 + h * 4 + dc: bc0 + h * 4 + dc + 1]), [br, CONST], [ur])
                            P.op("act", lambda e, u=u, ch=ch: e.activation(out=ccar[:, l, ch, :], in_=u[:, T:T + 4],
                                                                           func=AF.Copy), [ur], [ccr])
                            for j in range(4):
                                P.op("dve", lambda e, j=j, ch=ch: e.tensor_scalar(
                                    out=dg[:, j, :], in0=ident[:, :], scalar1=cw[:, l, ch, j:j + 1], scalar2=None,
                                    op0=ALU.mult), [CONST], [dgr])
                            for tb in range(NB):
                                bk, br = bank()
                                mm_group(bk[:, :], [(dg[:, j, :], u[:, 1 + j + tb * 512: 1 + j + (tb + 1) * 512])
                                                    for j in range(4)], [dgr, ur], br)
                                th = tht[tb]; thr = thtr[tb]
                                P.op("act", lambda e, bk=bk, ch=ch, th=th: e.activation(
                                    out=th, in_=bk[:, :], func=AF.Tanh, scale=0.5, bias=cbh[:, l, ch:ch + 1]),
                                    [br, CONST], [thr])
                                xp = thx[tb]; xpr = thxr[tb]
                                P.op("act", lambda e, bk=bk, ch=ch, xp=xp: e.activation(
                                    out=xp, in_=bk[:, :], func=AF.Identity, bias=cb[:, l, ch:ch + 1]),
                                    [br, CONST], [xpr])
                                P.op("dve", lambda e, tb=tb, dc=dc, th=th, xp=xp: e.scalar_tensor_tensor(
                                    out=dstT[:, dc, tb * 512:(tb + 1) * 512], in0=th, scalar=1.0,
                                    in1=xp, op0=ALU.add, op1=ALU.mult), [thr, xpr], [dstr] + xw)
                            yield
                    jobs.append((qk_load, qk_comp))
                for which, o_seg in enumerate((O_MV, O_MO)):
                    def vo_load(si, o_seg=o_seg, h=h, which=which):
                        wdma(si, sview(si, 8, 512), wsrc(win, 0, 8, o_seg + h * 512, 512))
                        P.dma("sp", lambda e: e.dma_start(
                            out=bbc[si][:, :], in_=bin_[o_seg + h * 512: o_seg + (h + 1) * 512].partition_broadcast(128)),
                            [], [bbr[si]], "bb%d" % si)

                    def vo_comp(si, which=which, V=BS["V"], Vr=BS["Vr"], sO=BS["sO"], sOr=BS["sOr"]):
                        wv = sview(si, 8, 512)
                        for tt in range(NT):
                            bk, br = bank()
                            mm_group(bk[:, :], [(XT[:, k, tt * 128:(tt + 1) * 128], wv[:, k, :]) for k in range(8)],
                                     [wreg[si], XTr[tt]], br)
                            if which == 0:
                                P.op("dve", lambda e, bk=bk, tt=tt: e.tensor_tensor(
                                    out=V[:, tt, :], in0=bk[:, :], in1=bbc[si][:, :], op=ALU.add), [br, bbr[si]], [Vr])
                            else:
                                P.op("dve", lambda e, bk=bk: e.tensor_tensor(
                                    out=tmpo, in0=bk[:, :], in1=bbc[si][:, :], op=ALU.add), [br, bbr[si]], [tmpor])
                                P.op("act", lambda e, tt=tt: e.activation(out=sO[:, tt, :], in_=tmpo, func=AF.Tanh,
                                                                          scale=0.5), [tmpor], [sOr])
                            yield
                    jobs.append((vo_load, vo_comp))
                def ch_load(si):
                    pass

                def ch_comp(si, h=h, BS=BS):
                    qT = BS["qT"]; kT = BS["kT"]; V = BS["V"]; sO = BS["sO"]
                    qTr = BS["qTr"]; kTr = BS["kTr"]; Vr = BS["Vr"]; sOr = BS["sOr"]
                    if first:
                        P.op("pool", lambda e: e.memset(C, 0.0), [], [Cr])
                        P.op("pool", lambda e: e.memset(Cbf2[1], 0.0), [], [Cbf2r[1]])
                        P.op("pool", lambda e: e.memset(nst[:, l, h, :], 0.0), [], [nsr])
                    else:
                        P.dma("sp", lambda e: e.dma_start(
                            out=C, in_=cst[l, h].rearrange("p (a b) -> p a b", b=512)), [], [Cr], "cs")
                        P.op("act", lambda e: e.activation(out=Cbf2[1], in_=C, func=AF.Copy), [Cr], [Cbf2r[1]])
                    P.op("act", lambda e: e.activation(out=nbf2[1], in_=nst[:, l, h, :], func=AF.Copy), [nsr], [nbf2r[1]])
                    for (a, b) in ((0, 512), (512, 1024), (1024, 1026)):
                        bk, br = bank()
                        mm_group(bk[:, 0:b - a], [(sel[:, h, :], gprow[:, a:b])], [gprr, CONST], br)
                        P.op("act", lambda e, bk=bk, a=a, b=b: e.activation(
                            out=GAM[:, a:b], in_=bk[:, 0:b - a], func=AF.Identity), [br], [GAMr])
                    if l == 0 and h == 0:
                        dump("qT0", qT[:, 0, :], qTr); dump("kT0", kT[:, 0, :], kTr)
                        dump("V0", V[:, 0, :], Vr); dump("sO0", sO[:, 0, :], sOr)
                        dump("GAM", GAM[:, 0:1024], GAMr); dump("acol", acolA[:, :, :].rearrange("p a b -> p (a b)"), colr)
                        dump("fcol", fcolA[:, :, :].rearrange("p a b -> p (a b)"), colr)
                        dump("gprow", gprow[:, 0:1024], gprr, p=4)

                    def F(c):
                        b = c % 3
                        t0 = c * 128
                        gs = GAM[:, 2 + t0: 2 + t0 + 128]
                        bS, bSr = bank()
                        mm_group(bS[:, 0:128], [(kT[:, dc, t0:t0 + 128], qT[:, dc, t0:t0 + 128]) for dc in range(4)],
                                 [kTr, qTr], bSr)
                        bK, bKr = bank()
                        pk = bfv(bK, 4, 128)
                        tr_group([pk[:, dc, :] for dc in range(4)], [kT[:, dc, t0:t0 + 128] for dc in range(4)],
                                 ident[:, :], [kTr, CONST], bKr)
                        P.op("act", lambda e: e.activation(
                            out=tS2[b], in_=gs, func=AF.Exp, scale=-1.0, bias=GAM[:, 1 + t0: 2 + t0]), [GAMr], [tS2r[b]])
                        P.op("dve", lambda e: e.scalar_tensor_tensor(
                            out=tE, in0=gs, scalar=acolA[:, c, h:h + 1], in1=maskbig[:, :], op0=ALU.subtract,
                            op1=ALU.max), [GAMr, colr, CONST], [tEr])
                        P.op("act", lambda e: e.activation(out=tD, in_=tE, func=AF.Exp, scale=-1.0), [tEr], [tDr])
                        P.op("dve", lambda e: e.tensor_tensor(
                            out=qTp2[b], in0=qT[:, :, t0:t0 + 128], in1=tS2[b].unsqueeze(1).broadcast_to([128, 4, 128]),
                            op=ALU.mult), [qTr, tS2r[b]], [qTp2r[b]])
                        P.op("dve", lambda e: e.tensor_tensor(out=tW2[b], in0=bS[:, 0:128], in1=tD, op=ALU.mult),
                             [bSr, tDr], [tW2r[b]])
                        P.op("act", lambda e: e.activation(
                            out=Kp2[b], in_=bK[:, :].bitcast(BF16)[:, 0:512], func=AF.Identity, scale=tD[:, 127:128]),
                            [bKr, tDr], [Kp2r[b]])

                    def U(c):
                        b = c % 2
                        kb = c % 3
                        for dc in range(4):
                            bC, bCr = bank()
                            mm_group(bC[:, :], [(Kp2[kb][:, dc * 128:(dc + 1) * 128], V[:, c, :])], [Kp2r[kb], Vr], bCr)
                            P.op("dve", lambda e, bC=bC, dc=dc: e.scalar_tensor_tensor(
                                out=C[:, dc, :], in0=C[:, dc, :], scalar=tS2[kb][:, 127:128], in1=bC[:, :], op0=ALU.mult,
                                op1=ALU.add), [bCr, tS2r[kb], Cr], [Cr])
                        P.op("act", lambda e: e.activation(out=Cbf2[b], in_=C, func=AF.Copy), [Cr], [Cbf2r[b]])
                        bn_, bnr = bank()

                        def fn(e):
                            ins = None
                            for dc in range(4):
                                ins = e.matmul(bn_[:, 2 * dc:2 * dc + 1], lhsT=Kp2[kb][:, dc * 128:(dc + 1) * 128],
                                               rhs=ones_bf[:, 0:1], start=True, stop=True)
                            return ins
                        P.op("pe", fn, [Kp2r[kb], CONST], [bnr])
                        P.op("dve", lambda e: e.scalar_tensor_tensor(
                            out=nst[:, l, h, :], in0=nst[:, l, h, :], scalar=tS2[kb][:, 127:128],
                            in1=bn_[:, 0:8].rearrange("p (a b) -> p a b", b=2)[:, :, 0], op0=ALU.mult, op1=ALU.add),
                            [bnr, tS2r[kb], nsr], [nsr])
                        P.op("act", lambda e: e.activation(out=nbf2[b], in_=nst[:, l, h, :], func=AF.Copy),
                             [nsr], [nbf2r[b]])

                    def M(c):
                        b = c % 2
                        pb = (c - 1) % 2
                        kb = c % 3
                        bN, bNr = bank()
                        mm_group(bN[:, :], [(tW2[kb], V[:, c, :])] + [(qTp2[kb][:, dc, :], Cbf2[pb][:, dc, :])
                                                                     for dc in range(4)],
                                 [tW2r[kb], Vr, qTp2r[kb], Cbf2r[pb]], bNr)
                        bD, bDr = bank()
                        mm_group(bD[:, 0:1], [(tW2[kb], ones_bf[:, 0:1])] + [(qTp2[kb][:, dc, :], nbf2[pb][:, dc:dc + 1])
                                                                           for dc in range(4)],
                                 [tW2r[kb], CONST, qTp2r[kb], nbf2r[pb]], bDr)
                        mst[b] = (bN, bNr, bD, bDr)

                    def M_ew(c):
                        b = c % 2
                        bN, bNr, bD, bDr = mst[b]
                        smm = smM[b]; smmr = smMr[b]
                        P.op("act", lambda e: e.activation(out=smm[:, 0:1], in_=bD[:, 0:1], func=AF.Abs),
                             [bDr, smmr], [smmr])
                        P.op("dve", lambda e: e.tensor_scalar(
                            out=smm[:, 1:2], in0=smm[:, 0:1], scalar1=fcolA[:, c, h:h + 1], scalar2=None,
                            op0=ALU.max), [colr, smmr], [smmr])
                        P.op("dve", lambda e: e.reciprocal(out=smm[:, 2:3], in_=smm[:, 1:2]), [smmr], [smmr])
                        P.op("act", lambda e: e.activation(out=hb2[b], in_=bN[:, :], func=AF.Identity,
                                                           scale=smm[:, 2:3]), [bNr, smmr], [hb2r[b]])

                    def G1(c):
                        b = c % 2
                        hbb = hb2[b]; hbbr = hb2r[b]
                        hg = hg2[b]; hgr = hg2r[b]
                        P.op("dve", lambda e: e.bn_stats(out=sm[:, 2:8], in_=hbb), [hbbr, smr], [smr])
                        P.op("dve", lambda e: e.bn_aggr(out=sm[:, 8:10], in_=sm[:, 2:8]), [smr], [smr])
                        P.op("pool", lambda e: e.tensor_scalar(out=sm[:, 10:11], in0=sm[:, 9:10], scalar1=1.0, scalar2=HN_EPS, op0=ALU.mult, op1=ALU.add), [smr], [smr])
                        P.op("pool", lambda e: e.tensor_tensor(out=sm[:, 11:12], in0=sm[:, 10:11], in1=mhalf[:, 0:1],
                                                               op=ALU.pow), [smr, CONST], [smr])

                    def G1b(c):
                        b = c % 2
                        hbb = hb2[b]; hbbr = hb2r[b]
                        hg = hg2[b]; hgr = hg2r[b]
                        P.op("dve", lambda e: e.tensor_scalar(out=sm[:, 12:13], in0=sm[:, 8:9], scalar1=sm[:, 11:12],
                                                              scalar2=-1.0, op0=ALU.mult, op1=ALU.mult), [smr], [smr])
                        P.op("act", lambda e: e.activation(out=hbb, in_=hbb, func=AF.Identity, scale=sm[:, 11:12],
                                                           bias=sm[:, 12:13]), [hbbr, smr], [hbbr])
                        if l == 0 and h == 0 and c == 1:
                            dump("hb", hbb, hbbr)
                        P.op("dve", lambda e: e.scalar_tensor_tensor(out=hg, in0=sO[:, c, :], scalar=1.0, in1=hbb,
                                                                     op0=ALU.add, op1=ALU.mult), [hbbr, sOr], [hgr])

                    def G2(c):
                        b = c % 2
                        t0 = c * 128
                        hg = hg2[b]; hgr = hg2r[b]
                        bT, bTr = bank()
                        pv = bfv(bT, 4, 128)
                        tr_group([pv[:, dc, :] for dc in range(4)], [hg[:, dc * 128:(dc + 1) * 128] for dc in range(4)],
                                 ident[:, :], [hgr, CONST], bTr)
                        gst[b] = (pv, bTr)

                    def G2_ew(c):
                        b = c % 2
                        t0 = c * 128
                        pv, bTr = gst[b]
                        P.op("dve", lambda e: e.tensor_tensor(
                            out=hAT[:, h * 4:(h + 1) * 4, t0:t0 + 128], in0=pv,
                            in1=mg[:, l, h * 4:(h + 1) * 4].unsqueeze(2).broadcast_to([128, 4, 128]), op=ALU.mult),
                            [bTr, CONST], [hATr[c // 4]])

                    mst = {}
                    gst = {}
                    F(0)
                    F(1)
                    yield
                    for i in range(NT):
                        if i + 2 < NT:
                            F(i + 2)
                        U(i)
                        M(i)
                        if i >= 2:
                            G2(i - 2)
                        if i >= 1:
                            G1(i - 1)
                        M_ew(i)
                        if i >= 1:
                            G1b(i - 1)
                        if i >= 2:
                            G2_ew(i - 2)
                        yield
                    G1(NT - 1)
                    G1b(NT - 1)
                    G2(NT - 2)
                    G2_ew(NT - 2)
                    yield
                    G2(NT - 1)
                    G2_ew(NT - 1)
                    P.dma("sp", lambda e: e.dma_start(out=cst[l, h].rearrange("p (a b) -> p a b", b=512), in_=C),
                          [Cr], [], "cs")
                    if l == 0 and h == 0:
                        dump("hAT0", hAT[:, 0, :], hATr); dump("C0", C[:, 0, :], Cr)
                jobs.append((ch_load, ch_comp, 0.5))
            run_jobs()
            del jobs[:]

            P.barrier()
            WA.reset()
            BSET = [dict(sgq=WA.alloc([128, 2, T], BF16), kk=WA.alloc([128, 2, T], BF16),
                         aa=WA.alloc([128, 2, T], F32), V2=WA.alloc([128, NT, 256], BF16),
                         sG=WA.alloc([128, NT, 256], BF16), sgqr=Reg(), kkr=Reg(), aar=Reg(), V2r=Reg(), sGr=Reg())
                    for _ in range(2)]
            lga = WA.alloc([128, T], F32); lgar = Reg()
            S = WA.alloc([128, 2, 128], F32); Sr = Reg()
            Sbf2 = [WA.alloc([128, 2, 128], BF16) for _ in range(2)]; Sbf2r = [Reg(), Reg()]
            t1 = WA.alloc([128, 512], F32); t1r = Reg()
            t2 = WA.alloc([128, 512], F32); t2r = Reg()
            t3 = WA.alloc([128, 512], BF16); t3r = Reg()
            t4 = WA.alloc([128, 512], BF16); t4r = Reg()
            d1 = WA.alloc([128, 2, 128], F32); d1r = Reg()
            d2 = WA.alloc([128, 2, 128], F32); d2r = Reg()
            e0 = WA.alloc([128, 2, 128], BF16); e0r = Reg()
            e1 = WA.alloc([128, 2, 128], BF16); e1r = Reg()
            e1n = WA.alloc([128, 2, 128], BF16); e1nr = Reg()
            e2 = WA.alloc([128, 2, 128], BF16); e2r = Reg()
            q02 = [WA.alloc([128, 2, 128], BF16) for _ in range(3)]; q02r = [Reg() for _ in range(3)]
            qm2 = [WA.alloc([128, 2, 128], BF16) for _ in range(2)]; qm2r = [Reg(), Reg()]
            km2 = [WA.alloc([128, 2, 128], BF16) for _ in range(2)]; km2r = [Reg(), Reg()]
            Kh2 = [WA.alloc([128, 2, 128], BF16) for _ in range(2)]; Kh2r = [Reg(), Reg()]
            KhT2 = [WA.alloc([128, 2, 128], BF16) for _ in range(2)]; KhT2r = [Reg(), Reg()]
            scm2 = [WA.alloc([128, 2, 128], BF16) for _ in range(2)]; scm2r = [Reg(), Reg()]
            sq = WA.alloc([128, 256], F32); sqr = Reg()
            sqM = WA.alloc([128, 256], BF16); sqMr = Reg()
            hn2 = [WA.alloc([128, 2, 128], BF16) for _ in range(2)]; hn2r = [Reg(), Reg()]
            hgt2 = [WA.alloc([128, 256], BF16) for _ in range(2)]; hgt2r = [Reg(), Reg()]
            eae2 = [WA.alloc([128, 2], F32) for _ in range(3)]; eae2r = [Reg() for _ in range(3)]
            smB = [WA.alloc([128, 8], F32) for _ in range(2)]; smBr = [Reg(), Reg()]
            for g in range(4):
                def b1_load(si, g=g):
                    wdma(si, sview(si, 8, 256, 0, 512), wsrc(win, 0, 8, O_HQ + g * 256, 256))
                    wdma(si, sview(si, 8, 256, 256, 512), wsrc(win, 0, 8, O_HF + g * 256, 256))

                QS = BSET[g % 2]

                def b1_comp(si, g=g, QS=QS):
                    sgq = QS["sgq"]; kk = QS["kk"]; aa = QS["aa"]
                    sgqr = QS["sgqr"]; kkr = QS["kkr"]; aar = QS["aar"]
                    wv = sview(si, 8, 512)
                    for j in range(2):
                        hd = 2 * g + j
                        for tb in range(NB):
                            bk, br = bank()
                            mm_group(bk[:, :], [(wv[:, k, j * 128:(j + 1) * 128], XT[:, k, tb * 512:(tb + 1) * 512])
                                                for k in range(8)], [wreg[si]] + XTB(tb), br)
                            P.op("act", lambda e, bk=bk, hd=hd: e.activation(
                                out=t3, in_=bk[:, :], func=AF.Tanh, scale=0.5, bias=bcolh[:, l, 32 + hd:33 + hd]),
                                [br, CONST], [t3r])
                            P.op("act", lambda e, bk=bk, hd=hd: e.activation(
                                out=t4, in_=bk[:, :], func=AF.Identity, bias=bcol[:, l, 32 + hd:33 + hd]),
                                [br, CONST], [t4r])
                            P.op("dve", lambda e, j=j, tb=tb: e.scalar_tensor_tensor(
                                out=sgq[:, j, tb * 512:(tb + 1) * 512], in0=t3, scalar=1.0,
                                in1=t4, op0=ALU.add, op1=ALU.mult), [t3r, t4r], [sgqr])
                            bk, br = bank()
                            mm_group(bk[:, :], [(wv[:, k, 256 + j * 128: 256 + (j + 1) * 128],
                                                 XT[:, k, tb * 512:(tb + 1) * 512]) for k in range(8)],
                                     [wreg[si]] + XTB(tb), br)
                            P.op("act", lambda e, bk=bk, hd=hd: e.activation(
                                out=t1, in_=bk[:, :], func=AF.Tanh, scale=0.5, bias=bcolh[:, l, 40 + hd:41 + hd]),
                                [br, CONST], [t1r])
                            P.op("dve", lambda e, hd=hd: e.tensor_scalar(
                                out=t2, in0=t1, scalar1=lbc1[:, l, hd:hd + 1], scalar2=lbc0[:, l, hd:hd + 1],
                                op0=ALU.mult, op1=ALU.add), [t1r, CONST], [t2r])
                            P.op("act", lambda e, j=j, tb=tb: e.activation(
                                out=lga[:, tb * 512:(tb + 1) * 512], in_=t2, func=AF.Ln), [t2r], [lgar])
                            P.op("dve", lambda e, j=j, tb=tb: e.tensor_scalar(
                                out=kk[:, j, tb * 512:(tb + 1) * 512], in0=t2, scalar1=-1.0, scalar2=1.0,
                                op0=ALU.mult, op1=ALU.add), [t2r], [kkr])
                            yield
                        P.op("dve", lambda e, j=j: e.tensor_tensor_scan(
                            out=aa[:, j, :], data0=cmk[:, :], data1=lga[:, :], initial=0.0, op0=ALU.mult,
                            op1=ALU.add), [lgar, CONST], [aar])
                    if l == 0 and g == 0:
                        dump("sgq0", sgq[:, 0, :], sgqr); dump("kk0", kk[:, 0, :], kkr); dump("aa0", aa[:, 0, :], aar)
                jobs.append((b1_load, b1_comp))

                def b2_load(si, g=g):
                    wdma(si, sview(si, 8, 256, 0, 512), wsrc(win, 0, 8, O_HI + g * 256, 256))
                    wdma(si, sview(si, 8, 256, 256, 512), wsrc(win, 0, 8, O_HG + g * 256, 256))
                    P.dma("sp", lambda e: e.dma_start(
                        out=bbc[si][:, 0:256], in_=bin_[O_HI + g * 256: O_HI + (g + 1) * 256].partition_broadcast(128)),
                        [], [bbr[si]], "bb%d" % si)
                    P.dma("sp", lambda e: e.dma_start(
                        out=bbc[si][:, 256:512], in_=bin_[O_HG + g * 256: O_HG + (g + 1) * 256].partition_broadcast(128)),
                        [], [bbr[si]], "bb%d" % si)

                def b2_comp(si, g=g, QS=QS):
                    V2 = QS["V2"]; sG = QS["sG"]; V2r = QS["V2r"]; sGr = QS["sGr"]
                    wv = sview(si, 8, 512)
                    for tt in range(NT):
                        bk, br = bank()
                        mm_group(bk[:, :], [(XT[:, k, tt * 128:(tt + 1) * 128], wv[:, k, :]) for k in range(8)],
                                 [wreg[si], XTr[tt]], br)
                        P.op("dve", lambda e, bk=bk: e.tensor_tensor(out=t1, in0=bk[:, :], in1=bbc[si][:, :], op=ALU.add),
                             [br, bbr[si]], [t1r])
                        P.op("act", lambda e, tt=tt: e.activation(out=V2[:, tt, :], in_=t1[:, 0:256], func=AF.Copy),
                             [t1r], [V2r])
                        P.op("act", lambda e, tt=tt: e.activation(out=sG[:, tt, :], in_=t1[:, 256:512], func=AF.Tanh,
                                                                  scale=0.5), [t1r], [sGr])
                        yield
                jobs.append((b2_load, b2_comp))

                def bch_comp(si, g=g, QS=QS):
                    sgq = QS["sgq"]; kk = QS["kk"]; aa = QS["aa"]; V2 = QS["V2"]; sG = QS["sG"]
                    sgqr = QS["sgqr"]; kkr = QS["kkr"]; aar = QS["aar"]; V2r = QS["V2r"]; sGr = QS["sGr"]
                    if first:
                        P.op("pool", lambda e: e.memset(S, 0.0), [], [Sr])
                        P.op("pool", lambda e: e.memset(Sbf2[1], 0.0), [], [Sbf2r[1]])
                    else:
                        P.dma("sp", lambda e: e.dma_start(out=S, in_=sst[l, g].rearrange("p (a b) -> p a b", b=128)),
                              [], [Sr], "ss")
                        P.op("act", lambda e: e.activation(out=Sbf2[1], in_=S, func=AF.Copy), [Sr], [Sbf2r[1]])

                    def F1(c):
                        b = c % 2
                        b3 = c % 3
                        t0 = c * 128
                        ac = aa[:, :, t0:t0 + 128]
                        amid = aa[:, :, t0 + 63:t0 + 64].broadcast_to([128, 2, 128])
                        aend = aa[:, :, t0 + 127:t0 + 128].broadcast_to([128, 2, 128])
                        P.op("dve", lambda e: e.tensor_tensor(out=d1, in0=ac, in1=amid, op=ALU.subtract), [aar], [d1r])
                        P.op("dve", lambda e: e.tensor_tensor(out=d2, in0=ac, in1=aend, op=ALU.subtract), [aar], [d2r])
                        P.op("act", lambda e: e.activation(out=e0, in_=ac, func=AF.Exp), [aar], [e0r])
                        P.op("act", lambda e: e.activation(out=e1, in_=d1, func=AF.Exp), [d1r], [e1r])
                        P.op("act", lambda e: e.activation(out=e1n, in_=d1, func=AF.Exp, scale=-1.0), [d1r], [e1nr])
                        P.op("act", lambda e: e.activation(out=e2, in_=d2, func=AF.Exp, scale=-1.0), [d2r], [e2r])
                        P.op("act", lambda e: e.activation(out=eae2[b3], in_=aa[:, :, t0 + 127], func=AF.Exp),
                             [aar], [eae2r[b3]])
                        P.op("dve", lambda e: e.tensor_tensor(out=q02[b3], in0=sgq[:, :, t0:t0 + 128], in1=e0,
                                                              op=ALU.mult), [sgqr, e0r], [q02r[b3]])
                        P.op("dve", lambda e: e.tensor_tensor(out=qm2[b], in0=sgq[:, :, t0:t0 + 128], in1=e1,
                                                              op=ALU.mult), [sgqr, e1r], [qm2r[b]])
                        P.op("dve", lambda e: e.tensor_tensor(out=km2[b], in0=kk[:, :, t0:t0 + 128], in1=e1n,
                                                              op=ALU.mult), [kkr, e1nr], [km2r[b]])
                        P.op("dve", lambda e: e.tensor_tensor(out=Kh2[b], in0=kk[:, :, t0:t0 + 128], in1=e2,
                                                              op=ALU.mult), [kkr, e2r], [Kh2r[b]])

                    def F2(c):
                        b = c % 2
                        bS, bSr = bank()

                        def fn(e):
                            ins = None
                            for j in range(2):
                                ins = e.matmul(bS[:, j * 128:(j + 1) * 128], lhsT=km2[b][:, j, :], rhs=qm2[b][:, j, :],
                                               start=True, stop=True)
                            return ins
                        P.op("pe", fn, [km2r[b], qm2r[b]], [bSr])
                        P.op("dve", lambda e: e.tensor_scalar(
                            out=sq, in0=bS[:, 0:256], scalar1=1e30, scalar2=-1e30, op0=ALU.min, op1=ALU.max),
                            [bSr, sqr], [sqr])
                        P.op("dve", lambda e: e.tensor_tensor(
                            out=scm2[b], in0=sq.rearrange("p (a b) -> p a b", b=128),
                            in1=mask01[:, :].unsqueeze(1).broadcast_to([128, 2, 128]), op=ALU.mult),
                            [sqr, CONST], [scm2r[b]])
                        bK, bKr = bank()
                        pk = bfv(bK, 2, 128)
                        tr_group([pk[:, j, :] for j in range(2)], [Kh2[b][:, j, :] for j in range(2)], ident[:, :],
                                 [Kh2r[b], CONST], bKr)
                        P.op("act", lambda e: e.activation(out=KhT2[b], in_=pk, func=AF.Copy), [bKr], [KhT2r[b]])

                    def U(c):
                        b = c % 2
                        bD, bDr = bank()

                        def fn(e):
                            ins = None
                            for j in range(2):
                                ins = e.matmul(bD[:, j * 128:(j + 1) * 128], lhsT=KhT2[b][:, j, :],
                                               rhs=V2[:, c, j * 128:(j + 1) * 128], start=True, stop=True)
                            return ins
                        P.op("pe", fn, [KhT2r[b], V2r], [bDr])
                        P.op("dve", lambda e: e.tensor_tensor(
                            out=S, in0=S, in1=eae2[c % 3].unsqueeze(2).broadcast_to([128, 2, 128]), op=ALU.mult),
                            [Sr, eae2r[c % 3]], [Sr])
                        P.op("dve", lambda e: e.tensor_tensor(
                            out=S, in0=S, in1=bD[:, 0:256].rearrange("p (a b) -> p a b", b=128), op=ALU.add),
                            [Sr, bDr], [Sr])
                        P.op("act", lambda e: e.activation(out=Sbf2[b], in_=S, func=AF.Copy), [Sr], [Sbf2r[b]])

                    def M(c):
                        b = c % 2
                        pb = (c - 1) % 2
                        bO, bOr = bank()

                        def fn(e):
                            ins = None
                            for j in range(2):
                                e.matmul(bO[:, j * 128:(j + 1) * 128], lhsT=scm2[b][:, j, :],
                                         rhs=V2[:, c, j * 128:(j + 1) * 128], start=True, stop=False)
                                ins = e.matmul(bO[:, j * 128:(j + 1) * 128], lhsT=q02[c % 3][:, j, :], rhs=Sbf2[pb][:, j, :],
                                               start=False, stop=True)
                            return ins
                        P.op("pe", fn, [scm2r[b], V2r, q02r[c % 3], Sbf2r[pb]], [bOr])
                        smb = smB[b]; smbr = smBr[b]
                        P.op("act", lambda e: e.activation(out=sqM, in_=bO[:, 0:256], func=AF.Square), [bOr], [sqMr])
                        P.op("dve", lambda e: e.tensor_reduce(
                            out=smb[:, 2:4], in_=sqM.rearrange("p (a b) -> p a b", b=128), axis=AX.X, op=ALU.add),
                            [sqMr, smbr], [smbr])
                        P.op("pool", lambda e: e.tensor_scalar(out=smb[:, 4:6], in0=smb[:, 2:4], scalar1=1.0 / 128.0,
                                                               scalar2=4.0 * HN_EPS, op0=ALU.mult, op1=ALU.add),
                             [smbr], [smbr])
                        P.op("pool", lambda e: e.tensor_tensor(out=smb[:, 6:8], in0=smb[:, 4:6], in1=mhalf[:, 0:2],
                                                               op=ALU.pow), [smbr, CONST], [smbr])
                        P.op("dve", lambda e: e.tensor_tensor(
                            out=hn2[b], in0=bO[:, 0:256].rearrange("p (a b) -> p a b", b=128),
                            in1=smb[:, 6:8].unsqueeze(2).broadcast_to([128, 2, 128]), op=ALU.mult),
                            [bOr, smbr], [hn2r[b]])

                    def G1(c):
                        b = c % 2
                        P.op("dve", lambda e: e.scalar_tensor_tensor(
                            out=hgt2[b], in0=sG[:, c, :], scalar=1.0, in1=hn2[b].rearrange("p a b -> p (a b)"),
                            op0=ALU.add, op1=ALU.mult), [hn2r[b], sGr], [hgt2r[b]])

                    def G2(c):
                        b = c % 2
                        t0 = c * 128
                        hgt = hgt2[b]; hgtr = hgt2r[b]
                        bT, bTr = bank()
                        pv = bfv(bT, 2, 128)
                        tr_group([pv[:, j, :] for j in range(2)], [hgt[:, j * 128:(j + 1) * 128] for j in range(2)],
                                 ident[:, :], [hgtr, CONST], bTr)
                        P.op("dve", lambda e: e.tensor_tensor(
                            out=hBT[:, 2 * g:2 * g + 2, t0:t0 + 128], in0=pv,
                            in1=hgn[:, l, 2 * g:2 * g + 2].unsqueeze(2).broadcast_to([128, 2, 128]), op=ALU.mult),
                            [bTr, CONST], [hBTr[c // 4]])

                    F1(0)
                    F1(1)
                    F2(0)
                    yield
                    for i in range(NT):
                        if i + 2 < NT:
                            F1(i + 2)
                        if i + 1 < NT:
                            F2(i + 1)
                        U(i)
                        M(i)
                        if i >= 1:
                            G1(i - 1)
                        if i >= 2:
                            G2(i - 2)
                        yield
                    G1(NT - 1)
                    G2(NT - 2)
                    yield
                    G2(NT - 1)
                    P.dma("sp", lambda e: e.dma_start(out=sst[l, g].rearrange("p (a b) -> p a b", b=128), in_=S),
                          [Sr], [], "ss")
                    if l == 0 and g == 0:
                        dump("V20", V2[:, 0, :], V2r); dump("hBT0", hBT[:, 0, :], hBTr); dump("S0", S[:, 0, :], Sr)
                jobs.append((lambda si: None, bch_comp, 1.0))
            run_jobs()
            del jobs[:]

            P.barrier()
            WA.reset()
            tmpA = WA.alloc([128, NB, 512], F32); tmpAr = Reg()
            sga = WA.alloc([128, 512], F32); sgar = Reg()
            sgb = WA.alloc([128, 512], F32); sgbr = Reg()
            for fc in range(8):
                def c1_load(si, fc=fc):
                    wdma(si, sview(si, 16, 128), wsrc(dr["w_proj_a"][l], 0, 16, fc * 128, 128))

                def c1_comp(si, fc=fc):
                    wv = sview(si, 16, 128)
                    for tb in range(NB):
                        bk, br = bank()
                        mm_group(bk[:, :], [(wv[:, kc, :], hAT[:, kc, tb * 512:(tb + 1) * 512]) for kc in range(16)],
                                 [wreg[si], hATr[tb]], br)
                        P.op("act", lambda e, bk=bk, tb=tb: e.activation(out=tmpA[:, tb, :], in_=bk[:, :], func=AF.Copy),
                             [br], [tmpAr])
                jobs.append((c1_load, c1_comp))

                def c2_load(si, fc=fc):
                    wdma(si, sview(si, 24, 128)[:, 0:8, :], wsrc(dr["w_proj_b"][l], 0, 8, fc * 128, 128))
                    wdma(si, sview(si, 24, 128)[:, 8:16, :], wsrc(win, 0, 8, O_GA + fc * 128, 128))
                    wdma(si, sview(si, 24, 128)[:, 16:24, :], wsrc(win, 0, 8, O_GB + fc * 128, 128))

                def c2_comp(si, fc=fc):
                    wv = sview(si, 24, 128)
                    for tb in range(NB):
                        xs = lambda k: XT[:, k, tb * 512:(tb + 1) * 512]
                        bk, br = bank()
                        mm_group(bk[:, :], [(wv[:, 8 + k, :], xs(k)) for k in range(8)], [wreg[si]] + XTB(tb), br)
                        P.op("act", lambda e, bk=bk: e.activation(out=sga, in_=bk[:, :], func=AF.Tanh, scale=0.5,
                                                                  bias=bcolh[:, l, 48 + fc:49 + fc]), [br, CONST], [sgar])
                        bk, br = bank()
                        mm_group(bk[:, :], [(wv[:, 16 + k, :], xs(k)) for k in range(8)], [wreg[si]] + XTB(tb), br)
                        P.op("act", lambda e, bk=bk: e.activation(out=sgb, in_=bk[:, :], func=AF.Tanh, scale=0.5,
                                                                  bias=bcolh[:, l, 56 + fc:57 + fc]), [br, CONST], [sgbr])
                        bk, br = bank()
                        mm_group(bk[:, :], [(wv[:, k, :], hBT[:, k, tb * 512:(tb + 1) * 512]) for k in range(8)],
                                 [wreg[si], hBTr[tb]], br)
                        P.op("dve", lambda e, tb=tb: e.scalar_tensor_tensor(
                            out=sga, in0=sga, scalar=1.0, in1=tmpA[:, tb, :], op0=ALU.add, op1=ALU.mult),
                            [sgar, tmpAr], [sgar])
                        P.op("dve", lambda e, bk=bk: e.scalar_tensor_tensor(
                            out=sgb, in0=sgb, scalar=1.0, in1=bk[:, :], op0=ALU.add, op1=ALU.mult),
                            [br, sgbr], [sgbr])
                        P.op("dve", lambda e, tb=tb: e.tensor_tensor(
                            out=MIXT[:, fc, tb * 512:(tb + 1) * 512], in0=sga, in1=sgb, op=ALU.add),
                            [sgar, sgbr], [MIXr[tb]])
                jobs.append((c2_load, c2_comp))
            run_jobs()
            del jobs[:]

            if l == 0:
                dump("MIXT0", MIXT[:, 0, :], MIXr)
            P.barrier()
            WA.reset()
            YT = WA.alloc([128, 8, T], F32); YTr = [Reg() for _ in range(NT)]
            lnb = WA.alloc([128, 2, D], F32); lnbr = Reg()
            xres = [WA.alloc([128, D], F32) for _ in range(2)]; xresr = [Reg(), Reg()]
            rb2 = [WA.alloc([128, D], F32) for _ in range(2)]; rb2r = [Reg(), Reg()]
            xb22 = [WA.alloc([128, D], BF16) for _ in range(2)]; xb22r = [Reg(), Reg()]
            sm32 = [WA.alloc([128, 32], F32) for _ in range(2)]; sm32r = [Reg(), Reg()]
            esg = [WA.alloc([128, 512], F32) for _ in range(2)]; esgr = [Reg(), Reg()]

            def ln_stage(gname, bname, res_src, res_srcr, res_dst, res_dstr, make_xt):
                P.dma("sp", lambda e: e.dma_start(out=lnb[:, 0, :], in_=dr[gname][l, :].partition_broadcast(128)),
                      [], [lnbr], "lnb")
                P.dma("sp", lambda e: e.dma_start(out=lnb[:, 1, :], in_=dr[bname][l, :].partition_broadcast(128)),
                      [], [lnbr], "lnb")

                def LA(tt):
                    i = tt % 2
                    rb = rb2[i]; rbr = rb2r[i]; sm3 = sm32[i]; sm3r = sm32r[i]
                    P.dma("sp", lambda e: e.dma_start(out=xres[i], in_=res_src(tt)),
                          [res_srcr[tt]] if res_srcr else [], [xresr[i]], "xr%d" % i)
                    for hf in range(2):
                        bk, br = bank()
                        tr_group([bk[:, j * 128:(j + 1) * 128] for j in range(4)],
                                 [YT[:, hf * 4 + j, tt * 128:(tt + 1) * 128] for j in range(4)], identf[:, :],
                                 [YTr[tt], CONST], br)
                        P.op("dve", lambda e, bk=bk, hf=hf: e.scalar_tensor_tensor(
                            out=rb[:, hf * 512:(hf + 1) * 512], in0=xres[i][:, hf * 512:(hf + 1) * 512], scalar=ALPHA,
                            in1=bk[:, :], op0=ALU.mult, op1=ALU.add), [br, xresr[i], rbr], [rbr])
                    for hf in range(2):
                        P.op("dve", lambda e, hf=hf: e.bn_stats(out=sm3[:, hf * 6:(hf + 1) * 6],
                                                                in_=rb[:, hf * 512:(hf + 1) * 512]), [rbr, sm3r], [sm3r])
                    P.op("dve", lambda e: e.bn_aggr(out=sm3[:, 12:14], in_=sm3[:, 0:12]), [sm3r], [sm3r])
                    P.op("pool", lambda e: e.tensor_scalar(out=sm3[:, 14:15], in0=sm3[:, 13:14], scalar1=1.0, scalar2=LN_EPS, op0=ALU.mult, op1=ALU.add), [sm3r], [sm3r])
                    P.op("pool", lambda e: e.tensor_tensor(out=sm3[:, 15:16], in0=sm3[:, 14:15], in1=mhalf[:, 0:1],
                                                           op=ALU.pow), [sm3r, CONST], [sm3r])
                    P.op("dve", lambda e: e.tensor_scalar(out=sm3[:, 16:17], in0=sm3[:, 12:13], scalar1=sm3[:, 15:16],
                                                          scalar2=-1.0, op0=ALU.mult, op1=ALU.mult), [sm3r], [sm3r])

                def LB(tt):
                    i = tt % 2
                    rb = rb2[i]; rbr = rb2r[i]; sm3 = sm32[i]; sm3r = sm32r[i]
                    P.op("act", lambda e: e.activation(out=rb, in_=rb, func=AF.Identity, scale=sm3[:, 15:16],
                                                       bias=sm3[:, 16:17]), [rbr, sm3r], [rbr])
                    P.op("dve", lambda e: e.tensor_tensor(out=rb, in0=rb, in1=lnb[:, 0, :], op=ALU.mult),
                         [rbr, lnbr], [rbr])
                    P.op("dve", lambda e: e.tensor_tensor(out=rb, in0=rb, in1=lnb[:, 1, :], op=ALU.add),
                         [rbr, lnbr], [rbr])
                    if l == 0 and tt == 0:
                        dump(gname, rb, rbr)
                    P.dma("sp", lambda e: e.dma_start(out=res_dst(tt), in_=rb), [rbr], [res_dstr[tt]], "ro%d" % i)
                    if make_xt:
                        to_xt(rb, rbr, tt, xb22[i], xb22r[i])
                LA(0)
                for tt in range(NT):
                    if tt + 1 < NT:
                        LA(tt + 1)
                    LB(tt)

            for fc in range(8):
                def d_load(si, fc=fc):
                    wdma(si, sview(si, 8, 128), wsrc(dr["w_out"][l], 0, 8, fc * 128, 128))

                def d_comp(si, fc=fc):
                    wv = sview(si, 8, 128)
                    for tb in range(NB):
                        bk, br = bank()
                        mm_group(bk[:, :], [(wv[:, k, :], MIXT[:, k, tb * 512:(tb + 1) * 512]) for k in range(8)],
                                 [wreg[si], MIXr[tb]], br)
                        P.op("act", lambda e, bk=bk, tb=tb: e.activation(
                            out=YT[:, fc, tb * 512:(tb + 1) * 512], in_=bk[:, :], func=AF.Identity, scale=0.5),
                            [br], [YTr[4 * tb + i] for i in range(4)])
                jobs.append((d_load, d_comp))
            jobs.append((lambda si: None, lambda si: ln_stage(
                "ln1_g", "ln1_b", res_in, (R2 if l > 0 else None), lambda tt: res1[tt * 128:(tt + 1) * 128, :], R1, True)))

            for jb in range(NHC // 2):
                def e_load(si, jb=jb):
                    wdma(si, sview(si, 8, 256, 0, 512), wsrc(dr["w_ffn_gate"][l], 0, 8, jb * 256, 256))
                    wdma(si, sview(si, 8, 256, 256, 512), wsrc(dr["w_ffn_up"][l], 0, 8, jb * 256, 256))

                def e_comp(si, jb=jb):
                    wv = sview(si, 8, 512)
                    for j in range(2):
                        hc = 2 * jb + j
                        for tb in range(NB):
                            bg, bgr = bank()
                            mm_group(bg[:, :], [(wv[:, k, j * 128:(j + 1) * 128], XT[:, k, tb * 512:(tb + 1) * 512])
                                                for k in range(8)], [wreg[si]] + XTB(tb), bgr)
                            bu, bur = bank()
                            mm_group(bu[:, :], [(wv[:, k, 256 + j * 128:256 + (j + 1) * 128],
                                                 XT[:, k, tb * 512:(tb + 1) * 512]) for k in range(8)],
                                     [wreg[si]] + XTB(tb), bur)
                            sg = esg[(2 * j + tb) % 2]
                            sgr = esgr[(2 * j + tb) % 2]
                            P.op("act", lambda e, bg=bg, sg=sg: e.activation(out=sg, in_=bg[:, :], func=AF.Tanh,
                                                                             scale=0.5), [bgr], [sgr])
                            P.op("dve", lambda e, bg=bg, sg=sg: e.scalar_tensor_tensor(
                                out=sg, in0=sg, scalar=1.0, in1=bg[:, :], op0=ALU.add, op1=ALU.mult), [bgr, sgr], [sgr])
                            P.op("dve", lambda e, bu=bu, sg=sg, hc=hc, tb=tb: e.tensor_tensor(
                                out=HIDT[:, hc, tb * 512:(tb + 1) * 512], in0=bu[:, :], in1=sg, op=ALU.mult),
                                [bur, sgr], [HIDr[tb]])
                jobs.append((e_load, e_comp))
            for fc in range(8):
                def f_load(si, fc=fc):
                    wdma(si, sview(si, NHC, 128), wsrc(dr["w_ffn_down"][l], 0, NHC, fc * 128, 128))

                def f_comp(si, fc=fc):
                    wv = sview(si, NHC, 128)
                    for tb in range(NB):
                        bk, br = bank()
                        mm_group(bk[:, :], [(wv[:, hc, :], HIDT[:, hc, tb * 512:(tb + 1) * 512]) for hc in range(NHC)],
                                 [wreg[si], HIDr[tb]], br)
                        P.op("act", lambda e, bk=bk, tb=tb: e.activation(
                            out=YT[:, fc, tb * 512:(tb + 1) * 512], in_=bk[:, :], func=AF.Identity, scale=0.5),
                            [br], [YTr[4 * tb + i] for i in range(4)])
                jobs.append((f_load, f_comp))
            jobs.append((lambda si: None, lambda si: ln_stage(
                "ln2_g", "ln2_b", lambda tt: res1[tt * 128:(tt + 1) * 128, :], R1, res_out, R2, not last)))
            run_jobs()
            del jobs[:]

        setup()
        for seq in range(n_seq):
            for half in range(SEQ // T):
                do_pass(seq, half)
        P.barrier(final=True)
        P.emit()
    nc._dbg_names = dstate["names"]
    return nc


_CACHE = {}


def kernel(**inputs):
    n = 8
    x = np.ascontiguousarray(inputs["x"], dtype=np.float32)
    nseq = x.shape[0] // n
    key = (nseq, DEPTH)
    if key not in _CACHE:
        _CACHE[key] = build(nseq, DEPTH)
    nc = _CACHE[key]
    shared = {k: np.ascontiguousarray(v, dtype=np.float32) for k, v in inputs.items() if k != "x"}
    in_maps = []
    for i in range(n):
        m = dict(shared)
        m["x"] = x[i * nseq:(i + 1) * nseq]
        in_maps.append(m)
    res = run_bass_kernel_spmd(nc, in_maps, core_ids=list(range(n)))
    return np.concatenate([r["out"] for r in res.results], axis=0)
```

```python
import numpy as np
from contextlib import ExitStack
import concourse.bass as bass
import concourse.mybir as mybir
from concourse.bass_utils import run_bass_kernel_spmd

F32 = mybir.dt.float32
BF16 = mybir.dt.bfloat16
AF = mybir.ActivationFunctionType
ALU = mybir.AluOpType
AX = mybir.AxisListType

D = 1024
SEQ = 2048
DEPTH = 4
T = 1024
NT = T // 128
NB = T // 512
MW = 2048
FFN = 2816
NHC = FFN // 128
ALPHA = float((2 * DEPTH) ** 0.25)
OFF = [0, 2048, 4096, 6144, 8192, 8196, 8200, 9224, 10248, 11272, 12296, 13320, 14344]
O_MQ, O_MK, O_MV, O_MO, O_MI, O_MF, O_HQ, O_HF, O_HI, O_HG, O_GA, O_GB = OFF[:12]
LN_EPS = 1e-5
HN_EPS = 1e-6
SAME_SYNC = True
DBG_HALF = 0


class Reg:
    __slots__ = ("w", "r")

    def __init__(self):
        self.w = None
        self.r = {}


class Prog:
    ENG = {"pe": "tensor", "act": "scalar", "dve": "vector", "pool": "gpsimd", "sp": "sync"}

    def __init__(self, nc, stack):
        self.nc = nc
        self.stack = stack
        self.ops = {e: [] for e in self.ENG}
        self.sem = {}
        self.cnt = {}
        self.seen = {e: {} for e in self.ENG}
        for e in ("pe", "act", "dve", "pool"):
            self.newsem("c_" + e)

    def newsem(self, name):
        self.sem[name] = self.stack.enter_context(self.nc.semaphore(name))
        self.cnt[name] = 0

    def _deps(self, eng, reads, writes):
        need = {}
        for r in reads:
            if r.w is not None and need.get(r.w[0], 0) < r.w[1]:
                need[r.w[0]] = r.w[1]
        for w in writes:
            if w.w is not None and need.get(w.w[0], 0) < w.w[1]:
                need[w.w[0]] = w.w[1]
            for s, v in w.r.items():
                if need.get(s, 0) < v:
                    need[s] = v
        waits = []
        own = "c_" + eng
        seen = self.seen[eng]
        for s, v in need.items():
            if s == own and (eng == "pe" or not SAME_SYNC):
                continue
            if seen.get(s, 0) >= v:
                continue
            seen[s] = v
            waits.append((s, v))
        return waits

    def _commit(self, tok, reads, writes):
        s, v = tok
        for r in reads:
            if r.r.get(s, 0) < v:
                r.r[s] = v
        for w in writes:
            w.w = tok
            w.r = {}

    def op(self, eng, fn, reads=(), writes=()):
        waits = self._deps(eng, reads, writes)
        s = "c_" + eng
        self.cnt[s] += 1
        self.ops[eng].append((waits, fn, (s, 1)))
        self._commit((s, self.cnt[s]), reads, writes)

    def dma(self, eng, fn, reads, writes, sem):
        if sem not in self.sem:
            self.newsem(sem)
        waits = self._deps(eng, reads, writes)
        self.cnt[sem] += 16
        self.ops[eng].append((waits, fn, (sem, 16)))
        self._commit((sem, self.cnt[sem]), reads, writes)

    def barrier(self, final=False):
        for e in self.ENG:
            waits = []
            for s, c in self.cnt.items():
                if c == 0 or (s.startswith("w") and not final):
                    continue
                if s == "c_" + e:
                    continue
                if self.seen[e].get(s, 0) < c:
                    self.seen[e][s] = c
                    waits.append((s, c))
            if waits:
                self.ops[e].append((waits, None, None))

    def emit(self):
        nc = self.nc
        with nc.Block() as block:
            for e, attr in self.ENG.items():
                def mk(e):
                    def body(engine):
                        for waits, fn, inc in self.ops[e]:
                            for s, v in waits:
                                engine.wait_ge(self.sem[s], v)
                            if fn is not None:
                                ins = fn(engine)
                                ins.then_inc(self.sem[inc[0]], inc[1])
                    return body
                getattr(block, attr)(mk(e))


class Arena:
    def __init__(self, tensor, nbytes):
        self.t = tensor
        self.nbytes = nbytes
        self.off = 0

    def reset(self, off=0):
        self.off = off

    def alloc(self, shape, dtype):
        esz = 4 if dtype == F32 else 2
        n = 1
        for s in shape[1:]:
            n *= s
        nb = (n * esz + 31) // 32 * 32
        assert self.off + nb <= self.nbytes, (self.off, nb, self.nbytes)
        a = self.off // 2
        v = self.t[0:shape[0], a:a + nb // 2]
        self.off += nb
        if dtype == F32:
            v = v.bitcast(F32)
        v = v[:, 0:n]
        if len(shape) == 3:
            v = v.rearrange("p (a b) -> p a b", b=shape[2])
        return v


def build(n_seq, n_layers, debug=None):
    nc = bass.Bass("TRN2", target_bir_lowering=False)
    dr = {}

    def din(name, shape):
        dr[name] = nc.dram_tensor(name, list(shape), F32, kind="ExternalInput").ap()
    din("x", [n_seq, SEQ, D])
    din("w_in", [DEPTH, D, 14344]); din("b_in", [DEPTH, 14344])
    din("conv_w", [DEPTH, 4, 4096]); din("conv_b", [DEPTH, 4096])
    din("m_norm_g", [DEPTH, MW]); din("lb_logits", [DEPTH, 1024]); din("h_norm_g", [DEPTH, 1024])
    din("w_proj_a", [DEPTH, MW, D]); din("w_proj_b", [DEPTH, D, D]); din("w_out", [DEPTH, D, D])
    din("ln1_g", [DEPTH, D]); din("ln1_b", [DEPTH, D])
    din("w_ffn_gate", [DEPTH, D, FFN]); din("w_ffn_up", [DEPTH, D, FFN]); din("w_ffn_down", [DEPTH, FFN, D])
    din("ln2_g", [DEPTH, D]); din("ln2_b", [DEPTH, D])
    out = nc.dram_tensor("out", [n_seq, SEQ, D], F32, kind="ExternalOutput").ap()
    res1 = nc.dram_tensor("res1", [T, D], F32, kind="Internal").ap()
    res2 = nc.dram_tensor("res2", [T, D], F32, kind="Internal").ap()
    cst = nc.dram_tensor("cst", [n_layers, 4, 128, 2048], F32, kind="Internal").ap()
    sst = nc.dram_tensor("sst", [n_layers, 4, 128, 256], F32, kind="Internal").ap()
    dbg = None
    if debug is not None:
        dbg = nc.dram_tensor("dbg", list(debug), F32, kind="ExternalOutput").ap()

    with ExitStack() as st:
        P = Prog(nc, st)

        def sb(name, shape, dt):
            return st.enter_context(nc.sbuf_tensor(name, list(shape), dt))
        XT = sb("XT", [128, 8, T], BF16); XTr = [Reg() for _ in range(NT)]
        MIXb = sb("MIX", [128, 8 * T], BF16)
        MIXT = MIXb[:, :].rearrange("p (a b) -> p a b", b=T); MIXr = [Reg() for _ in range(NB)]
        BIGb = sb("BIG", [128, 24 * T], BF16)
        hAT = BIGb[:, 0:16 * T].rearrange("p (a b) -> p a b", b=T); hATr = [Reg() for _ in range(NB)]
        hBT = BIGb[:, 16 * T:24 * T].rearrange("p (a b) -> p a b", b=T); hBTr = [Reg() for _ in range(NB)]
        HIDT = BIGb[:, 0:NHC * T].rearrange("p (a b) -> p a b", b=T); HIDr = [Reg() for _ in range(NB)]
        WORKB = 75 * 1024 + 512
        WORKt = sb("WORK", [128, WORKB // 2], BF16)
        WA = Arena(WORKt, WORKB)
        MA = Arena(MIXb, 16 * 1024)
        NSLOT = 3
        wslot = [sb("wslot%d" % i, [128, 4096], BF16) for i in range(NSLOT)]
        wreg = [Reg() for _ in range(NSLOT)]
        bbc = [sb("bbc%d" % i, [128, 512], F32) for i in range(NSLOT)]
        bbr = [Reg() for _ in range(NSLOT)]
        GAM = sb("GAM", [128, T + 2], F32); GAMr = Reg()
        gprow = sb("gprow", [4, T + 2], F32); gprr = Reg()
        ident = sb("ident", [128, 128], BF16)
        identf = sb("identf", [128, 128], F32)
        maskbig = sb("maskbig", [128, 128], F32)
        mask01 = sb("mask01", [128, 128], F32)
        ones_row = sb("ones_row", [4, 8], F32)
        ones_bf = sb("ones_bf", [128, 2], BF16)
        cmk = sb("cmk", [128, T], BF16)
        sel = sb("sel", [4, 4, 128], F32)
        i4 = sb("i4", [4, 4], F32)
        bcol = sb("bcol", [128, DEPTH, 64], F32)
        cw = sb("cw", [128, DEPTH, 32, 4], F32)
        cb = sb("cb", [128, DEPTH, 32], F32)
        mg = sb("mg", [128, DEPTH, 16], F32)
        hgn = sb("hgn", [128, DEPTH, 8], F32)
        lbl = sb("lbl", [128, 8, DEPTH], F32)
        lbp = sb("lbp", [128, 8, DEPTH], F32)
        lb = sb("lb", [128, DEPTH, 8], F32)
        oml = sb("oml", [128, DEPTH, 8], F32)
        gbias = sb("gbias", [4, DEPTH, 2], F32)
        car = sb("car", [4, DEPTH, 4], F32); carr = Reg()
        ccar = sb("ccar", [128, DEPTH, 32, 4], BF16); ccr = Reg()
        nst = sb("nst", [128, DEPTH, 4, 4], F32); nsr = Reg()
        acolA = sb("acolA", [128, NT, 4], F32)
        fcolA = sb("fcolA", [128, NT, 4], F32); colr = Reg()
        small = sb("small", [128, 64], F32)
        cbh = sb("cbh", [128, DEPTH, 32], F32)
        bcolh = sb("bcolh", [128, DEPTH, 64], F32)
        lbc0 = sb("lbc0", [128, DEPTH, 8], F32)
        lbc1 = sb("lbc1", [128, DEPTH, 8], F32)
        mhalf = sb("mhalf", [128, 8], F32)
        CONST = Reg()
        banks = [st.enter_context(nc.psum_tensor("bank%d" % i, [128, 512], F32)) for i in range(8)]
        bankr = [Reg() for _ in range(8)]
        bstate = [0]

        dstate = {"on": False, "names": []}

        def dump(name, ap, reg, p=128):
            if dbg is None or not dstate["on"] or name in dstate["names"] or len(dstate["names"]) >= dbg.shape[0]:
                return
            i = len(dstate["names"])
            dstate["names"].append(name)
            n = ap.shape[-1] if len(ap.shape) == 2 else None
            regs = reg if isinstance(reg, list) else [reg]
            P.dma("pool", lambda e: e.dma_start(out=dbg[i, 0:p, 0:n], in_=ap), regs, [], "dbg")

        def bank():
            i = bstate[0]
            bstate[0] = (i + 1) % 8
            return banks[i], bankr[i]

        def bfv(bk, a, b):
            return bk[:, :].bitcast(BF16)[:, 0:a * b].rearrange("p (a b) -> p a b", b=b)

        def setup():
            def c1(e):
                e.memset(ident[:, :], 0.0)
                e.memset(identf[:, :], 0.0)
                e.memset(maskbig[:, :], 0.0)
                e.memset(mask01[:, :], 1.0)
                e.memset(ones_row[:, :], 1.0)
                e.memset(ones_bf[:, :], 1.0)
                e.memset(cmk[:, :], 1.0)
                e.memset(sel[:, :, :], 0.0)
                e.memset(i4[:, :], 0.0)
                e.memset(car[:, :, :], 0.0)
                e.memset(ccar[:, :, :, :], 0.0)
                e.memset(nst[:, :, :, :], 0.0)
                e.memset(gprow[:, :], 0.0)
                e.memset(mhalf[:, :], -0.5)
                return e.memset(lb[:, :, :], 0.0)
            P.op("pool", c1, [], [CONST])

            def c2(e):
                e.affine_select(out=ident[:, :], in_=ident[:, :], pattern=[[-1, 128]], compare_op=ALU.not_equal,
                                fill=1.0, base=0, channel_multiplier=1)
                e.affine_select(out=identf[:, :], in_=identf[:, :], pattern=[[-1, 128]], compare_op=ALU.not_equal,
                                fill=1.0, base=0, channel_multiplier=1)
                e.affine_select(out=maskbig[:, :], in_=maskbig[:, :], pattern=[[1, 128]], compare_op=ALU.is_ge,
                                fill=30000.0, base=0, channel_multiplier=-1)
                e.affine_select(out=mask01[:, :], in_=mask01[:, :], pattern=[[1, 128]], compare_op=ALU.is_ge,
                                fill=0.0, base=0, channel_multiplier=-1)
                e.affine_select(out=sel[:, :, :], in_=sel[:, :, :], pattern=[[1, 4], [0, 128]],
                                compare_op=ALU.not_equal, fill=1.0, base=0, channel_multiplier=-1)
                e.affine_select(out=i4[:, :], in_=i4[:, :], pattern=[[1, 4]], compare_op=ALU.not_equal,
                                fill=1.0, base=0, channel_multiplier=-1)
                return e.memset(cmk[:, :].rearrange("p (c l) -> p c l", l=128)[:, :, 0:1], 0.0)
            P.op("pool", c2, [CONST], [CONST])

            segs = [(O_MQ, 16, 0), (O_MK, 16, 16), (O_HQ, 8, 32), (O_HF, 8, 40), (O_GA, 8, 48), (O_GB, 8, 56)]
            for l in range(n_layers):
                for (o, n, c0) in segs:
                    P.dma("sp", lambda e, l=l, o=o, n=n, c0=c0: e.dma_start(
                        out=bcol[:, l, c0:c0 + n], in_=dr["b_in"][l, o:o + n * 128].rearrange("(c p) -> p c", p=128),
                        allow_slow_non_contiguous=True), [], [CONST], "cst")
                for j in range(4):
                    for c8 in range(4):
                        P.dma("sp", lambda e, l=l, j=j, c8=c8: e.dma_start(
                            out=cw[:, l, c8 * 8:(c8 + 1) * 8, j],
                            in_=dr["conv_w"][l, j, c8 * 1024:(c8 + 1) * 1024].rearrange("(c p) -> p c", p=128),
                            allow_slow_non_contiguous=True), [], [CONST], "cst")
                for c8 in range(2):
                    P.dma("sp", lambda e, l=l, c8=c8: e.dma_start(
                        out=cb[:, l, c8 * 16:(c8 + 1) * 16],
                        in_=dr["conv_b"][l, c8 * 2048:(c8 + 1) * 2048].rearrange("(c p) -> p c", p=128),
                        allow_slow_non_contiguous=True), [], [CONST], "cst")
                P.dma("sp", lambda e, l=l: e.dma_start(
                    out=mg[:, l, :], in_=dr["m_norm_g"][l, :].rearrange("(c p) -> p c", p=128),
                    allow_slow_non_contiguous=True), [], [CONST], "cst")
                P.dma("sp", lambda e, l=l: e.dma_start(
                    out=hgn[:, l, :], in_=dr["h_norm_g"][l, :].rearrange("(c p) -> p c", p=128),
                    allow_slow_non_contiguous=True), [], [CONST], "cst")
                P.dma("sp", lambda e, l=l: e.dma_start(
                    out=gbias[:, l, :], in_=dr["b_in"][l, O_MI:O_MI + 8].rearrange("(g h) -> h g", h=4),
                    allow_slow_non_contiguous=True), [], [CONST], "cst")
            for l in range(DEPTH):
                P.dma("sp", lambda e, l=l: e.dma_start(
                    out=lbl[:, :, l], in_=dr["lb_logits"][l, :].rearrange("(c p) -> p c", p=128),
                    allow_slow_non_contiguous=True), [], [CONST], "cst")
            P.op("act", lambda e: e.activation(out=lbp[:, :, :], in_=lbl[:, :, :], func=AF.Exp), [CONST], [CONST])
            P.op("dve", lambda e: e.tensor_reduce(out=small[:, 0:8], in_=lbp[:, :, :], axis=AX.X, op=ALU.add),
                 [CONST], [CONST])
            P.op("dve", lambda e: e.reciprocal(out=small[:, 8:16], in_=small[:, 0:8]), [CONST], [CONST])
            P.op("dve", lambda e: e.tensor_tensor(out=lbp[:, :, :], in0=lbp[:, :, :],
                                                  in1=small[:, 8:16].unsqueeze(2).broadcast_to([128, 8, DEPTH]),
                                                  op=ALU.mult), [CONST], [CONST])
            for l in range(1, DEPTH):
                P.op("dve", lambda e, l=l: e.tensor_tensor(out=lb[:, l, :], in0=lb[:, l - 1, :], in1=lbp[:, :, l],
                                                           op=ALU.add), [CONST], [CONST])
            P.op("dve", lambda e: e.tensor_scalar(out=oml[:, :, :], in0=lb[:, :, :], scalar1=-1.0, scalar2=1.0,
                                                  op0=ALU.mult, op1=ALU.add), [CONST], [CONST])
            P.op("dve", lambda e: e.tensor_scalar(out=lbc1[:, :, :], in0=oml[:, :, :], scalar1=0.5, scalar2=None,
                                                  op0=ALU.mult), [CONST], [CONST])
            P.op("dve", lambda e: e.tensor_tensor(out=lbc0[:, :, :], in0=lb[:, :, :], in1=lbc1[:, :, :], op=ALU.add),
                 [CONST], [CONST])
            P.op("dve", lambda e: e.tensor_scalar(out=cbh[:, :, :], in0=cb[:, :, :], scalar1=0.5, scalar2=None,
                                                  op0=ALU.mult), [CONST], [CONST])
            P.op("dve", lambda e: e.tensor_scalar(out=bcolh[:, :, :], in0=bcol[:, :, :], scalar1=0.5, scalar2=None,
                                                  op0=ALU.mult), [CONST], [CONST])
            P.op("dve", lambda e: e.tensor_scalar(out=mg[:, :, :], in0=mg[:, :, :], scalar1=0.5, scalar2=None,
                                                  op0=ALU.mult), [CONST], [CONST])
            P.op("dve", lambda e: e.tensor_scalar(out=hgn[:, :, :], in0=hgn[:, :, :], scalar1=0.5, scalar2=None,
                                                  op0=ALU.mult), [CONST], [CONST])

        jobs = []

        def wdma(slot_i, view, src):
            P.dma("pool", lambda e: e.dma_start(out=view, in_=src), [], [wreg[slot_i]], "w%d" % slot_i)

        def wsrc(w2d, k0, nk, c0, ncol):
            return w2d[k0 * 128:(k0 + nk) * 128, c0:c0 + ncol].rearrange("(k p) n -> p k n", p=128)

        def sview(slot_i, nk, ncol, col0=0, tot=None):
            tot = tot or ncol
            return wslot[slot_i][:, 0:nk * tot].rearrange("p (k c) -> p k c", c=tot)[:, :, col0:col0 + ncol]

        def run_jobs():
            issued = 0
            bg = [None, 0.0, 0.0]

            def step_bg():
                if bg[0] is None:
                    return
                try:
                    next(bg[0])
                except StopIteration:
                    bg[0] = None

            def drain_bg():
                while bg[0] is not None:
                    step_bg()
            for idx in range(len(jobs)):
                while issued < len(jobs) and issued <= idx + 2:
                    jobs[issued][0](issued % NSLOT)
                    issued += 1
                job = jobs[idx]
                g = job[1](idx % NSLOT)
                if len(job) > 2:
                    drain_bg()
                    bg[0] = g
                    bg[1] = job[2]
                    bg[2] = 0.0
                    step_bg()
                    continue
                if g is None:
                    continue
                for _ in g:
                    bg[2] += bg[1]
                    while bg[2] >= 1.0:
                        bg[2] -= 1.0
                        step_bg()
            drain_bg()

        def mm_group(out_ap, pairs, rd, wr):
            n = len(pairs)

            def fn(e):
                ins = None
                for i, (a, b) in enumerate(pairs):
                    ins = e.matmul(out_ap, lhsT=a, rhs=b, start=(i == 0), stop=(i == n - 1))
                return ins
            P.op("pe", fn, rd, [wr])

        def tr_group(dst_views, src_views, idn, rd, wr):
            def fn(e):
                ins = None
                for d_, s_ in zip(dst_views, src_views):
                    ins = e.transpose(out=d_, in_=s_, identity=idn)
                return ins
            P.op("pe", fn, rd, [wr])

        def to_xt(tile_f32, treg, tt, xb, xbr):
            P.op("act", lambda e: e.activation(out=xb, in_=tile_f32, func=AF.Copy), [treg], [xbr])
            bk, br = bank()
            pv = bfv(bk, 8, 128)
            tr_group([pv[:, k, :] for k in range(8)], [xb[:, k * 128:(k + 1) * 128] for k in range(8)],
                     ident[:, :], [xbr, CONST], br)
            P.op("dve", lambda e: e.tensor_copy(out=XT[:, :, tt * 128:(tt + 1) * 128], in_=pv), [br], [XTr[tt]])

        XTB = lambda tb: [XTr[4 * tb + i] for i in range(4)]

        def do_pass(seq, half):
            tok0 = half * T
            first = (half == 0)
            dstate["on"] = (seq == 0 and half == DBG_HALF)
            P.barrier()
            WA.reset()
            xin = [WA.alloc([128, D], F32) for _ in range(2)]
            xinr = [Reg(), Reg()]
            xb = [WA.alloc([128, D], BF16) for _ in range(2)]
            xbr = [Reg(), Reg()]
            for tt in range(NT):
                i = tt % 2
                P.dma("sp", lambda e, tt=tt, i=i: e.dma_start(
                    out=xin[i], in_=dr["x"][seq, tok0 + tt * 128: tok0 + (tt + 1) * 128, :]),
                    [], [xinr[i]], "xi%d" % i)
                to_xt(xin[i], xinr[i], tt, xb[i], xbr[i])
            dump("XT0", XT[:, 0, :], list(XTr))
            for l in range(n_layers):
                last = (l == n_layers - 1)
                res_in = (lambda tt: dr["x"][seq, tok0 + tt * 128: tok0 + (tt + 1) * 128, :]) if l == 0 else \
                    (lambda tt: res2[tt * 128:(tt + 1) * 128, :])
                res_out = (lambda tt: out[seq, tok0 + tt * 128: tok0 + (tt + 1) * 128, :]) if last else \
                    (lambda tt: res2[tt * 128:(tt + 1) * 128, :])
                layer(l, first, res_in, res_out, last)

        R1 = [Reg() for _ in range(NT)]
        R2 = [Reg() for _ in range(NT)]

        def layer(l, first, res_in, res_out, last):
            win = dr["w_in"][l]
            bin_ = dr["b_in"][l]
            del jobs[:]
            P.barrier()
            WA.reset(); MA.reset()
            rowr = Reg()
            hbv = BIGb[:, 16 * T:24 * T]
            ASET = [
                dict(qT=WA.alloc([128, 4, T], BF16), kT=WA.alloc([128, 4, T], BF16),
                     V=WA.alloc([128, NT, 512], BF16), sO=WA.alloc([128, NT, 512], BF16),
                     qTr=Reg(), kTr=Reg(), Vr=Reg(), sOr=Reg(), xw=[]),
                dict(qT=MIXb[:, 0:4 * T].rearrange("p (a b) -> p a b", b=T),
                     kT=MIXb[:, 4 * T:8 * T].rearrange("p (a b) -> p a b", b=T),
                     V=hbv[:, 0:NT * 512].rearrange("p (a b) -> p a b", b=512),
                     sO=hbv[:, NT * 512:2 * NT * 512].rearrange("p (a b) -> p a b", b=512),
                     qTr=Reg(), kTr=Reg(), Vr=Reg(), sOr=Reg(), xw=[rowr]),
            ]
            ub = [WA.alloc([128, T + 4], BF16) for _ in range(2)]; ubr = [Reg(), Reg()]
            C = WA.alloc([128, 4, 512], F32); Cr = Reg()
            Cbf2 = [WA.alloc([128, 4, 512], BF16) for _ in range(2)]; Cbf2r = [Reg(), Reg()]
            nbf2 = [WA.alloc([128, 4], BF16) for _ in range(2)]; nbf2r = [Reg(), Reg()]
            dg = WA.alloc([128, 4, 128], BF16); dgr = Reg()
            tE = WA.alloc([128, 128], F32); tEr = Reg()
            tD = WA.alloc([128, 128], F32); tDr = Reg()
            tS2 = [WA.alloc([128, 128], F32) for _ in range(3)]; tS2r = [Reg() for _ in range(3)]
            tW2 = [WA.alloc([128, 128], BF16) for _ in range(3)]; tW2r = [Reg() for _ in range(3)]
            qTp2 = [WA.alloc([128, 4, 128], BF16) for _ in range(3)]; qTp2r = [Reg() for _ in range(3)]
            hb2 = [WA.alloc([128, 512], F32) for _ in range(2)]; hb2r = [Reg(), Reg()]
            hg2 = [WA.alloc([128, 512], BF16) for _ in range(2)]; hg2r = [Reg(), Reg()]
            Kp2 = [WA.alloc([128, 512], BF16) for _ in range(3)]; Kp2r = [Reg() for _ in range(3)]
            smM = [WA.alloc([128, 8], F32) for _ in range(2)]; smMr = [Reg(), Reg()]
            tmpo = WA.alloc([128, 512], F32); tmpor = Reg()
            tht = [WA.alloc([128, 512], BF16) for _ in range(2)]; thtr = [Reg(), Reg()]
            thx = [WA.alloc([128, 512], BF16) for _ in range(2)]; thxr = [Reg(), Reg()]
            sm = WA.alloc([128, 16], F32); smr = Reg()
            r_i = MA.alloc([4, T], F32); r_f = MA.alloc([4, T], F32)
            r_B = MA.alloc([4, T], F32); r_m = MA.alloc([4, T], F32)

            def g_load(si):
                wdma(si, sview(si, 8, 8), wsrc(win, 0, 8, O_MI, 8))

            def g_comp(si):
                wv = sview(si, 8, 8)
                for g, dst in ((0, r_i), (1, r_f)):
                    for tb in range(NB):
                        bk, br = bank()
                        mm_group(bk[0:4, :], [(wv[:, k, g * 4:(g + 1) * 4], XT[:, k, tb * 512:(tb + 1) * 512])
                                              for k in range(8)], [wreg[si]] + XTB(tb), br)
                        P.op("act", lambda e, bk=bk, dst=dst, tb=tb, g=g: e.activation(
                            out=dst[:, tb * 512:(tb + 1) * 512], in_=bk[0:4, :], func=AF.Identity,
                            bias=gbias[:, l, g:g + 1]), [br, CONST], [rowr])
                P.op("act", lambda e: e.activation(out=r_f, in_=r_f, func=AF.Exp, scale=-1.0), [rowr], [rowr])
                P.op("act", lambda e: e.activation(out=r_f, in_=r_f, func=AF.Ln, bias=1.0), [rowr], [rowr])
                P.op("dve", lambda e: e.tensor_scalar(out=r_f, in0=r_f, scalar1=-1.0, scalar2=None, op0=ALU.mult),
                     [rowr], [rowr])
                if first:
                    P.op("pool", lambda e: e.memset(car[:, l, :], 0.0), [carr], [carr])
                P.op("dve", lambda e: e.tensor_tensor_scan(
                    out=r_B, data0=ones_row[:, 0:1].broadcast_to([4, T]), data1=r_f, initial=car[:, l, 0:1],
                    op0=ALU.mult, op1=ALU.add), [rowr, carr, CONST], [rowr])
                P.op("dve", lambda e: e.tensor_tensor_scan(
                    out=r_m, data0=r_f, data1=r_i, initial=car[:, l, 1:2], op0=ALU.add, op1=ALU.max),
                    [rowr, carr], [rowr])
                P.op("dve", lambda e: e.tensor_copy(out=gprow[:, 1:2], in_=car[:, l, 2:3]), [carr, gprr], [gprr])
                P.op("dve", lambda e: e.tensor_tensor(out=gprow[:, 2:T + 2], in0=r_m, in1=r_B, op=ALU.subtract),
                     [rowr, gprr], [gprr])
                P.op("dve", lambda e: e.tensor_tensor(out=r_i, in0=r_i, in1=r_B, op=ALU.subtract), [rowr], [rowr])
                P.op("dve", lambda e: e.tensor_copy(out=car[:, l, 0:1], in_=r_B[:, T - 1:T]), [rowr, carr], [carr])
                P.op("dve", lambda e: e.tensor_copy(out=car[:, l, 1:2], in_=r_m[:, T - 1:T]), [rowr, carr], [carr])
                P.op("dve", lambda e: e.tensor_copy(out=car[:, l, 2:3], in_=gprow[:, T + 1:T + 2]),
                     [gprr, carr], [carr])
                for src, dstc, isf in ((r_i, acolA, False), (r_m, fcolA, True)):
                    bk, br = bank()

                    def fn(e, src=src, bk=bk):
                        ins = None
                        for c in range(NT):
                            ins = e.matmul(bk[:, c * 4:(c + 1) * 4], lhsT=src[:, c * 128:(c + 1) * 128],
                                           rhs=i4[:, :], start=True, stop=True)
                        return ins
                    P.op("pe", fn, [rowr, CONST], [br])
                    if isf:
                        P.op("act", lambda e, bk=bk, dstc=dstc: e.activation(
                            out=dstc[:, :, :], in_=bk[:, 0:NT * 4].rearrange("p (c h) -> p c h", h=4),
                            func=AF.Exp, scale=-1.0, bias=float(np.log(4.0 * np.sqrt(512.0)))), [br], [colr])
                    else:
                        P.op("act", lambda e, bk=bk, dstc=dstc: e.activation(
                            out=dstc[:, :, :], in_=bk[:, 0:NT * 4].rearrange("p (c h) -> p c h", h=4),
                            func=AF.Identity), [br], [colr])
            jobs.append((g_load, g_comp))

            for h in range(4):
                BS = ASET[h % 2]
                for which, (o_seg, dstT, dstr, bc0) in enumerate(((O_MQ, BS["qT"], BS["qTr"], 0),
                                                                  (O_MK, BS["kT"], BS["kTr"], 16))):
                    def qk_load(si, o_seg=o_seg, h=h):
                        wdma(si, sview(si, 8, 512), wsrc(win, 0, 8, o_seg + h * 512, 512))

                    def qk_comp(si, o_seg=o_seg, h=h, dstT=dstT, dstr=dstr, bc0=bc0, which=which, xw=BS["xw"]):
                        wv = sview(si, 8, 512)
                        for dc in range(4):
                            ch = which * 16 + h * 4 + dc
                            u = ub[dc % 2]; ur = ubr[dc % 2]
                            if first:
                                P.op("pool", lambda e, u=u: e.memset(u[:, 0:4], 0.0), [], [ur])
                            else:
                                P.op("act", lambda e, u=u, ch=ch: e.activation(out=u[:, 0:4], in_=ccar[:, l, ch, :],
                                                                               func=AF.Copy), [ccr], [ur])
                            for tb in range(NB):
                                bk, br = bank()
                                mm_group(bk[:, :], [(wv[:, k, dc * 128:(dc + 1) * 128],
                                                     XT[:, k, tb * 512:(tb + 1) * 512]) for k in range(8)],
                                         [wreg[si]] + XTB(tb), br)
                                P.op("act", lambda e, bk=bk, u=u, tb=tb, dc=dc: e.activation(
                                    out=u[:, 4 + tb * 512: 4 + (tb + 1) * 512], in_=bk[:, :], func=AF.Identity,
                                    bias=bcol[:, l, bc0 + h * 4 + dc: bc0 + h * 4 + dc + 1]), [br, CONST], [ur])
                            P.op("act", lambda e, u=u, ch=ch: e.activation(out=ccar[:, l, ch, :], in_=u[:, T:T + 4],
                                                                           func=AF.Copy), [ur], [ccr])
                            for j in range(4):
                                P.op("dve", lambda e, j=j, ch=ch: e.tensor_scalar(
                                    out=dg[:, j, :], in0=ident[:, :], scalar1=cw[:, l, ch, j:j + 1], scalar2=None,
                                    op0=ALU.mult), [CONST], [dgr])
                            for tb in range(NB):
                                bk, br = bank()
                                mm_group(bk[:, :], [(dg[:, j, :], u[:, 1 + j + tb * 512: 1 + j + (tb + 1) * 512])
                                                    for j in range(4)], [dgr, ur], br)
                                th = tht[tb]; thr = thtr[tb]
                                P.op("act", lambda e, bk=bk, ch=ch, th=th: e.activation(
                                    out=th, in_=bk[:, :], func=AF.Tanh, scale=0.5, bias=cbh[:, l, ch:ch + 1]),
                                    [br, CONST], [thr])
                                xp = thx[tb]; xpr = thxr[tb]
                                P.op("act", lambda e, bk=bk, ch=ch, xp=xp: e.activation(
                                    out=xp, in_=bk[:, :], func=AF.Identity, bias=cb[:, l, ch:ch + 1]),
                                    [br, CONST], [xpr])
                                P.op("dve", lambda e, tb=tb, dc=dc, th=th, xp=xp: e.scalar_tensor_tensor(
                                    out=dstT[:, dc, tb * 512:(tb + 1) * 512], in0=th, scalar=1.0,
                                    in1=xp, op0=ALU.add, op1=ALU.mult), [thr, xpr], [dstr] + xw)
                            yield
                    jobs.append((qk_load, qk_comp))
                for which, o_seg in enumerate((O_MV, O_MO)):
                    def vo_load(si, o_seg=o_seg, h=h, which=which):
                        wdma(si, sview(si, 8, 512), wsrc(win, 0, 8, o_seg + h * 512, 512))
                        P.dma("sp", lambda e: e.dma_start(
                            out=bbc[si][:, :], in_=bin_[o_seg + h * 512: o_seg + (h + 1) * 512].partition_broadcast(128)),
                            [], [bbr[si]], "bb%d" % si)

                    def vo_comp(si, which=which, V=BS["V"], Vr=BS["Vr"], sO=BS["sO"], sOr=BS["sOr"]):
                        wv = sview(si, 8, 512)
                        for tt in range(NT):
                            bk, br = bank()
                            mm_group(bk[:, :], [(XT[:, k, tt * 128:(tt + 1) * 128], wv[:, k, :]) for k in range(8)],
                                     [wreg[si], XTr[tt]], br)
                            if which == 0:
                                P.op("dve", lambda e, bk=bk, tt=tt: e.tensor_tensor(
                                    out=V[:, tt, :], in0=bk[:, :], in1=bbc[si][:, :], op=ALU.add), [br, bbr[si]], [Vr])
                            else:
                                P.op("dve", lambda e, bk=bk: e.tensor_tensor(
                                    out=tmpo, in0=bk[:, :], in1=bbc[si][:, :], op=ALU.add), [br, bbr[si]], [tmpor])
                                P.op("act", lambda e, tt=tt: e.activation(out=sO[:, tt, :], in_=tmpo, func=AF.Tanh,
                                                                          scale=0.5), [tmpor], [sOr])
                            yield
                    jobs.append((vo_load, vo_comp))
                def ch_load(si):
                    pass

                def ch_comp(si, h=h, BS=BS):
                    qT = BS["qT"]; kT = BS["kT"]; V = BS["V"]; sO = BS["sO"]
                    qTr = BS["qTr"]; kTr = BS["kTr"]; Vr = BS["Vr"]; sOr = BS["sOr"]
                    if first:
                        P.op("pool", lambda e: e.memset(C, 0.0), [], [Cr])
                        P.op("pool", lambda e: e.memset(Cbf2[1], 0.0), [], [Cbf2r[1]])
                        P.op("pool", lambda e: e.memset(nst[:, l, h, :], 0.0), [], [nsr])
                    else:
                        P.dma("sp", lambda e: e.dma_start(
                            out=C, in_=cst[l, h].rearrange("p (a b) -> p a b", b=512)), [], [Cr], "cs")
                        P.op("act", lambda e: e.activation(out=Cbf2[1], in_=C, func=AF.Copy), [Cr], [Cbf2r[1]])
                    P.op("act", lambda e: e.activation(out=nbf2[1], in_=nst[:, l, h, :], func=AF.Copy), [nsr], [nbf2r[1]])
                    for (a, b) in ((0, 512), (512, 1024), (1024, 1026)):
                        bk, br = bank()
                        mm_group(bk[:, 0:b - a], [(sel[:, h, :], gprow[:, a:b])], [gprr, CONST], br)
                        P.op("act", lambda e, bk=bk, a=a, b=b: e.activation(
                            out=GAM[:, a:b], in_=bk[:, 0:b - a], func=AF.Identity), [br], [GAMr])
                    if l == 0 and h == 0:
                        dump("qT0", qT[:, 0, :], qTr); dump("kT0", kT[:, 0, :], kTr)
                        dump("V0", V[:, 0, :], Vr); dump("sO0", sO[:, 0, :], sOr)
                        dump("GAM", GAM[:, 0:1024], GAMr); dump("acol", acolA[:, :, :].rearrange("p a b -> p (a b)"), colr)
                        dump("fcol", fcolA[:, :, :].rearrange("p a b -> p (a b)"), colr)
                        dump("gprow", gprow[:, 0:1024], gprr, p=4)

                    def F(c):
                        b = c % 3
                        t0 = c * 128
                        gs = GAM[:, 2 + t0: 2 + t0 + 128]
                        bS, bSr = bank()
                        mm_group(bS[:, 0:128], [(kT[:, dc, t0:t0 + 128], qT[:, dc, t0:t0 + 128]) for dc in range(4)],
                                 [kTr, qTr], bSr)
                        bK, bKr = bank()
                        pk = bfv(bK, 4, 128)
                        tr_group([pk[:, dc, :] for dc in range(4)], [kT[:, dc, t0:t0 + 128] for dc in range(4)],
                                 ident[:, :], [kTr, CONST], bKr)
                        P.op("act", lambda e: e.activation(
                            out=tS2[b], in_=gs, func=AF.Exp, scale=-1.0, bias=GAM[:, 1 + t0: 2 + t0]), [GAMr], [tS2r[b]])
                        P.op("dve", lambda e: e.scalar_tensor_tensor(
                            out=tE, in0=gs, scalar=acolA[:, c, h:h + 1], in1=maskbig[:, :], op0=ALU.subtract,
                            op1=ALU.max), [GAMr, colr, CONST], [tEr])
                        P.op("act", lambda e: e.activation(out=tD, in_=tE, func=AF.Exp, scale=-1.0), [tEr], [tDr])
                        P.op("dve", lambda e: e.tensor_tensor(
                            out=qTp2[b], in0=qT[:, :, t0:t0 + 128], in1=tS2[b].unsqueeze(1).broadcast_to([128, 4, 128]),
                            op=ALU.mult), [qTr, tS2r[b]], [qTp2r[b]])
                        P.op("dve", lambda e: e.tensor_tensor(out=tW2[b], in0=bS[:, 0:128], in1=tD, op=ALU.mult),
                             [bSr, tDr], [tW2r[b]])
                        P.op("act", lambda e: e.activation(
                            out=Kp2[b], in_=bK[:, :].bitcast(BF16)[:, 0:512], func=AF.Identity, scale=tD[:, 127:128]),
                            [bKr, tDr], [Kp2r[b]])

                    def U(c):
                        b = c % 2
                        kb = c % 3
                        for dc in range(4):
                            bC, bCr = bank()
                            mm_group(bC[:, :], [(Kp2[kb][:, dc * 128:(dc + 1) * 128], V[:, c, :])], [Kp2r[kb], Vr], bCr)
                            P.op("dve", lambda e, bC=bC, dc=dc: e.scalar_tensor_tensor(
                                out=C[:, dc, :], in0=C[:, dc, :], scalar=tS2[kb][:, 127:128], in1=bC[:, :], op0=ALU.mult,
                                op1=ALU.add), [bCr, tS2r[kb], Cr], [Cr])
                        P.op("act", lambda e: e.activation(out=Cbf2[b], in_=C, func=AF.Copy), [Cr], [Cbf2r[b]])
                        bn_, bnr = bank()

                        def fn(e):
                            ins = None
                            for dc in range(4):
                                ins = e.matmul(bn_[:, 2 * dc:2 * dc + 1], lhsT=Kp2[kb][:, dc * 128:(dc + 1) * 128],
                                               rhs=ones_bf[:, 0:1], start=True, stop=True)
                            return ins
                        P.op("pe", fn, [Kp2r[kb], CONST], [bnr])
                        P.op("dve", lambda e: e.scalar_tensor_tensor(
                            out=nst[:, l, h, :], in0=nst[:, l, h, :], scalar=tS2[kb][:, 127:128],
                            in1=bn_[:, 0:8].rearrange("p (a b) -> p a b", b=2)[:, :, 0], op0=ALU.mult, op1=ALU.add),
                            [bnr, tS2r[kb], nsr], [nsr])
                        P.op("act", lambda e: e.activation(out=nbf2[b], in_=nst[:, l, h, :], func=AF.Copy),
                             [nsr], [nbf2r[b]])

                    def M(c):
                        b = c % 2
                        pb = (c - 1) % 2
                        kb = c % 3
                        bN, bNr = bank()
                        mm_group(bN[:, :], [(tW2[kb], V[:, c, :])] + [(qTp2[kb][:, dc, :], Cbf2[pb][:, dc, :])
                                                                     for dc in range(4)],
                                 [tW2r[kb], Vr, qTp2r[kb], Cbf2r[pb]], bNr)
                        bD, bDr = bank()
                        mm_group(bD[:, 0:1], [(tW2[kb], ones_bf[:, 0:1])] + [(qTp2[kb][:, dc, :], nbf2[pb][:, dc:dc + 1])
                                                                           for dc in range(4)],
                                 [tW2r[kb], CONST, qTp2r[kb], nbf2r[pb]], bDr)
                        mst[b] = (bN, bNr, bD, bDr)

                    def M_ew(c):
                        b = c % 2
                        bN, bNr, bD, bDr = mst[b]
                        smm = smM[b]; smmr = smMr[b]
                        P.op("act", lambda e: e.activation(out=smm[:, 0:1], in_=bD[:, 0:1], func=AF.Abs),
                             [bDr, smmr], [smmr])
                        P.op("dve", lambda e: e.tensor_scalar(
                            out=smm[:, 1:2], in0=smm[:, 0:1], scalar1=fcolA[:, c, h:h + 1], scalar2=None,
                            op0=ALU.max), [colr, smmr], [smmr])
                        P.op("dve", lambda e: e.reciprocal(out=smm[:, 2:3], in_=smm[:, 1:2]), [smmr], [smmr])
                        P.op("act", lambda e: e.activation(out=hb2[b], in_=bN[:, :], func=AF.Identity,
                                                           scale=smm[:, 2:3]), [bNr, smmr], [hb2r[b]])

                    def G1(c):
                        b = c % 2
                        hbb = hb2[b]; hbbr = hb2r[b]
                        hg = hg2[b]; hgr = hg2r[b]
                        P.op("dve", lambda e: e.bn_stats(out=sm[:, 2:8], in_=hbb), [hbbr, smr], [smr])
                        P.op("dve", lambda e: e.bn_aggr(out=sm[:, 8:10], in_=sm[:, 2:8]), [smr], [smr])
                        P.op("pool", lambda e: e.tensor_scalar(out=sm[:, 10:11], in0=sm[:, 9:10], scalar1=1.0, scalar2=HN_EPS, op0=ALU.mult, op1=ALU.add), [smr], [smr])
                        P.op("pool", lambda e: e.tensor_tensor(out=sm[:, 11:12], in0=sm[:, 10:11], in1=mhalf[:, 0:1],
                                                               op=ALU.pow), [smr, CONST], [smr])

                    def G1b(c):
                        b = c % 2
                        hbb = hb2[b]; hbbr = hb2r[b]
                        hg = hg2[b]; hgr = hg2r[b]
                        P.op("dve", lambda e: e.tensor_scalar(out=sm[:, 12:13], in0=sm[:, 8:9], scalar1=sm[:, 11:12],
                                                              scalar2=-1.0, op0=ALU.mult, op1=ALU.mult), [smr], [smr])
                        P.op("act", lambda e: e.activation(out=hbb, in_=hbb, func=AF.Identity, scale=sm[:, 11:12],
                                                           bias=sm[:, 12:13]), [hbbr, smr], [hbbr])
                        if l == 0 and h == 0 and c == 1:
                            dump("hb", hbb, hbbr)
                        P.op("dve", lambda e: e.scalar_tensor_tensor(out=hg, in0=sO[:, c, :], scalar=1.0, in1=hbb,
                                                                     op0=ALU.add, op1=ALU.mult), [hbbr, sOr], [hgr])

                    def G2(c):
                        b = c % 2
                        t0 = c * 128
                        hg = hg2[b]; hgr = hg2r[b]
                        bT, bTr = bank()
                        pv = bfv(bT, 4, 128)
                        tr_group([pv[:, dc, :] for dc in range(4)], [hg[:, dc * 128:(dc + 1) * 128] for dc in range(4)],
                                 ident[:, :], [hgr, CONST], bTr)
                        gst[b] = (pv, bTr)

                    def G2_ew(c):
                        b = c % 2
                        t0 = c * 128
                        pv, bTr = gst[b]
                        P.op("dve", lambda e: e.tensor_tensor(
                            out=hAT[:, h * 4:(h + 1) * 4, t0:t0 + 128], in0=pv,
                            in1=mg[:, l, h * 4:(h + 1) * 4].unsqueeze(2).broadcast_to([128, 4, 128]), op=ALU.mult),
                            [bTr, CONST], [hATr[c // 4]])

                    mst = {}
                    gst = {}
                    F(0)
                    F(1)
                    yield
                    for i in range(NT):
                        if i + 2 < NT:
                            F(i + 2)
                        U(i)
                        M(i)
                        if i >= 2:
                            G2(i - 2)
                        if i >= 1:
                            G1(i - 1)
                        M_ew(i)
                        if i >= 1:
                            G1b(i - 1)
                        if i >= 2:
                            G2_ew(i - 2)
                        yield
                    G1(NT - 1)
                    G1b(NT - 1)
                    G2(NT - 2)
                    G2_ew(NT - 2)
                    yield
                    G2(NT - 1)
                    G2_ew(NT - 1)
                    P.dma("sp", lambda e: e.dma_start(out=cst[l, h].rearrange("p (a b) -> p a b", b=512), in_=C),
                          [Cr], [], "cs")
                    if l == 0 and h == 0:
                        dump("hAT0", hAT[:, 0, :], hATr); dump("C0", C[:, 0, :], Cr)
                jobs.append((ch_load, ch_comp, 0.5))
            run_jobs()
            del jobs[:]

            P.barrier()
            WA.reset()
            BSET = [dict(sgq=WA.alloc([128, 2, T], BF16), kk=WA.alloc([128, 2, T], BF16),
                         aa=WA.alloc([128, 2, T], F32), V2=WA.alloc([128, NT, 256], BF16),
                         sG=WA.alloc([128, NT, 256], BF16), sgqr=Reg(), kkr=Reg(), aar=Reg(), V2r=Reg(), sGr=Reg())
                    for _ in range(2)]
            lga = WA.alloc([128, T], F32); lgar = Reg()
            S = WA.alloc([128, 2, 128], F32); Sr = Reg()
            Sbf2 = [WA.alloc([128, 2, 128], BF16) for _ in range(2)]; Sbf2r = [Reg(), Reg()]
            t1 = WA.alloc([128, 512], F32); t1r = Reg()
            t2 = WA.alloc([128, 512], F32); t2r = Reg()
            t3 = WA.alloc([128, 512], BF16); t3r = Reg()
            t4 = WA.alloc([128, 512], BF16); t4r = Reg()
            d1 = WA.alloc([128, 2, 128], F32); d1r = Reg()
            d2 = WA.alloc([128, 2, 128], F32); d2r = Reg()
            e0 = WA.alloc([128, 2, 128], BF16); e0r = Reg()
            e1 = WA.alloc([128, 2, 128], BF16); e1r = Reg()
            e1n = WA.alloc([128, 2, 128], BF16); e1nr = Reg()
            e2 = WA.alloc([128, 2, 128], BF16); e2r = Reg()
            q02 = [WA.alloc([128, 2, 128], BF16) for _ in range(3)]; q02r = [Reg() for _ in range(3)]
            qm2 = [WA.alloc([128, 2, 128], BF16) for _ in range(2)]; qm2r = [Reg(), Reg()]
            km2 = [WA.alloc([128, 2, 128], BF16) for _ in range(2)]; km2r = [Reg(), Reg()]
            Kh2 = [WA.alloc([128, 2, 128], BF16) for _ in range(2)]; Kh2r = [Reg(), Reg()]
            KhT2 = [WA.alloc([128, 2, 128], BF16) for _ in range(2)]; KhT2r = [Reg(), Reg()]
            scm2 = [WA.alloc([128, 2, 128], BF16) for _ in range(2)]; scm2r = [Reg(), Reg()]
            sq = WA.alloc([128, 256], F32); sqr = Reg()
            sqM = WA.alloc([128, 256], BF16); sqMr = Reg()
            hn2 = [WA.alloc([128, 2, 128], BF16) for _ in range(2)]; hn2r = [Reg(), Reg()]
            hgt2 = [WA.alloc([128, 256], BF16) for _ in range(2)]; hgt2r = [Reg(), Reg()]
            eae2 = [WA.alloc([128, 2], F32) for _ in range(3)]; eae2r = [Reg() for _ in range(3)]
            smB = [WA.alloc([128, 8], F32) for _ in range(2)]; smBr = [Reg(), Reg()]
            for g in range(4):
                def b1_load(si, g=g):
                    wdma(si, sview(si, 8, 256, 0, 512), wsrc(win, 0, 8, O_HQ + g * 256, 256))
                    wdma(si, sview(si, 8, 256, 256, 512), wsrc(win, 0, 8, O_HF + g * 256, 256))

                QS = BSET[g % 2]

                def b1_comp(si, g=g, QS=QS):
                    sgq = QS["sgq"]; kk = QS["kk"]; aa = QS["aa"]
                    sgqr = QS["sgqr"]; kkr = QS["kkr"]; aar = QS["aar"]
                    wv = sview(si, 8, 512)
                    for j in range(2):
                        hd = 2 * g + j
                        for tb in range(NB):
                            bk, br = bank()
                            mm_group(bk[:, :], [(wv[:, k, j * 128:(j + 1) * 128], XT[:, k, tb * 512:(tb + 1) * 512])
                                                for k in range(8)], [wreg[si]] + XTB(tb), br)
                            P.op("act", lambda e, bk=bk, hd=hd: e.activation(
                                out=t3, in_=bk[:, :], func=AF.Tanh, scale=0.5, bias=bcolh[:, l, 32 + hd:33 + hd]),
                                [br, CONST], [t3r])
                            P.op("act", lambda e, bk=bk, hd=hd: e.activation(
                                out=t4, in_=bk[:, :], func=AF.Identity, bias=bcol[:, l, 32 + hd:33 + hd]),
                                [br, CONST], [t4r])
                            P.op("dve", lambda e, j=j, tb=tb: e.scalar_tensor_tensor(
                                out=sgq[:, j, tb * 512:(tb + 1) * 512], in0=t3, scalar=1.0,
                                in1=t4, op0=ALU.add, op1=ALU.mult), [t3r, t4r], [sgqr])
                            bk, br = bank()
                            mm_group(bk[:, :], [(wv[:, k, 256 + j * 128: 256 + (j + 1) * 128],
                                                 XT[:, k, tb * 512:(tb + 1) * 512]) for k in range(8)],
                                     [wreg[si]] + XTB(tb), br)
                            P.op("act", lambda e, bk=bk, hd=hd: e.activation(
                                out=t1, in_=bk[:, :], func=AF.Tanh, scale=0.5, bias=bcolh[:, l, 40 + hd:41 + hd]),
                                [br, CONST], [t1r])
                            P.op("dve", lambda e, hd=hd: e.tensor_scalar(
                                out=t2, in0=t1, scalar1=lbc1[:, l, hd:hd + 1], scalar2=lbc0[:, l, hd:hd + 1],
                                op0=ALU.mult, op1=ALU.add), [t1r, CONST], [t2r])
                            P.op("act", lambda e, j=j, tb=tb: e.activation(
                                out=lga[:, tb * 512:(tb + 1) * 512], in_=t2, func=AF.Ln), [t2r], [lgar])
                            P.op("dve", lambda e, j=j, tb=tb: e.tensor_scalar(
                                out=kk[:, j, tb * 512:(tb + 1) * 512], in0=t2, scalar1=-1.0, scalar2=1.0,
                                op0=ALU.mult, op1=ALU.add), [t2r], [kkr])
                            yield
                        P.op("dve", lambda e, j=j: e.tensor_tensor_scan(
                            out=aa[:, j, :], data0=cmk[:, :], data1=lga[:, :], initial=0.0, op0=ALU.mult,
                            op1=ALU.add), [lgar, CONST], [aar])
                    if l == 0 and g == 0:
                        dump("sgq0", sgq[:, 0, :], sgqr); dump("kk0", kk[:, 0, :], kkr); dump("aa0", aa[:, 0, :], aar)
                jobs.append((b1_load, b1_comp))

                def b2_load(si, g=g):
                    wdma(si, sview(si, 8, 256, 0, 512), wsrc(win, 0, 8, O_HI + g * 256, 256))
                    wdma(si, sview(si, 8, 256, 256, 512), wsrc(win, 0, 8, O_HG + g * 256, 256))
                    P.dma("sp", lambda e: e.dma_start(
                        out=bbc[si][:, 0:256], in_=bin_[O_HI + g * 256: O_HI + (g + 1) * 256].partition_broadcast(128)),
                        [], [bbr[si]], "bb%d" % si)
                    P.dma("sp", lambda e: e.dma_start(
                        out=bbc[si][:, 256:512], in_=bin_[O_HG + g * 256: O_HG + (g + 1) * 256].partition_broadcast(128)),
                        [], [bbr[si]], "bb%d" % si)

                def b2_comp(si, g=g, QS=QS):
                    V2 = QS["V2"]; sG = QS["sG"]; V2r = QS["V2r"]; sGr = QS["sGr"]
                    wv = sview(si, 8, 512)
                    for tt in range(NT):
                        bk, br = bank()
                        mm_group(bk[:, :], [(XT[:, k, tt * 128:(tt + 1) * 128], wv[:, k, :]) for k in range(8)],
                                 [wreg[si], XTr[tt]], br)
                        P.op("dve", lambda e, bk=bk: e.tensor_tensor(out=t1, in0=bk[:, :], in1=bbc[si][:, :], op=ALU.add),
                             [br, bbr[si]], [t1r])
                        P.op("act", lambda e, tt=tt: e.activation(out=V2[:, tt, :], in_=t1[:, 0:256], func=AF.Copy),
                             [t1r], [V2r])
                        P.op("act", lambda e, tt=tt: e.activation(out=sG[:, tt, :], in_=t1[:, 256:512], func=AF.Tanh,
                                                                  scale=0.5), [t1r], [sGr])
                        yield
                jobs.append((b2_load, b2_comp))

                def bch_comp(si, g=g, QS=QS):
                    sgq = QS["sgq"]; kk = QS["kk"]; aa = QS["aa"]; V2 = QS["V2"]; sG = QS["sG"]
                    sgqr = QS["sgqr"]; kkr = QS["kkr"]; aar = QS["aar"]; V2r = QS["V2r"]; sGr = QS["sGr"]
                    if first:
                        P.op("pool", lambda e: e.memset(S, 0.0), [], [Sr])
                        P.op("pool", lambda e: e.memset(Sbf2[1], 0.0), [], [Sbf2r[1]])
                    else:
                        P.dma("sp", lambda e: e.dma_start(out=S, in_=sst[l, g].rearrange("p (a b) -> p a b", b=128)),
                              [], [Sr], "ss")
                        P.op("act", lambda e: e.activation(out=Sbf2[1], in_=S, func=AF.Copy), [Sr], [Sbf2r[1]])

                    def F1(c):
                        b = c % 2
                        b3 = c % 3
                        t0 = c * 128
                        ac = aa[:, :, t0:t0 + 128]
                        amid = aa[:, :, t0 + 63:t0 + 64].broadcast_to([128, 2, 128])
                        aend = aa[:, :, t0 + 127:t0 + 128].broadcast_to([128, 2, 128])
                        P.op("dve", lambda e: e.tensor_tensor(out=d1, in0=ac, in1=amid, op=ALU.subtract), [aar], [d1r])
                        P.op("dve", lambda e: e.tensor_tensor(out=d2, in0=ac, in1=aend, op=ALU.subtract), [aar], [d2r])
                        P.op("act", lambda e: e.activation(out=e0, in_=ac, func=AF.Exp), [aar], [e0r])
                        P.op("act", lambda e: e.activation(out=e1, in_=d1, func=AF.Exp), [d1r], [e1r])
                        P.op("act", lambda e: e.activation(out=e1n, in_=d1, func=AF.Exp, scale=-1.0), [d1r], [e1nr])
                        P.op("act", lambda e: e.activation(out=e2, in_=d2, func=AF.Exp, scale=-1.0), [d2r], [e2r])
                        P.op("act", lambda e: e.activation(out=eae2[b3], in_=aa[:, :, t0 + 127], func=AF.Exp),
                             [aar], [eae2r[b3]])

                    def F1b(c):
                        b = c % 2
                        b3 = c % 3
                        t0 = c * 128
                        P.op("dve", lambda e: e.tensor_tensor(out=q02[b3], in0=sgq[:, :, t0:t0 + 128], in1=e0,
                                                              op=ALU.mult), [sgqr, e0r], [q02r[b3]])
                        P.op("dve", lambda e: e.tensor_tensor(out=qm2[b], in0=sgq[:, :, t0:t0 + 128], in1=e1,
                                                              op=ALU.mult), [sgqr, e1r], [qm2r[b]])
                        P.op("dve", lambda e: e.tensor_tensor(out=km2[b], in0=kk[:, :, t0:t0 + 128], in1=e1n,
                                                              op=ALU.mult), [kkr, e1nr], [km2r[b]])
                        P.op("dve", lambda e: e.tensor_tensor(out=Kh2[b], in0=kk[:, :, t0:t0 + 128], in1=e2,
                                                              op=ALU.mult), [kkr, e2r], [Kh2r[b]])

                    def F2(c):
                        b = c % 2
                        bS, bSr = bank()

                        def fn(e):
                            ins = None
                            for j in range(2):
                                ins = e.matmul(bS[:, j * 128:(j + 1) * 128], lhsT=km2[b][:, j, :], rhs=qm2[b][:, j, :],
                                               start=True, stop=True)
                            return ins
                        P.op("pe", fn, [km2r[b], qm2r[b]], [bSr])
                        bK, bKr = bank()
                        pk = bfv(bK, 2, 128)
                        tr_group([pk[:, j, :] for j in range(2)], [Kh2[b][:, j, :] for j in range(2)], ident[:, :],
                                 [Kh2r[b], CONST], bKr)
                        hst["F2", b] = (bS, bSr, pk, bKr)

                    def F2_ew(c):
                        b = c % 2
                        bS, bSr, pk, bKr = hst["F2", b]
                        P.op("dve", lambda e: e.tensor_scalar(
                            out=sq, in0=bS[:, 0:256], scalar1=1e30, scalar2=-1e30, op0=ALU.min, op1=ALU.max),
                            [bSr, sqr], [sqr])
                        P.op("dve", lambda e: e.tensor_tensor(
                            out=scm2[b], in0=sq.rearrange("p (a b) -> p a b", b=128),
                            in1=mask01[:, :].unsqueeze(1).broadcast_to([128, 2, 128]), op=ALU.mult),
                            [sqr, CONST], [scm2r[b]])
                        P.op("act", lambda e: e.activation(out=KhT2[b], in_=pk, func=AF.Copy), [bKr], [KhT2r[b]])

                    def U(c):
                        b = c % 2
                        bD, bDr = bank()

                        def fn(e):
                            ins = None
                            for j in range(2):
                                ins = e.matmul(bD[:, j * 128:(j + 1) * 128], lhsT=KhT2[b][:, j, :],
                                               rhs=V2[:, c, j * 128:(j + 1) * 128], start=True, stop=True)
                            return ins
                        P.op("pe", fn, [KhT2r[b], V2r], [bDr])
                        hst["U", b] = (bD, bDr)

                    def U_ew(c):
                        b = c % 2
                        bD, bDr = hst["U", b]
                        P.op("dve", lambda e: e.tensor_tensor(
                            out=S, in0=S, in1=eae2[c % 3].unsqueeze(2).broadcast_to([128, 2, 128]), op=ALU.mult),
                            [Sr, eae2r[c % 3]], [Sr])
                        P.op("dve", lambda e: e.tensor_tensor(
                            out=S, in0=S, in1=bD[:, 0:256].rearrange("p (a b) -> p a b", b=128), op=ALU.add),
                            [Sr, bDr], [Sr])
                        P.op("act", lambda e: e.activation(out=Sbf2[b], in_=S, func=AF.Copy), [Sr], [Sbf2r[b]])

                    def M(c):
                        b = c % 2
                        pb = (c - 1) % 2
                        bO, bOr = bank()

                        def fn(e):
                            ins = None
                            for j in range(2):
                                e.matmul(bO[:, j * 128:(j + 1) * 128], lhsT=scm2[b][:, j, :],
                                         rhs=V2[:, c, j * 128:(j + 1) * 128], start=True, stop=False)
                                ins = e.matmul(bO[:, j * 128:(j + 1) * 128], lhsT=q02[c % 3][:, j, :], rhs=Sbf2[pb][:, j, :],
                                               start=False, stop=True)
                            return ins
                        P.op("pe", fn, [scm2r[b], V2r, q02r[c % 3], Sbf2r[pb]], [bOr])
                        hst["M", b] = (bO, bOr)

                    def M_ew1(c):
                        b = c % 2
                        bO, bOr = hst["M", b]
                        smb = smB[b]; smbr = smBr[b]
                        P.op("act", lambda e: e.activation(out=sqM, in_=bO[:, 0:256], func=AF.Square), [bOr], [sqMr])
                        P.op("dve", lambda e: e.tensor_reduce(
                            out=smb[:, 2:4], in_=sqM.rearrange("p (a b) -> p a b", b=128), axis=AX.X, op=ALU.add),
                            [sqMr, smbr], [smbr])
                        P.op("pool", lambda e: e.tensor_scalar(out=smb[:, 4:6], in0=smb[:, 2:4], scalar1=1.0 / 128.0,
                                                               scalar2=4.0 * HN_EPS, op0=ALU.mult, op1=ALU.add),
                             [smbr], [smbr])
                        P.op("pool", lambda e: e.tensor_tensor(out=smb[:, 6:8], in0=smb[:, 4:6], in1=mhalf[:, 0:2],
                                                               op=ALU.pow), [smbr, CONST], [smbr])

                    def M_ew2(c):
                        b = c % 2
                        bO, bOr = hst["M", b]
                        smb = smB[b]; smbr = smBr[b]
                        P.op("dve", lambda e: e.tensor_tensor(
                            out=hn2[b], in0=bO[:, 0:256].rearrange("p (a b) -> p a b", b=128),
                            in1=smb[:, 6:8].unsqueeze(2).broadcast_to([128, 2, 128]), op=ALU.mult),
                            [bOr, smbr], [hn2r[b]])

                    def G1(c):
                        b = c % 2
                        P.op("dve", lambda e: e.scalar_tensor_tensor(
                            out=hgt2[b], in0=sG[:, c, :], scalar=1.0, in1=hn2[b].rearrange("p a b -> p (a b)"),
                            op0=ALU.add, op1=ALU.mult), [hn2r[b], sGr], [hgt2r[b]])

                    def G2(c):
                        b = c % 2
                        t0 = c * 128
                        hgt = hgt2[b]; hgtr = hgt2r[b]
                        bT, bTr = bank()
                        pv = bfv(bT, 2, 128)
                        tr_group([pv[:, j, :] for j in range(2)], [hgt[:, j * 128:(j + 1) * 128] for j in range(2)],
                                 ident[:, :], [hgtr, CONST], bTr)
                        hst["G2", b] = (pv, bTr)

                    def G2_ew(c):
                        b = c % 2
                        t0 = c * 128
                        pv, bTr = hst["G2", b]
                        P.op("dve", lambda e: e.tensor_tensor(
                            out=hBT[:, 2 * g:2 * g + 2, t0:t0 + 128], in0=pv,
                            in1=hgn[:, l, 2 * g:2 * g + 2].unsqueeze(2).broadcast_to([128, 2, 128]), op=ALU.mult),
                            [bTr, CONST], [hBTr[c // 4]])

                    hst = {}
                    F1(0); F1b(0)
                    F1(1); F1b(1)
                    F2(0); F2_ew(0)
                    yield
                    for i in range(NT):
                        if i + 1 < NT:
                            F2(i + 1)
                        U(i)
                        M(i)
                        if i >= 2:
                            G2(i - 2)
                        if i + 2 < NT:
                            F1(i + 2)
                        if i + 1 < NT:
                            F2_ew(i + 1)
                        U_ew(i)
                        M_ew1(i)
                        if i + 2 < NT:
                            F1b(i + 2)
                        if i >= 1:
                            G1(i - 1)
                        if i >= 2:
                            G2_ew(i - 2)
                        M_ew2(i)
                        yield
                    G1(NT - 1)
                    G2(NT - 2); G2_ew(NT - 2)
                    yield
                    G2(NT - 1); G2_ew(NT - 1)
                    P.dma("sp", lambda e: e.dma_start(out=sst[l, g].rearrange("p (a b) -> p a b", b=128), in_=S),
                          [Sr], [], "ss")
                    if l == 0 and g == 0:
                        dump("V20", V2[:, 0, :], V2r); dump("hBT0", hBT[:, 0, :], hBTr); dump("S0", S[:, 0, :], Sr)
                jobs.append((lambda si: None, bch_comp, 1.0))
            run_jobs()
            del jobs[:]

            P.barrier()
            WA.reset()
            tmpA = WA.alloc([128, NB, 512], F32); tmpAr = Reg()
            sga = WA.alloc([128, 512], F32); sgar = Reg()
            sgb = WA.alloc([128, 512], F32); sgbr = Reg()
            for fc in range(8):
                def c1_load(si, fc=fc):
                    wdma(si, sview(si, 16, 128), wsrc(dr["w_proj_a"][l], 0, 16, fc * 128, 128))

                def c1_comp(si, fc=fc):
                    wv = sview(si, 16, 128)
                    for tb in range(NB):
                        bk, br = bank()
                        mm_group(bk[:, :], [(wv[:, kc, :], hAT[:, kc, tb * 512:(tb + 1) * 512]) for kc in range(16)],
                                 [wreg[si], hATr[tb]], br)
                        P.op("act", lambda e, bk=bk, tb=tb: e.activation(out=tmpA[:, tb, :], in_=bk[:, :], func=AF.Copy),
                             [br], [tmpAr])
                jobs.append((c1_load, c1_comp))

                def c2_load(si, fc=fc):
                    wdma(si, sview(si, 24, 128)[:, 0:8, :], wsrc(dr["w_proj_b"][l], 0, 8, fc * 128, 128))
                    wdma(si, sview(si, 24, 128)[:, 8:16, :], wsrc(win, 0, 8, O_GA + fc * 128, 128))
                    wdma(si, sview(si, 24, 128)[:, 16:24, :], wsrc(win, 0, 8, O_GB + fc * 128, 128))

                def c2_comp(si, fc=fc):
                    wv = sview(si, 24, 128)
                    for tb in range(NB):
                        xs = lambda k: XT[:, k, tb * 512:(tb + 1) * 512]
                        bk, br = bank()
                        mm_group(bk[:, :], [(wv[:, 8 + k, :], xs(k)) for k in range(8)], [wreg[si]] + XTB(tb), br)
                        P.op("act", lambda e, bk=bk: e.activation(out=sga, in_=bk[:, :], func=AF.Tanh, scale=0.5,
                                                                  bias=bcolh[:, l, 48 + fc:49 + fc]), [br, CONST], [sgar])
                        bk, br = bank()
                        mm_group(bk[:, :], [(wv[:, 16 + k, :], xs(k)) for k in range(8)], [wreg[si]] + XTB(tb), br)
                        P.op("act", lambda e, bk=bk: e.activation(out=sgb, in_=bk[:, :], func=AF.Tanh, scale=0.5,
                                                                  bias=bcolh[:, l, 56 + fc:57 + fc]), [br, CONST], [sgbr])
                        bk, br = bank()
                        mm_group(bk[:, :], [(wv[:, k, :], hBT[:, k, tb * 512:(tb + 1) * 512]) for k in range(8)],
                                 [wreg[si], hBTr[tb]], br)
                        P.op("dve", lambda e, tb=tb: e.scalar_tensor_tensor(
                            out=sga, in0=sga, scalar=1.0, in1=tmpA[:, tb, :], op0=ALU.add, op1=ALU.mult),
                            [sgar, tmpAr], [sgar])
                        P.op("dve", lambda e, bk=bk: e.scalar_tensor_tensor(
                            out=sgb, in0=sgb, scalar=1.0, in1=bk[:, :], op0=ALU.add, op1=ALU.mult),
                            [br, sgbr], [sgbr])
                        P.op("dve", lambda e, tb=tb: e.tensor_tensor(
                            out=MIXT[:, fc, tb * 512:(tb + 1) * 512], in0=sga, in1=sgb, op=ALU.add),
                            [sgar, sgbr], [MIXr[tb]])
                jobs.append((c2_load, c2_comp))
            run_jobs()
            del jobs[:]

            if l == 0:
                dump("MIXT0", MIXT[:, 0, :], MIXr)
            P.barrier()
            WA.reset()
            YT = WA.alloc([128, 8, T], F32); YTr = [Reg() for _ in range(NT)]
            lnb = WA.alloc([128, 2, D], F32); lnbr = Reg()
            xres = [WA.alloc([128, D], F32) for _ in range(2)]; xresr = [Reg(), Reg()]
            rb2 = [WA.alloc([128, D], F32) for _ in range(2)]; rb2r = [Reg(), Reg()]
            xb22 = [WA.alloc([128, D], BF16) for _ in range(2)]; xb22r = [Reg(), Reg()]
            sm32 = [WA.alloc([128, 32], F32) for _ in range(2)]; sm32r = [Reg(), Reg()]
            esg = [WA.alloc([128, 512], F32) for _ in range(2)]; esgr = [Reg(), Reg()]

            def ln_stage(gname, bname, res_src, res_srcr, res_dst, res_dstr, make_xt):
                P.dma("sp", lambda e: e.dma_start(out=lnb[:, 0, :], in_=dr[gname][l, :].partition_broadcast(128)),
                      [], [lnbr], "lnb")
                P.dma("sp", lambda e: e.dma_start(out=lnb[:, 1, :], in_=dr[bname][l, :].partition_broadcast(128)),
                      [], [lnbr], "lnb")

                def LA(tt):
                    i = tt % 2
                    rb = rb2[i]; rbr = rb2r[i]; sm3 = sm32[i]; sm3r = sm32r[i]
                    P.dma("sp", lambda e: e.dma_start(out=xres[i], in_=res_src(tt)),
                          [res_srcr[tt]] if res_srcr else [], [xresr[i]], "xr%d" % i)
                    for hf in range(2):
                        bk, br = bank()
                        tr_group([bk[:, j * 128:(j + 1) * 128] for j in range(4)],
                                 [YT[:, hf * 4 + j, tt * 128:(tt + 1) * 128] for j in range(4)], identf[:, :],
                                 [YTr[tt], CONST], br)
                        P.op("dve", lambda e, bk=bk, hf=hf: e.scalar_tensor_tensor(
                            out=rb[:, hf * 512:(hf + 1) * 512], in0=xres[i][:, hf * 512:(hf + 1) * 512], scalar=ALPHA,
                            in1=bk[:, :], op0=ALU.mult, op1=ALU.add), [br, xresr[i], rbr], [rbr])
                    for hf in range(2):
                        P.op("dve", lambda e, hf=hf: e.bn_stats(out=sm3[:, hf * 6:(hf + 1) * 6],
                                                                in_=rb[:, hf * 512:(hf + 1) * 512]), [rbr, sm3r], [sm3r])
                    P.op("dve", lambda e: e.bn_aggr(out=sm3[:, 12:14], in_=sm3[:, 0:12]), [sm3r], [sm3r])
                    P.op("pool", lambda e: e.tensor_scalar(out=sm3[:, 14:15], in0=sm3[:, 13:14], scalar1=1.0, scalar2=LN_EPS, op0=ALU.mult, op1=ALU.add), [sm3r], [sm3r])
                    P.op("pool", lambda e: e.tensor_tensor(out=sm3[:, 15:16], in0=sm3[:, 14:15], in1=mhalf[:, 0:1],
                                                           op=ALU.pow), [sm3r, CONST], [sm3r])
                    P.op("dve", lambda e: e.tensor_scalar(out=sm3[:, 16:17], in0=sm3[:, 12:13], scalar1=sm3[:, 15:16],
                                                          scalar2=-1.0, op0=ALU.mult, op1=ALU.mult), [sm3r], [sm3r])

                def LB(tt):
                    i = tt % 2
                    rb = rb2[i]; rbr = rb2r[i]; sm3 = sm32[i]; sm3r = sm32r[i]
                    P.op("act", lambda e: e.activation(out=rb, in_=rb, func=AF.Identity, scale=sm3[:, 15:16],
                                                       bias=sm3[:, 16:17]), [rbr, sm3r], [rbr])
                    P.op("dve", lambda e: e.tensor_tensor(out=rb, in0=rb, in1=lnb[:, 0, :], op=ALU.mult),
                         [rbr, lnbr], [rbr])
                    P.op("dve", lambda e: e.tensor_tensor(out=rb, in0=rb, in1=lnb[:, 1, :], op=ALU.add),
                         [rbr, lnbr], [rbr])
                    if l == 0 and tt == 0:
                        dump(gname, rb, rbr)
                    P.dma("sp", lambda e: e.dma_start(out=res_dst(tt), in_=rb), [rbr], [res_dstr[tt]], "ro%d" % i)
                    if make_xt:
                        to_xt(rb, rbr, tt, xb22[i], xb22r[i])
                LA(0)
                for tt in range(NT):
                    if tt + 1 < NT:
                        LA(tt + 1)
                    LB(tt)

            for fc in range(8):
                def d_load(si, fc=fc):
                    wdma(si, sview(si, 8, 128), wsrc(dr["w_out"][l], 0, 8, fc * 128, 128))

                def d_comp(si, fc=fc):
                    wv = sview(si, 8, 128)
                    for tb in range(NB):
                        bk, br = bank()
                        mm_group(bk[:, :], [(wv[:, k, :], MIXT[:, k, tb * 512:(tb + 1) * 512]) for k in range(8)],
                                 [wreg[si], MIXr[tb]], br)
                        P.op("act", lambda e, bk=bk, tb=tb: e.activation(
                            out=YT[:, fc, tb * 512:(tb + 1) * 512], in_=bk[:, :], func=AF.Identity, scale=0.5),
                            [br], [YTr[4 * tb + i] for i in range(4)])
                jobs.append((d_load, d_comp))
            jobs.append((lambda si: None, lambda si: ln_stage(
                "ln1_g", "ln1_b", res_in, (R2 if l > 0 else None), lambda tt: res1[tt * 128:(tt + 1) * 128, :], R1, True)))

            for jb in range(NHC // 2):
                def e_load(si, jb=jb):
                    wdma(si, sview(si, 8, 256, 0, 512), wsrc(dr["w_ffn_gate"][l], 0, 8, jb * 256, 256))
                    wdma(si, sview(si, 8, 256, 256, 512), wsrc(dr["w_ffn_up"][l], 0, 8, jb * 256, 256))

                def e_comp(si, jb=jb):
                    wv = sview(si, 8, 512)
                    for j in range(2):
                        hc = 2 * jb + j
                        for tb in range(NB):
                            bg, bgr = bank()
                            mm_group(bg[:, :], [(wv[:, k, j * 128:(j + 1) * 128], XT[:, k, tb * 512:(tb + 1) * 512])
                                                for k in range(8)], [wreg[si]] + XTB(tb), bgr)
                            bu, bur = bank()
                            mm_group(bu[:, :], [(wv[:, k, 256 + j * 128:256 + (j + 1) * 128],
                                                 XT[:, k, tb * 512:(tb + 1) * 512]) for k in range(8)],
                                     [wreg[si]] + XTB(tb), bur)
                            sg = esg[(2 * j + tb) % 2]
                            sgr = esgr[(2 * j + tb) % 2]
                            P.op("act", lambda e, bg=bg, sg=sg: e.activation(out=sg, in_=bg[:, :], func=AF.Tanh,
                                                                             scale=0.5), [bgr], [sgr])
                            P.op("dve", lambda e, bg=bg, sg=sg: e.scalar_tensor_tensor(
                                out=sg, in0=sg, scalar=1.0, in1=bg[:, :], op0=ALU.add, op1=ALU.mult), [bgr, sgr], [sgr])
                            P.op("dve", lambda e, bu=bu, sg=sg, hc=hc, tb=tb: e.tensor_tensor(
                                out=HIDT[:, hc, tb * 512:(tb + 1) * 512], in0=bu[:, :], in1=sg, op=ALU.mult),
                                [bur, sgr], [HIDr[tb]])
                jobs.append((e_load, e_comp))
            for fc in range(8):
                def f_load(si, fc=fc):
                    wdma(si, sview(si, NHC, 128), wsrc(dr["w_ffn_down"][l], 0, NHC, fc * 128, 128))

                def f_comp(si, fc=fc):
                    wv = sview(si, NHC, 128)
                    for tb in range(NB):
                        bk, br = bank()
                        mm_group(bk[:, :], [(wv[:, hc, :], HIDT[:, hc, tb * 512:(tb + 1) * 512]) for hc in range(NHC)],
                                 [wreg[si], HIDr[tb]], br)
                        P.op("act", lambda e, bk=bk, tb=tb: e.activation(
                            out=YT[:, fc, tb * 512:(tb + 1) * 512], in_=bk[:, :], func=AF.Identity, scale=0.5),
                            [br], [YTr[4 * tb + i] for i in range(4)])
                jobs.append((f_load, f_comp))
            jobs.append((lambda si: None, lambda si: ln_stage(
                "ln2_g", "ln2_b", lambda tt: res1[tt * 128:(tt + 1) * 128, :], R1, res_out, R2, not last)))
            run_jobs()
            del jobs[:]

        setup()
        for seq in range(n_seq):
            for half in range(SEQ // T):
                do_pass(seq, half)
        P.barrier(final=True)
        P.emit()
    nc._dbg_names = dstate["names"]
    return nc


_CACHE = {}


def kernel(**inputs):
    n = 8
    x = np.ascontiguousarray(inputs["x"], dtype=np.float32)
    nseq = x.shape[0] // n
    key = (nseq, DEPTH)
    if key not in _CACHE:
        _CACHE[key] = build(nseq, DEPTH)
    nc = _CACHE[key]
    shared = {k: np.ascontiguousarray(v, dtype=np.float32) for k, v in inputs.items() if k != "x"}
    in_maps = []
    for i in range(n):
        m = dict(shared)
        m["x"] = x[i * nseq:(i + 1) * nseq]
        in_maps.append(m)
    res = run_bass_kernel_spmd(nc, in_maps, core_ids=list(range(n)))
    return np.concatenate([r["out"] for r in res.results], axis=0)
```

```python
import numpy as np
from contextlib import ExitStack
import concourse.bass as bass
import concourse.mybir as mybir
from concourse.bass_utils import run_bass_kernel_spmd

F32 = mybir.dt.float32
BF16 = mybir.dt.bfloat16
AF = mybir.ActivationFunctionType
ALU = mybir.AluOpType
AX = mybir.AxisListType

D = 1024
SEQ = 2048
DEPTH = 4
T = 1024
NT = T // 128
NB = T // 512
MW = 2048
FFN = 2816
NHC = FFN // 128
ALPHA = float((2 * DEPTH) ** 0.25)
OFF = [0, 2048, 4096, 6144, 8192, 8196, 8200, 9224, 10248, 11272, 12296, 13320, 14344]
O_MQ, O_MK, O_MV, O_MO, O_MI, O_MF, O_HQ, O_HF, O_HI, O_HG, O_GA, O_GB = OFF[:12]
LN_EPS = 1e-5
HN_EPS = 1e-6
SAME_SYNC = True
DBG_HALF = 0


class Reg:
    __slots__ = ("w", "r")

    def __init__(self):
        self.w = None
        self.r = {}


class Prog:
    ENG = {"pe": "tensor", "act": "scalar", "dve": "vector", "pool": "gpsimd", "sp": "sync"}

    def __init__(self, nc, stack):
        self.nc = nc
        self.stack = stack
        self.ops = {e: [] for e in self.ENG}
        self.sem = {}
        self.cnt = {}
        self.seen = {e: {} for e in self.ENG}
        for e in ("pe", "act", "dve", "pool"):
            self.newsem("c_" + e)

    def newsem(self, name):
        self.sem[name] = self.stack.enter_context(self.nc.semaphore(name))
        self.cnt[name] = 0

    def _deps(self, eng, reads, writes):
        need = {}
        for r in reads:
            if r.w is not None and need.get(r.w[0], 0) < r.w[1]:
                need[r.w[0]] = r.w[1]
        for w in writes:
            if w.w is not None and need.get(w.w[0], 0) < w.w[1]:
                need[w.w[0]] = w.w[1]
            for s, v in w.r.items():
                if need.get(s, 0) < v:
                    need[s] = v
        waits = []
        own = "c_" + eng
        seen = self.seen[eng]
        for s, v in need.items():
            if s == own and (eng == "pe" or not SAME_SYNC):
                continue
            if seen.get(s, 0) >= v:
                continue
            seen[s] = v
            waits.append((s, v))
        return waits

    def _commit(self, tok, reads, writes):
        s, v = tok
        for r in reads:
            if r.r.get(s, 0) < v:
                r.r[s] = v
        for w in writes:
            w.w = tok
            w.r = {}

    def op(self, eng, fn, reads=(), writes=()):
        waits = self._deps(eng, reads, writes)
        s = "c_" + eng
        self.cnt[s] += 1
        self.ops[eng].append((waits, fn, (s, 1)))
        self._commit((s, self.cnt[s]), reads, writes)

    def dma(self, eng, fn, reads, writes, sem):
        if sem not in self.sem:
            self.newsem(sem)
        waits = self._deps(eng, reads, writes)
        self.cnt[sem] += 16
        self.ops[eng].append((waits, fn, (sem, 16)))
        self._commit((sem, self.cnt[sem]), reads, writes)

    def barrier(self, final=False):
        for e in self.ENG:
            waits = []
            for s, c in self.cnt.items():
                if c == 0 or (s.startswith("w") and not final):
                    continue
                if s == "c_" + e:
                    continue
                if self.seen[e].get(s, 0) < c:
                    self.seen[e][s] = c
                    waits.append((s, c))
            if waits:
                self.ops[e].append((waits, None, None))

    def emit(self):
        nc = self.nc
        with nc.Block() as block:
            for e, attr in self.ENG.items():
                def mk(e):
                    def body(engine):
                        for waits, fn, inc in self.ops[e]:
                            for s, v in waits:
                                engine.wait_ge(self.sem[s], v)
                            if fn is not None:
                                ins = fn(engine)
                                ins.then_inc(self.sem[inc[0]], inc[1])
                    return body
                getattr(block, attr)(mk(e))


class Arena:
    def __init__(self, tensor, nbytes):
        self.t = tensor
        self.nbytes = nbytes
        self.off = 0

    def reset(self, off=0):
        self.off = off

    def alloc(self, shape, dtype):
        esz = 4 if dtype == F32 else 2
        n = 1
        for s in shape[1:]:
            n *= s
        nb = (n * esz + 31) // 32 * 32
        assert self.off + nb <= self.nbytes, (self.off, nb, self.nbytes)
        a = self.off // 2
        v = self.t[0:shape[0], a:a + nb // 2]
        self.off += nb
        if dtype == F32:
            v = v.bitcast(F32)
        v = v[:, 0:n]
        if len(shape) == 3:
            v = v.rearrange("p (a b) -> p a b", b=shape[2])
        return v


def build(n_seq, n_layers, debug=None):
    nc = bass.Bass("TRN2", target_bir_lowering=False)
    dr = {}

    def din(name, shape):
        dr[name] = nc.dram_tensor(name, list(shape), F32, kind="ExternalInput").ap()
    din("x", [n_seq, SEQ, D])
    din("w_in", [DEPTH, D, 14344]); din("b_in", [DEPTH, 14344])
    din("conv_w", [DEPTH, 4, 4096]); din("conv_b", [DEPTH, 4096])
    din("m_norm_g", [DEPTH, MW]); din("lb_logits", [DEPTH, 1024]); din("h_norm_g", [DEPTH, 1024])
    din("w_proj_a", [DEPTH, MW, D]); din("w_proj_b", [DEPTH, D, D]); din("w_out", [DEPTH, D, D])
    din("ln1_g", [DEPTH, D]); din("ln1_b", [DEPTH, D])
    din("w_ffn_gate", [DEPTH, D, FFN]); din("w_ffn_up", [DEPTH, D, FFN]); din("w_ffn_down", [DEPTH, FFN, D])
    din("ln2_g", [DEPTH, D]); din("ln2_b", [DEPTH, D])
    out = nc.dram_tensor("out", [n_seq, SEQ, D], F32, kind="ExternalOutput").ap()
    res1 = nc.dram_tensor("res1", [T, D], F32, kind="Internal").ap()
    res2 = nc.dram_tensor("res2", [T, D], F32, kind="Internal").ap()
    cst = nc.dram_tensor("cst", [n_layers, 4, 128, 2048], F32, kind="Internal").ap()
    sst = nc.dram_tensor("sst", [n_layers, 4, 128, 256], F32, kind="Internal").ap()
    dbg = None
    if debug is not None:
        dbg = nc.dram_tensor("dbg", list(debug), F32, kind="ExternalOutput").ap()

    with ExitStack() as st:
        P = Prog(nc, st)

        def sb(name, shape, dt):
            return st.enter_context(nc.sbuf_tensor(name, list(shape), dt))
        XT = sb("XT", [128, 8, T], BF16); XTr = [Reg() for _ in range(NT)]
        MIXb = sb("MIX", [128, 8 * T], BF16)
        MIXT = MIXb[:, :].rearrange("p (a b) -> p a b", b=T); MIXr = [Reg() for _ in range(NB)]
        BIGb = sb("BIG", [128, 24 * T], BF16)
        hAT = BIGb[:, 0:16 * T].rearrange("p (a b) -> p a b", b=T); hATr = [Reg() for _ in range(NB)]
        hBT = BIGb[:, 16 * T:24 * T].rearrange("p (a b) -> p a b", b=T); hBTr = [Reg() for _ in range(NB)]
        HIDT = BIGb[:, 0:NHC * T].rearrange("p (a b) -> p a b", b=T); HIDr = [Reg() for _ in range(NB)]
        WORKB = 75 * 1024 + 512
        WORKt = sb("WORK", [128, WORKB // 2], BF16)
        WA = Arena(WORKt, WORKB)
        MA = Arena(MIXb, 16 * 1024)
        NSLOT = 3
        wslot = [sb("wslot%d" % i, [128, 4096], BF16) for i in range(NSLOT)]
        wreg = [Reg() for _ in range(NSLOT)]
        bbc = [sb("bbc%d" % i, [128, 512], F32) for i in range(NSLOT)]
        bbr = [Reg() for _ in range(NSLOT)]
        GAM = sb("GAM", [128, T + 2], F32); GAMr = Reg()
        gprow = sb("gprow", [4, T + 2], F32); gprr = Reg()
        ident = sb("ident", [128, 128], BF16)
        identf = sb("identf", [128, 128], F32)
        maskbig = sb("maskbig", [128, 128], F32)
        mask01 = sb("mask01", [128, 128], F32)
        ones_row = sb("ones_row", [4, 8], F32)
        ones_bf = sb("ones_bf", [128, 2], BF16)
        cmk = sb("cmk", [128, T], BF16)
        sel = sb("sel", [4, 4, 128], F32)
        i4 = sb("i4", [4, 4], F32)
        bcol = sb("bcol", [128, DEPTH, 64], F32)
        cw = sb("cw", [128, DEPTH, 32, 4], F32)
        cb = sb("cb", [128, DEPTH, 32], F32)
        mg = sb("mg", [128, DEPTH, 16], F32)
        hgn = sb("hgn", [128, DEPTH, 8], F32)
        lbl = sb("lbl", [128, 8, DEPTH], F32)
        lbp = sb("lbp", [128, 8, DEPTH], F32)
        lb = sb("lb", [128, DEPTH, 8], F32)
        oml = sb("oml", [128, DEPTH, 8], F32)
        gbias = sb("gbias", [4, DEPTH, 2], F32)
        car = sb("car", [4, DEPTH, 4], F32); carr = Reg()
        ccar = sb("ccar", [128, DEPTH, 32, 4], BF16); ccr = Reg()
        nst = sb("nst", [128, DEPTH, 4, 4], F32); nsr = Reg()
        acolA = sb("acolA", [128, NT, 4], F32)
        fcolA = sb("fcolA", [128, NT, 4], F32); colr = Reg()
        small = sb("small", [128, 64], F32)
        cbh = sb("cbh", [128, DEPTH, 32], F32)
        bcolh = sb("bcolh", [128, DEPTH, 64], F32)
        lbc0 = sb("lbc0", [128, DEPTH, 8], F32)
        lbc1 = sb("lbc1", [128, DEPTH, 8], F32)
        mhalf = sb("mhalf", [128, 8], F32)
        CONST = Reg()
        banks = [st.enter_context(nc.psum_tensor("bank%d" % i, [128, 512], F32)) for i in range(8)]
        bankr = [Reg() for _ in range(8)]
        bstate = [0]

        dstate = {"on": False, "names": []}

        def dump(name, ap, reg, p=128):
            if dbg is None or not dstate["on"] or name in dstate["names"] or len(dstate["names"]) >= dbg.shape[0]:
                return
            i = len(dstate["names"])
            dstate["names"].append(name)
            n = ap.shape[-1] if len(ap.shape) == 2 else None
            regs = reg if isinstance(reg, list) else [reg]
            P.dma("pool", lambda e: e.dma_start(out=dbg[i, 0:p, 0:n], in_=ap), regs, [], "dbg")

        def bank():
            i = bstate[0]
            bstate[0] = (i + 1) % 8
            return banks[i], bankr[i]

        def bfv(bk, a, b):
            return bk[:, :].bitcast(BF16)[:, 0:a * b].rearrange("p (a b) -> p a b", b=b)

        def setup():
            def c1(e):
                e.memset(ident[:, :], 0.0)
                e.memset(identf[:, :], 0.0)
                e.memset(maskbig[:, :], 0.0)
                e.memset(mask01[:, :], 1.0)
                e.memset(ones_row[:, :], 1.0)
                e.memset(ones_bf[:, :], 1.0)
                e.memset(cmk[:, :], 1.0)
                e.memset(sel[:, :, :], 0.0)
                e.memset(i4[:, :], 0.0)
                e.memset(car[:, :, :], 0.0)
                e.memset(ccar[:, :, :, :], 0.0)
                e.memset(nst[:, :, :, :], 0.0)
                e.memset(gprow[:, :], 0.0)
                e.memset(mhalf[:, :], -0.5)
                return e.memset(lb[:, :, :], 0.0)
            P.op("pool", c1, [], [CONST])

            def c2(e):
                e.affine_select(out=ident[:, :], in_=ident[:, :], pattern=[[-1, 128]], compare_op=ALU.not_equal,
                                fill=1.0, base=0, channel_multiplier=1)
                e.affine_select(out=identf[:, :], in_=identf[:, :], pattern=[[-1, 128]], compare_op=ALU.not_equal,
                                fill=1.0, base=0, channel_multiplier=1)
                e.affine_select(out=maskbig[:, :], in_=maskbig[:, :], pattern=[[1, 128]], compare_op=ALU.is_ge,
                                fill=30000.0, base=0, channel_multiplier=-1)
                e.affine_select(out=mask01[:, :], in_=mask01[:, :], pattern=[[1, 128]], compare_op=ALU.is_ge,
                                fill=0.0, base=0, channel_multiplier=-1)
                e.affine_select(out=sel[:, :, :], in_=sel[:, :, :], pattern=[[1, 4], [0, 128]],
                                compare_op=ALU.not_equal, fill=1.0, base=0, channel_multiplier=-1)
                e.affine_select(out=i4[:, :], in_=i4[:, :], pattern=[[1, 4]], compare_op=ALU.not_equal,
                                fill=1.0, base=0, channel_multiplier=-1)
                return e.memset(cmk[:, :].rearrange("p (c l) -> p c l", l=128)[:, :, 0:1], 0.0)
            P.op("pool", c2, [CONST], [CONST])

            segs = [(O_MQ, 16, 0), (O_MK, 16, 16), (O_HQ, 8, 32), (O_HF, 8, 40), (O_GA, 8, 48), (O_GB, 8, 56)]
            for l in range(n_layers):
                for (o, n, c0) in segs:
                    P.dma("sp", lambda e, l=l, o=o, n=n, c0=c0: e.dma_start(
                        out=bcol[:, l, c0:c0 + n], in_=dr["b_in"][l, o:o + n * 128].rearrange("(c p) -> p c", p=128),
                        allow_slow_non_contiguous=True), [], [CONST], "cst")
                for j in range(4):
                    for c8 in range(4):
                        P.dma("sp", lambda e, l=l, j=j, c8=c8: e.dma_start(
                            out=cw[:, l, c8 * 8:(c8 + 1) * 8, j],
                            in_=dr["conv_w"][l, j, c8 * 1024:(c8 + 1) * 1024].rearrange("(c p) -> p c", p=128),
                            allow_slow_non_contiguous=True), [], [CONST], "cst")
                for c8 in range(2):
                    P.dma("sp", lambda e, l=l, c8=c8: e.dma_start(
                        out=cb[:, l, c8 * 16:(c8 + 1) * 16],
                        in_=dr["conv_b"][l, c8 * 2048:(c8 + 1) * 2048].rearrange("(c p) -> p c", p=128),
                        allow_slow_non_contiguous=True), [], [CONST], "cst")
                P.dma("sp", lambda e, l=l: e.dma_start(
                    out=mg[:, l, :], in_=dr["m_norm_g"][l, :].rearrange("(c p) -> p c", p=128),
                    allow_slow_non_contiguous=True), [], [CONST], "cst")
                P.dma("sp", lambda e, l=l: e.dma_start(
                    out=hgn[:, l, :], in_=dr["h_norm_g"][l, :].rearrange("(c p) -> p c", p=128),
                    allow_slow_non_contiguous=True), [], [CONST], "cst")
                P.dma("sp", lambda e, l=l: e.dma_start(
                    out=gbias[:, l, :], in_=dr["b_in"][l, O_MI:O_MI + 8].rearrange("(g h) -> h g", h=4),
                    allow_slow_non_contiguous=True), [], [CONST], "cst")
            for l in range(DEPTH):
                P.dma("sp", lambda e, l=l: e.dma_start(
                    out=lbl[:, :, l], in_=dr["lb_logits"][l, :].rearrange("(c p) -> p c", p=128),
                    allow_slow_non_contiguous=True), [], [CONST], "cst")
            P.op("act", lambda e: e.activation(out=lbp[:, :, :], in_=lbl[:, :, :], func=AF.Exp), [CONST], [CONST])
            P.op("dve", lambda e: e.tensor_reduce(out=small[:, 0:8], in_=lbp[:, :, :], axis=AX.X, op=ALU.add),
                 [CONST], [CONST])
            P.op("dve", lambda e: e.reciprocal(out=small[:, 8:16], in_=small[:, 0:8]), [CONST], [CONST])
            P.op("dve", lambda e: e.tensor_tensor(out=lbp[:, :, :], in0=lbp[:, :, :],
                                                  in1=small[:, 8:16].unsqueeze(2).broadcast_to([128, 8, DEPTH]),
                                                  op=ALU.mult), [CONST], [CONST])
            for l in range(1, DEPTH):
                P.op("dve", lambda e, l=l: e.tensor_tensor(out=lb[:, l, :], in0=lb[:, l - 1, :], in1=lbp[:, :, l],
                                                           op=ALU.add), [CONST], [CONST])
            P.op("dve", lambda e: e.tensor_scalar(out=oml[:, :, :], in0=lb[:, :, :], scalar1=-1.0, scalar2=1.0,
                                                  op0=ALU.mult, op1=ALU.add), [CONST], [CONST])
            P.op("dve", lambda e: e.tensor_scalar(out=lbc1[:, :, :], in0=oml[:, :, :], scalar1=0.5, scalar2=None,
                                                  op0=ALU.mult), [CONST], [CONST])
            P.op("dve", lambda e: e.tensor_tensor(out=lbc0[:, :, :], in0=lb[:, :, :], in1=lbc1[:, :, :], op=ALU.add),
                 [CONST], [CONST])
            P.op("dve", lambda e: e.tensor_scalar(out=cbh[:, :, :], in0=cb[:, :, :], scalar1=0.5, scalar2=None,
                                                  op0=ALU.mult), [CONST], [CONST])
            P.op("dve", lambda e: e.tensor_scalar(out=bcolh[:, :, :], in0=bcol[:, :, :], scalar1=0.5, scalar2=None,
                                                  op0=ALU.mult), [CONST], [CONST])
            P.op("dve", lambda e: e.tensor_scalar(out=mg[:, :, :], in0=mg[:, :, :], scalar1=0.5, scalar2=None,
                                                  op0=ALU.mult), [CONST], [CONST])
            P.op("dve", lambda e: e.tensor_scalar(out=hgn[:, :, :], in0=hgn[:, :, :], scalar1=0.5, scalar2=None,
                                                  op0=ALU.mult), [CONST], [CONST])

        jobs = []

        def wdma(slot_i, view, src):
            P.dma("pool", lambda e: e.dma_start(out=view, in_=src), [], [wreg[slot_i]], "w%d" % slot_i)

        def wsrc(w2d, k0, nk, c0, ncol):
            return w2d[k0 * 128:(k0 + nk) * 128, c0:c0 + ncol].rearrange("(k p) n -> p k n", p=128)

        def sview(slot_i, nk, ncol, col0=0, tot=None):
            tot = tot or ncol
            return wslot[slot_i][:, 0:nk * tot].rearrange("p (k c) -> p k c", c=tot)[:, :, col0:col0 + ncol]

        def run_jobs():
            issued = 0
            bg = [None, 0.0, 0.0]

            def step_bg():
                if bg[0] is None:
                    return
                try:
                    next(bg[0])
                except StopIteration:
                    bg[0] = None

            def drain_bg():
                while bg[0] is not None:
                    step_bg()
            for idx in range(len(jobs)):
                while issued < len(jobs) and issued <= idx + 2:
                    jobs[issued][0](issued % NSLOT)
                    issued += 1
                job = jobs[idx]
                g = job[1](idx % NSLOT)
                if len(job) > 2:
                    drain_bg()
                    bg[0] = g
                    bg[1] = job[2]
                    bg[2] = 0.0
                    step_bg()
                    continue
                if g is None:
                    continue
                for _ in g:
                    bg[2] += bg[1]
                    while bg[2] >= 1.0:
                        bg[2] -= 1.0
                        step_bg()
            drain_bg()

        def mm_group(out_ap, pairs, rd, wr):
            n = len(pairs)

            def fn(e):
                ins = None
                for i, (a, b) in enumerate(pairs):
                    ins = e.matmul(out_ap, lhsT=a, rhs=b, start=(i == 0), stop=(i == n - 1))
                return ins
            P.op("pe", fn, rd, [wr])

        def tr_group(dst_views, src_views, idn, rd, wr):
            def fn(e):
                ins = None
                for d_, s_ in zip(dst_views, src_views):
                    ins = e.transpose(out=d_, in_=s_, identity=idn)
                return ins
            P.op("pe", fn, rd, [wr])

        def to_xt(tile_f32, treg, tt, xb, xbr):
            P.op("act", lambda e: e.activation(out=xb, in_=tile_f32, func=AF.Copy), [treg], [xbr])
            bk, br = bank()
            pv = bfv(bk, 8, 128)
            tr_group([pv[:, k, :] for k in range(8)], [xb[:, k * 128:(k + 1) * 128] for k in range(8)],
                     ident[:, :], [xbr, CONST], br)
            P.op("dve", lambda e: e.tensor_copy(out=XT[:, :, tt * 128:(tt + 1) * 128], in_=pv), [br], [XTr[tt]])

        XTB = lambda tb: [XTr[4 * tb + i] for i in range(4)]

        def do_pass(seq, half):
            tok0 = half * T
            first = (half == 0)
            dstate["on"] = (seq == 0 and half == DBG_HALF)
            P.barrier()
            WA.reset()
            xin = [WA.alloc([128, D], F32) for _ in range(2)]
            xinr = [Reg(), Reg()]
            xb = [WA.alloc([128, D], BF16) for _ in range(2)]
            xbr = [Reg(), Reg()]
            for tt in range(NT):
                i = tt % 2
                P.dma("sp", lambda e, tt=tt, i=i: e.dma_start(
                    out=xin[i], in_=dr["x"][seq, tok0 + tt * 128: tok0 + (tt + 1) * 128, :]),
                    [], [xinr[i]], "xi%d" % i)
                to_xt(xin[i], xinr[i], tt, xb[i], xbr[i])
            dump("XT0", XT[:, 0, :], list(XTr))
            for l in range(n_layers):
                last = (l == n_layers - 1)
                res_in = (lambda tt: dr["x"][seq, tok0 + tt * 128: tok0 + (tt + 1) * 128, :]) if l == 0 else \
                    (lambda tt: res2[tt * 128:(tt + 1) * 128, :])
                res_out = (lambda tt: out[seq, tok0 + tt * 128: tok0 + (tt + 1) * 128, :]) if last else \
                    (lambda tt: res2[tt * 128:(tt + 1) * 128, :])
                layer(l, first, res_in, res_out, last)

        R1 = [Reg() for _ in range(NT)]
        R2 = [Reg() for _ in range(NT)]

        def layer(l, first, res_in, res_out, last):
            win = dr["w_in"][l]
            bin_ = dr["b_in"][l]
            del jobs[:]
            P.barrier()
            WA.reset(); MA.reset()
            rowr = Reg()
            hbv = BIGb[:, 16 * T:24 * T]
            ASET = [
                dict(qT=WA.alloc([128, 4, T], BF16), kT=WA.alloc([128, 4, T], BF16),
                     V=WA.alloc([128, NT, 512], BF16), sO=WA.alloc([128, NT, 512], BF16),
                     qTr=Reg(), kTr=Reg(), Vr=Reg(), sOr=Reg(), xw=[]),
                dict(qT=MIXb[:, 0:4 * T].rearrange("p (a b) -> p a b", b=T),
                     kT=MIXb[:, 4 * T:8 * T].rearrange("p (a b) -> p a b", b=T),
                     V=hbv[:, 0:NT * 512].rearrange("p (a b) -> p a b", b=512),
                     sO=hbv[:, NT * 512:2 * NT * 512].rearrange("p (a b) -> p a b", b=512),
                     qTr=Reg(), kTr=Reg(), Vr=Reg(), sOr=Reg(), xw=[rowr]),
            ]
            ub = [WA.alloc([128, T + 4], BF16) for _ in range(2)]; ubr = [Reg(), Reg()]
            C = WA.alloc([128, 4, 512], F32); Cr = Reg()
            Cbf2 = [WA.alloc([128, 4, 512], BF16) for _ in range(2)]; Cbf2r = [Reg(), Reg()]
            nbf2 = [WA.alloc([128, 4], BF16) for _ in range(2)]; nbf2r = [Reg(), Reg()]
            dg = WA.alloc([128, 4, 128], BF16); dgr = Reg()
            tE = WA.alloc([128, 128], F32); tEr = Reg()
            tD = WA.alloc([128, 128], F32); tDr = Reg()
            tS2 = [WA.alloc([128, 128], F32) for _ in range(3)]; tS2r = [Reg() for _ in range(3)]
            tW2 = [WA.alloc([128, 128], BF16) for _ in range(3)]; tW2r = [Reg() for _ in range(3)]
            qTp2 = [WA.alloc([128, 4, 128], BF16) for _ in range(3)]; qTp2r = [Reg() for _ in range(3)]
            hb2 = [WA.alloc([128, 512], F32) for _ in range(2)]; hb2r = [Reg(), Reg()]
            hg2 = [WA.alloc([128, 512], BF16) for _ in range(2)]; hg2r = [Reg(), Reg()]
            Kp2 = [WA.alloc([128, 512], BF16) for _ in range(3)]; Kp2r = [Reg() for _ in range(3)]
            smM = [WA.alloc([128, 8], F32) for _ in range(2)]; smMr = [Reg(), Reg()]
            tmpo = WA.alloc([128, 512], F32); tmpor = Reg()
            tht = [WA.alloc([128, 512], BF16) for _ in range(2)]; thtr = [Reg(), Reg()]
            thx = [WA.alloc([128, 512], BF16) for _ in range(2)]; thxr = [Reg(), Reg()]
            sm = WA.alloc([128, 16], F32); smr = Reg()
            r_i = MA.alloc([4, T], F32); r_f = MA.alloc([4, T], F32)
            r_B = MA.alloc([4, T], F32); r_m = MA.alloc([4, T], F32)

            def g_load(si):
                wdma(si, sview(si, 8, 8), wsrc(win, 0, 8, O_MI, 8))

            def g_comp(si):
                wv = sview(si, 8, 8)
                for g, dst in ((0, r_i), (1, r_f)):
                    for tb in range(NB):
                        bk, br = bank()
                        mm_group(bk[0:4, :], [(wv[:, k, g * 4:(g + 1) * 4], XT[:, k, tb * 512:(tb + 1) * 512])
                                              for k in range(8)], [wreg[si]] + XTB(tb), br)
                        P.op("act", lambda e, bk=bk, dst=dst, tb=tb, g=g: e.activation(
                            out=dst[:, tb * 512:(tb + 1) * 512], in_=bk[0:4, :], func=AF.Identity,
                            bias=gbias[:, l, g:g + 1]), [br, CONST], [rowr])
                P.op("act", lambda e: e.activation(out=r_f, in_=r_f, func=AF.Exp, scale=-1.0), [rowr], [rowr])
                P.op("act", lambda e: e.activation(out=r_f, in_=r_f, func=AF.Ln, bias=1.0), [rowr], [rowr])
                P.op("dve", lambda e: e.tensor_scalar(out=r_f, in0=r_f, scalar1=-1.0, scalar2=None, op0=ALU.mult),
                     [rowr], [rowr])
                if first:
                    P.op("pool", lambda e: e.memset(car[:, l, :], 0.0), [carr], [carr])
                P.op("dve", lambda e: e.tensor_tensor_scan(
                    out=r_B, data0=ones_row[:, 0:1].broadcast_to([4, T]), data1=r_f, initial=car[:, l, 0:1],
                    op0=ALU.mult, op1=ALU.add), [rowr, carr, CONST], [rowr])
                P.op("dve", lambda e: e.tensor_tensor_scan(
                    out=r_m, data0=r_f, data1=r_i, initial=car[:, l, 1:2], op0=ALU.add, op1=ALU.max),
                    [rowr, carr], [rowr])
                P.op("dve", lambda e: e.tensor_copy(out=gprow[:, 1:2], in_=car[:, l, 2:3]), [carr, gprr], [gprr])
                P.op("dve", lambda e: e.tensor_tensor(out=gprow[:, 2:T + 2], in0=r_m, in1=r_B, op=ALU.subtract),
                     [rowr, gprr], [gprr])
                P.op("dve", lambda e: e.tensor_tensor(out=r_i, in0=r_i, in1=r_B, op=ALU.subtract), [rowr], [rowr])
                P.op("dve", lambda e: e.tensor_copy(out=car[:, l, 0:1], in_=r_B[:, T - 1:T]), [rowr, carr], [carr])
                P.op("dve", lambda e: e.tensor_copy(out=car[:, l, 1:2], in_=r_m[:, T - 1:T]), [rowr, carr], [carr])
                P.op("dve", lambda e: e.tensor_copy(out=car[:, l, 2:3], in_=gprow[:, T + 1:T + 2]),
                     [gprr, carr], [carr])
                for src, dstc, isf in ((r_i, acolA, False), (r_m, fcolA, True)):
                    bk, br = bank()

                    def fn(e, src=src, bk=bk):
                        ins = None
                        for c in range(NT):
                            ins = e.matmul(bk[:, c * 4:(c + 1) * 4], lhsT=src[:, c * 128:(c + 1) * 128],
                                           rhs=i4[:, :], start=True, stop=True)
                        return ins
                    P.op("pe", fn, [rowr, CONST], [br])
                    if isf:
                        P.op("act", lambda e, bk=bk, dstc=dstc: e.activation(
                            out=dstc[:, :, :], in_=bk[:, 0:NT * 4].rearrange("p (c h) -> p c h", h=4),
                            func=AF.Exp, scale=-1.0, bias=float(np.log(4.0 * np.sqrt(512.0)))), [br], [colr])
                    else:
                        P.op("act", lambda e, bk=bk, dstc=dstc: e.activation(
                            out=dstc[:, :, :], in_=bk[:, 0:NT * 4].rearrange("p (c h) -> p c h", h=4),
                            func=AF.Identity), [br], [colr])
            jobs.append((g_load, g_comp))

            for h in range(4):
                BS = ASET[h % 2]
                for which, (o_seg, dstT, dstr, bc0) in enumerate(((O_MQ, BS["qT"], BS["qTr"], 0),
                                                                  (O_MK, BS["kT"], BS["kTr"], 16))):
                    def qk_load(si, o_seg=o_seg, h=h):
                        wdma(si, sview(si, 8, 512), wsrc(win, 0, 8, o_seg + h * 512, 512))

                    def qk_comp(si, o_seg=o_seg, h=h, dstT=dstT, dstr=dstr, bc0=bc0, which=which, xw=BS["xw"]):
                        wv = sview(si, 8, 512)
                        for dc in range(4):
                            ch = which * 16 + h * 4 + dc
                            u = ub[dc % 2]; ur = ubr[dc % 2]
                            if first:
                                P.op("pool", lambda e, u=u: e.memset(u[:, 0:4], 0.0), [], [ur])
                            else:
                                P.op("act", lambda e, u=u, ch=ch: e.activation(out=u[:, 0:4], in_=ccar[:, l, ch, :],
                                                                               func=AF.Copy), [ccr], [ur])
                            for tb in range(NB):
                                bk, br = bank()
                                mm_group(bk[:, :], [(wv[:, k, dc * 128:(dc + 1) * 128],
                                                     XT[:, k, tb * 512:(tb + 1) * 512]) for k in range(8)],
                                         [wreg[si]] + XTB(tb), br)
                                P.op("act", lambda e, bk=bk, u=u, tb=tb, dc=dc: e.activation(
                                    out=u[:, 4 + tb * 512: 4 + (tb + 1) * 512], in_=bk[:, :], func=AF.Identity,
                                    bias=bcol[:, l, bc0 + h * 4 + dc: bc0 + h * 4 + dc + 1]), [br, CONST], [ur])
                            P.op("act", lambda e, u=u, ch=ch: e.activation(out=ccar[:, l, ch, :], in_=u[:, T:T + 4],
                                                                           func=AF.Copy), [ur], [ccr])
                            for j in range(4):
                                P.op("dve", lambda e, j=j, ch=ch: e.tensor_scalar(
                                    out=dg[:, j, :], in0=ident[:, :], scalar1=cw[:, l, ch, j:j + 1], scalar2=None,
                                    op0=ALU.mult), [CONST], [dgr])
                            for tb in range(NB):
                                bk, br = bank()
                                mm_group(bk[:, :], [(dg[:, j, :], u[:, 1 + j + tb * 512: 1 + j + (tb + 1) * 512])
                                                    for j in range(4)], [dgr, ur], br)
                                th = tht[tb]; thr = thtr[tb]
                                P.op("act", lambda e, bk=bk, ch=ch, th=th: e.activation(
                                    out=th, in_=bk[:, :], func=AF.Tanh, scale=0.5, bias=cbh[:, l, ch:ch + 1]),
                                    [br, CONST], [thr])
                                xp = thx[tb]; xpr = thxr[tb]
                                P.op("act", lambda e, bk=bk, ch=ch, xp=xp: e.activation(
                                    out=xp, in_=bk[:, :], func=AF.Identity, bias=cb[:, l, ch:ch + 1]),
                                    [br, CONST], [xpr])
                                P.op("dve", lambda e, tb=tb, dc=dc, th=th, xp=xp: e.scalar_tensor_tensor(
                                    out=dstT[:, dc, tb * 512:(tb + 1) * 512], in0=th, scalar=1.0,
                                    in1=xp, op0=ALU.add, op1=ALU.mult), [thr, xpr], [dstr] + xw)
                            yield
                    jobs.append((qk_load, qk_comp))
                for which, o_seg in enumerate((O_MV, O_MO)):
                    def vo_load(si, o_seg=o_seg, h=h, which=which):
                        wdma(si, sview(si, 8, 512), wsrc(win, 0, 8, o_seg + h * 512, 512))
                        P.dma("sp", lambda e: e.dma_start(
                            out=bbc[si][:, :], in_=bin_[o_seg + h * 512: o_seg + (h + 1) * 512].partition_broadcast(128)),
                            [], [bbr[si]], "bb%d" % si)

                    def vo_comp(si, which=which, V=BS["V"], Vr=BS["Vr"], sO=BS["sO"], sOr=BS["sOr"]):
                        wv = sview(si, 8, 512)
                        for tt in range(NT):
                            bk, br = bank()
                            mm_group(bk[:, :], [(XT[:, k, tt * 128:(tt + 1) * 128], wv[:, k, :]) for k in range(8)],
                                     [wreg[si], XTr[tt]], br)
                            if which == 0:
                                P.op("dve", lambda e, bk=bk, tt=tt: e.tensor_tensor(
                                    out=V[:, tt, :], in0=bk[:, :], in1=bbc[si][:, :], op=ALU.add), [br, bbr[si]], [Vr])
                            else:
                                P.op("dve", lambda e, bk=bk: e.tensor_tensor(
                                    out=tmpo, in0=bk[:, :], in1=bbc[si][:, :], op=ALU.add), [br, bbr[si]], [tmpor])
                                P.op("act", lambda e, tt=tt: e.activation(out=sO[:, tt, :], in_=tmpo, func=AF.Tanh,
                                                                          scale=0.5), [tmpor], [sOr])
                            yield
                    jobs.append((vo_load, vo_comp))
                def ch_load(si):
                    pass

                def ch_comp(si, h=h, BS=BS):
                    qT = BS["qT"]; kT = BS["kT"]; V = BS["V"]; sO = BS["sO"]
                    qTr = BS["qTr"]; kTr = BS["kTr"]; Vr = BS["Vr"]; sOr = BS["sOr"]
                    if first:
                        P.op("pool", lambda e: e.memset(C, 0.0), [], [Cr])
                        P.op("pool", lambda e: e.memset(Cbf2[1], 0.0), [], [Cbf2r[1]])
                        P.op("pool", lambda e: e.memset(nst[:, l, h, :], 0.0), [], [nsr])
                    else:
                        P.dma("sp", lambda e: e.dma_start(
                            out=C, in_=cst[l, h].rearrange("p (a b) -> p a b", b=512)), [], [Cr], "cs")
                        P.op("act", lambda e: e.activation(out=Cbf2[1], in_=C, func=AF.Copy), [Cr], [Cbf2r[1]])
                    P.op("act", lambda e: e.activation(out=nbf2[1], in_=nst[:, l, h, :], func=AF.Copy), [nsr], [nbf2r[1]])
                    for (a, b) in ((0, 512), (512, 1024), (1024, 1026)):
                        bk, br = bank()
                        mm_group(bk[:, 0:b - a], [(sel[:, h, :], gprow[:, a:b])], [gprr, CONST], br)
                        P.op("act", lambda e, bk=bk, a=a, b=b: e.activation(
                            out=GAM[:, a:b], in_=bk[:, 0:b - a], func=AF.Identity), [br], [GAMr])
                    if l == 0 and h == 0:
                        dump("qT0", qT[:, 0, :], qTr); dump("kT0", kT[:, 0, :], kTr)
                        dump("V0", V[:, 0, :], Vr); dump("sO0", sO[:, 0, :], sOr)
                        dump("GAM", GAM[:, 0:1024], GAMr); dump("acol", acolA[:, :, :].rearrange("p a b -> p (a b)"), colr)
                        dump("fcol", fcolA[:, :, :].rearrange("p a b -> p (a b)"), colr)
                        dump("gprow", gprow[:, 0:1024], gprr, p=4)

                    def F(c):
                        b = c % 3
                        t0 = c * 128
                        gs = GAM[:, 2 + t0: 2 + t0 + 128]
                        bS, bSr = bank()
                        mm_group(bS[:, 0:128], [(kT[:, dc, t0:t0 + 128], qT[:, dc, t0:t0 + 128]) for dc in range(4)],
                                 [kTr, qTr], bSr)
                        bK, bKr = bank()
                        pk = bfv(bK, 4, 128)
                        tr_group([pk[:, dc, :] for dc in range(4)], [kT[:, dc, t0:t0 + 128] for dc in range(4)],
                                 ident[:, :], [kTr, CONST], bKr)
                        P.op("act", lambda e: e.activation(
                            out=tS2[b], in_=gs, func=AF.Exp, scale=-1.0, bias=GAM[:, 1 + t0: 2 + t0]), [GAMr], [tS2r[b]])
                        P.op("dve", lambda e: e.scalar_tensor_tensor(
                            out=tE, in0=gs, scalar=acolA[:, c, h:h + 1], in1=maskbig[:, :], op0=ALU.subtract,
                            op1=ALU.max), [GAMr, colr, CONST], [tEr])
                        P.op("act", lambda e: e.activation(out=tD, in_=tE, func=AF.Exp, scale=-1.0), [tEr], [tDr])
                        P.op("dve", lambda e: e.tensor_tensor(
                            out=qTp2[b], in0=qT[:, :, t0:t0 + 128], in1=tS2[b].unsqueeze(1).broadcast_to([128, 4, 128]),
                            op=ALU.mult), [qTr, tS2r[b]], [qTp2r[b]])
                        P.op("dve", lambda e: e.tensor_tensor(out=tW2[b], in0=bS[:, 0:128], in1=tD, op=ALU.mult),
                             [bSr, tDr], [tW2r[b]])
                        P.op("act", lambda e: e.activation(
                            out=Kp2[b], in_=bK[:, :].bitcast(BF16)[:, 0:512], func=AF.Identity, scale=tD[:, 127:128]),
                            [bKr, tDr], [Kp2r[b]])

                    def U(c):
                        b = c % 2
                        kb = c % 3
                        for dc in range(4):
                            bC, bCr = bank()
                            mm_group(bC[:, :], [(Kp2[kb][:, dc * 128:(dc + 1) * 128], V[:, c, :])], [Kp2r[kb], Vr], bCr)
                            P.op("dve", lambda e, bC=bC, dc=dc: e.scalar_tensor_tensor(
                                out=C[:, dc, :], in0=C[:, dc, :], scalar=tS2[kb][:, 127:128], in1=bC[:, :], op0=ALU.mult,
                                op1=ALU.add), [bCr, tS2r[kb], Cr], [Cr])
                        P.op("act", lambda e: e.activation(out=Cbf2[b], in_=C, func=AF.Copy), [Cr], [Cbf2r[b]])
                        bn_, bnr = bank()

                        def fn(e):
                            ins = None
                            for dc in range(4):
                                ins = e.matmul(bn_[:, 2 * dc:2 * dc + 1], lhsT=Kp2[kb][:, dc * 128:(dc + 1) * 128],
                                               rhs=ones_bf[:, 0:1], start=True, stop=True)
                            return ins
                        P.op("pe", fn, [Kp2r[kb], CONST], [bnr])
                        P.op("dve", lambda e: e.scalar_tensor_tensor(
                            out=nst[:, l, h, :], in0=nst[:, l, h, :], scalar=tS2[kb][:, 127:128],
                            in1=bn_[:, 0:8].rearrange("p (a b) -> p a b", b=2)[:, :, 0], op0=ALU.mult, op1=ALU.add),
                            [bnr, tS2r[kb], nsr], [nsr])
                        P.op("act", lambda e: e.activation(out=nbf2[b], in_=nst[:, l, h, :], func=AF.Copy),
                             [nsr], [nbf2r[b]])

                    def M(c):
                        b = c % 2
                        pb = (c - 1) % 2
                        kb = c % 3
                        bN, bNr = bank()
                        mm_group(bN[:, :], [(tW2[kb], V[:, c, :])] + [(qTp2[kb][:, dc, :], Cbf2[pb][:, dc, :])
                                                                     for dc in range(4)],
                                 [tW2r[kb], Vr, qTp2r[kb], Cbf2r[pb]], bNr)
                        bD, bDr = bank()
                        mm_group(bD[:, 0:1], [(tW2[kb], ones_bf[:, 0:1])] + [(qTp2[kb][:, dc, :], nbf2[pb][:, dc:dc + 1])
                                                                           for dc in range(4)],
                                 [tW2r[kb], CONST, qTp2r[kb], nbf2r[pb]], bDr)
                        mst[b] = (bN, bNr, bD, bDr)

                    def M_ew(c):
                        b = c % 2
                        bN, bNr, bD, bDr = mst[b]
                        smm = smM[b]; smmr = smMr[b]
                        P.op("act", lambda e: e.activation(out=smm[:, 0:1], in_=bD[:, 0:1], func=AF.Abs),
                             [bDr, smmr], [smmr])
                        P.op("dve", lambda e: e.tensor_scalar(
                            out=smm[:, 1:2], in0=smm[:, 0:1], scalar1=fcolA[:, c, h:h + 1], scalar2=None,
                            op0=ALU.max), [colr, smmr], [smmr])
                        P.op("dve", lambda e: e.reciprocal(out=smm[:, 2:3], in_=smm[:, 1:2]), [smmr], [smmr])
                        P.op("act", lambda e: e.activation(out=hb2[b], in_=bN[:, :], func=AF.Identity,
                                                           scale=smm[:, 2:3]), [bNr, smmr], [hb2r[b]])

                    def G1(c):
                        b = c % 2
                        hbb = hb2[b]; hbbr = hb2r[b]
                        hg = hg2[b]; hgr = hg2r[b]
                        P.op("dve", lambda e: e.bn_stats(out=sm[:, 2:8], in_=hbb), [hbbr, smr], [smr])
                        P.op("dve", lambda e: e.bn_aggr(out=sm[:, 8:10], in_=sm[:, 2:8]), [smr], [smr])
                        P.op("pool", lambda e: e.tensor_scalar(out=sm[:, 10:11], in0=sm[:, 9:10], scalar1=1.0, scalar2=HN_EPS, op0=ALU.mult, op1=ALU.add), [smr], [smr])
                        P.op("pool", lambda e: e.tensor_tensor(out=sm[:, 11:12], in0=sm[:, 10:11], in1=mhalf[:, 0:1],
                                                               op=ALU.pow), [smr, CONST], [smr])

                    def G1b(c):
                        b = c % 2
                        hbb = hb2[b]; hbbr = hb2r[b]
                        hg = hg2[b]; hgr = hg2r[b]
                        P.op("dve", lambda e: e.tensor_scalar(out=sm[:, 12:13], in0=sm[:, 8:9], scalar1=sm[:, 11:12],
                                                              scalar2=-1.0, op0=ALU.mult, op1=ALU.mult), [smr], [smr])
                        P.op("act", lambda e: e.activation(out=hbb, in_=hbb, func=AF.Identity, scale=sm[:, 11:12],
                                                           bias=sm[:, 12:13]), [hbbr, smr], [hbbr])
                        if l == 0 and h == 0 and c == 1:
                            dump("hb", hbb, hbbr)
                        P.op("dve", lambda e: e.scalar_tensor_tensor(out=hg, in0=sO[:, c, :], scalar=1.0, in1=hbb,
                                                                     op0=ALU.add, op1=ALU.mult), [hbbr, sOr], [hgr])

                    def G2(c):
                        b = c % 2
                        t0 = c * 128
                        hg = hg2[b]; hgr = hg2r[b]
                        bT, bTr = bank()
                        pv = bfv(bT, 4, 128)
                        tr_group([pv[:, dc, :] for dc in range(4)], [hg[:, dc * 128:(dc + 1) * 128] for dc in range(4)],
                                 ident[:, :], [hgr, CONST], bTr)
                        gst[b] = (pv, bTr)

                    def G2_ew(c):
                        b = c % 2
                        t0 = c * 128
                        pv, bTr = gst[b]
                        P.op("dve", lambda e: e.tensor_tensor(
                            out=hAT[:, h * 4:(h + 1) * 4, t0:t0 + 128], in0=pv,
                            in1=mg[:, l, h * 4:(h + 1) * 4].unsqueeze(2).broadcast_to([128, 4, 128]), op=ALU.mult),
                            [bTr, CONST], [hATr[c // 4]])

                    mst = {}
                    gst = {}
                    F(0)
                    F(1)
                    yield
                    for i in range(NT):
                        if i + 2 < NT:
                            F(i + 2)
                        U(i)
                        M(i)
                        if i >= 2:
                            G2(i - 2)
                        if i >= 1:
                            G1(i - 1)
                        M_ew(i)
                        if i >= 1:
                            G1b(i - 1)
                        if i >= 2:
                            G2_ew(i - 2)
                        yield
                    G1(NT - 1)
                    G1b(NT - 1)
                    G2(NT - 2)
                    G2_ew(NT - 2)
                    yield
                    G2(NT - 1)
                    G2_ew(NT - 1)
                    P.dma("sp", lambda e: e.dma_start(out=cst[l, h].rearrange("p (a b) -> p a b", b=512), in_=C),
                          [Cr], [], "cs")
                    if l == 0 and h == 0:
                        dump("hAT0", hAT[:, 0, :], hATr); dump("C0", C[:, 0, :], Cr)
                jobs.append((ch_load, ch_comp, 0.5))
            run_jobs()
            del jobs[:]

            P.barrier()
            WA.reset()
            BSET = [dict(sgq=WA.alloc([128, 2, T], BF16), kk=WA.alloc([128, 2, T], BF16),
                         aa=WA.alloc([128, 2, T], F32), V2=WA.alloc([128, NT, 256], BF16),
                         sG=WA.alloc([128, NT, 256], BF16), sgqr=Reg(), kkr=Reg(), aar=Reg(), V2r=Reg(), sGr=Reg())
                    for _ in range(2)]
            lga = WA.alloc([128, T], F32); lgar = Reg()
            S = WA.alloc([128, 2, 128], F32); Sr = Reg()
            Sbf2 = [WA.alloc([128, 2, 128], BF16) for _ in range(2)]; Sbf2r = [Reg(), Reg()]
            t1 = WA.alloc([128, 512], F32); t1r = Reg()
            t2 = WA.alloc([128, 512], F32); t2r = Reg()
            t3 = WA.alloc([128, 512], BF16); t3r = Reg()
            t4 = WA.alloc([128, 512], BF16); t4r = Reg()
            d1 = WA.alloc([128, 2, 128], F32); d1r = Reg()
            d2 = WA.alloc([128, 2, 128], F32); d2r = Reg()
            e0 = WA.alloc([128, 2, 128], BF16); e0r = Reg()
            e1 = WA.alloc([128, 2, 128], BF16); e1r = Reg()
            e1n = WA.alloc([128, 2, 128], BF16); e1nr = Reg()
            e2 = WA.alloc([128, 2, 128], BF16); e2r = Reg()
            q02 = [WA.alloc([128, 2, 128], BF16) for _ in range(3)]; q02r = [Reg() for _ in range(3)]
            qm2 = [WA.alloc([128, 2, 128], BF16) for _ in range(2)]; qm2r = [Reg(), Reg()]
            km2 = [WA.alloc([128, 2, 128], BF16) for _ in range(2)]; km2r = [Reg(), Reg()]
            Kh2 = [WA.alloc([128, 2, 128], BF16) for _ in range(2)]; Kh2r = [Reg(), Reg()]
            KhT2 = [WA.alloc([128, 2, 128], BF16) for _ in range(2)]; KhT2r = [Reg(), Reg()]
            scm2 = [WA.alloc([128, 2, 128], BF16) for _ in range(2)]; scm2r = [Reg(), Reg()]
            sq = WA.alloc([128, 256], F32); sqr = Reg()
            sqM = WA.alloc([128, 256], BF16); sqMr = Reg()
            hn2 = [WA.alloc([128, 2, 128], BF16) for _ in range(2)]; hn2r = [Reg(), Reg()]
            hgt2 = [WA.alloc([128, 256], BF16) for _ in range(2)]; hgt2r = [Reg(), Reg()]
            eae2 = [WA.alloc([128, 2], F32) for _ in range(3)]; eae2r = [Reg() for _ in range(3)]
            smB = [WA.alloc([128, 8], F32) for _ in range(2)]; smBr = [Reg(), Reg()]
            for g in range(4):
                def b1_load(si, g=g):
                    wdma(si, sview(si, 8, 256, 0, 512), wsrc(win, 0, 8, O_HQ + g * 256, 256))
                    wdma(si, sview(si, 8, 256, 256, 512), wsrc(win, 0, 8, O_HF + g * 256, 256))

                QS = BSET[g % 2]

                def b1_comp(si, g=g, QS=QS):
                    sgq = QS["sgq"]; kk = QS["kk"]; aa = QS["aa"]
                    sgqr = QS["sgqr"]; kkr = QS["kkr"]; aar = QS["aar"]
                    wv = sview(si, 8, 512)
                    for j in range(2):
                        hd = 2 * g + j
                        for tb in range(NB):
                            bk, br = bank()
                            mm_group(bk[:, :], [(wv[:, k, j * 128:(j + 1) * 128], XT[:, k, tb * 512:(tb + 1) * 512])
                                                for k in range(8)], [wreg[si]] + XTB(tb), br)
                            P.op("act", lambda e, bk=bk, hd=hd: e.activation(
                                out=t3, in_=bk[:, :], func=AF.Tanh, scale=0.5, bias=bcolh[:, l, 32 + hd:33 + hd]),
                                [br, CONST], [t3r])
                            P.op("act", lambda e, bk=bk, hd=hd: e.activation(
                                out=t4, in_=bk[:, :], func=AF.Identity, bias=bcol[:, l, 32 + hd:33 + hd]),
                                [br, CONST], [t4r])
                            P.op("dve", lambda e, j=j, tb=tb: e.scalar_tensor_tensor(
                                out=sgq[:, j, tb * 512:(tb + 1) * 512], in0=t3, scalar=1.0,
                                in1=t4, op0=ALU.add, op1=ALU.mult), [t3r, t4r], [sgqr])
                            bk, br = bank()
                            mm_group(bk[:, :], [(wv[:, k, 256 + j * 128: 256 + (j + 1) * 128],
                                                 XT[:, k, tb * 512:(tb + 1) * 512]) for k in range(8)],
                                     [wreg[si]] + XTB(tb), br)
                            P.op("act", lambda e, bk=bk, hd=hd: e.activation(
                                out=t1, in_=bk[:, :], func=AF.Tanh, scale=0.5, bias=bcolh[:, l, 40 + hd:41 + hd]),
                                [br, CONST], [t1r])
                            P.op("dve", lambda e, hd=hd: e.tensor_scalar(
                                out=t2, in0=t1, scalar1=lbc1[:, l, hd:hd + 1], scalar2=lbc0[:, l, hd:hd + 1],
                                op0=ALU.mult, op1=ALU.add), [t1r, CONST], [t2r])
                            P.op("act", lambda e, j=j, tb=tb: e.activation(
                                out=lga[:, tb * 512:(tb + 1) * 512], in_=t2, func=AF.Ln), [t2r], [lgar])
                            P.op("dve", lambda e, j=j, tb=tb: e.tensor_scalar(
                                out=kk[:, j, tb * 512:(tb + 1) * 512], in0=t2, scalar1=-1.0, scalar2=1.0,
                                op0=ALU.mult, op1=ALU.add), [t2r], [kkr])
                            yield
                        P.op("dve", lambda e, j=j: e.tensor_tensor_scan(
                            out=aa[:, j, :], data0=cmk[:, :], data1=lga[:, :], initial=0.0, op0=ALU.mult,
                            op1=ALU.add), [lgar, CONST], [aar])
                    if l == 0 and g == 0:
                        dump("sgq0", sgq[:, 0, :], sgqr); dump("kk0", kk[:, 0, :], kkr); dump("aa0", aa[:, 0, :], aar)
                jobs.append((b1_load, b1_comp))

                def b2_load(si, g=g):
                    wdma(si, sview(si, 8, 256, 0, 512), wsrc(win, 0, 8, O_HI + g * 256, 256))
                    wdma(si, sview(si, 8, 256, 256, 512), wsrc(win, 0, 8, O_HG + g * 256, 256))
                    P.dma("sp", lambda e: e.dma_start(
                        out=bbc[si][:, 0:256], in_=bin_[O_HI + g * 256: O_HI + (g + 1) * 256].partition_broadcast(128)),
                        [], [bbr[si]], "bb%d" % si)
                    P.dma("sp", lambda e: e.dma_start(
                        out=bbc[si][:, 256:512], in_=bin_[O_HG + g * 256: O_HG + (g + 1) * 256].partition_broadcast(128)),
                        [], [bbr[si]], "bb%d" % si)

                def b2_comp(si, g=g, QS=QS):
                    V2 = QS["V2"]; sG = QS["sG"]; V2r = QS["V2r"]; sGr = QS["sGr"]
                    wv = sview(si, 8, 512)
                    for tt in range(NT):
                        bk, br = bank()
                        mm_group(bk[:, :], [(XT[:, k, tt * 128:(tt + 1) * 128], wv[:, k, :]) for k in range(8)],
                                 [wreg[si], XTr[tt]], br)
                        P.op("dve", lambda e, bk=bk: e.tensor_tensor(out=t1, in0=bk[:, :], in1=bbc[si][:, :], op=ALU.add),
                             [br, bbr[si]], [t1r])
                        P.op("act", lambda e, tt=tt: e.activation(out=V2[:, tt, :], in_=t1[:, 0:256], func=AF.Copy),
                             [t1r], [V2r])
                        P.op("act", lambda e, tt=tt: e.activation(out=sG[:, tt, :], in_=t1[:, 256:512], func=AF.Tanh,
                                                                  scale=0.5), [t1r], [sGr])
                        yield
                jobs.append((b2_load, b2_comp))

                def bch_comp(si, g=g, QS=QS):
                    sgq = QS["sgq"]; kk = QS["kk"]; aa = QS["aa"]; V2 = QS["V2"]; sG = QS["sG"]
                    sgqr = QS["sgqr"]; kkr = QS["kkr"]; aar = QS["aar"]; V2r = QS["V2r"]; sGr = QS["sGr"]
                    if first:
                        P.op("pool", lambda e: e.memset(S, 0.0), [], [Sr])
                        P.op("pool", lambda e: e.memset(Sbf2[1], 0.0), [], [Sbf2r[1]])
                    else:
                        P.dma("sp", lambda e: e.dma_start(out=S, in_=sst[l, g].rearrange("p (a b) -> p a b", b=128)),
                              [], [Sr], "ss")
                        P.op("act", lambda e: e.activation(out=Sbf2[1], in_=S, func=AF.Copy), [Sr], [Sbf2r[1]])

                    def F1(c):
                        b = c % 2
                        b3 = c % 3
                        t0 = c * 128
                        ac = aa[:, :, t0:t0 + 128]
                        amid = aa[:, :, t0 + 63:t0 + 64].broadcast_to([128, 2, 128])
                        aend = aa[:, :, t0 + 127:t0 + 128].broadcast_to([128, 2, 128])
                        P.op("dve", lambda e: e.tensor_tensor(out=d1, in0=ac, in1=amid, op=ALU.subtract), [aar], [d1r])
                        P.op("dve", lambda e: e.tensor_tensor(out=d2, in0=ac, in1=aend, op=ALU.subtract), [aar], [d2r])
                        P.op("act", lambda e: e.activation(out=e0, in_=ac, func=AF.Exp), [aar], [e0r])
                        P.op("act", lambda e: e.activation(out=e1, in_=d1, func=AF.Exp), [d1r], [e1r])
                        P.op("act", lambda e: e.activation(out=e1n, in_=d1, func=AF.Exp, scale=-1.0), [d1r], [e1nr])
                        P.op("act", lambda e: e.activation(out=e2, in_=d2, func=AF.Exp, scale=-1.0), [d2r], [e2r])
                        P.op("act", lambda e: e.activation(out=eae2[b3], in_=aa[:, :, t0 + 127], func=AF.Exp),
                             [aar], [eae2r[b3]])

                    def F1b(c):
                        b = c % 2
                        b3 = c % 3
                        t0 = c * 128
                        P.op("dve", lambda e: e.tensor_tensor(out=q02[b3], in0=sgq[:, :, t0:t0 + 128], in1=e0,
                                                              op=ALU.mult), [sgqr, e0r], [q02r[b3]])
                        P.op("dve", lambda e: e.tensor_tensor(out=qm2[b], in0=sgq[:, :, t0:t0 + 128], in1=e1,
                                                              op=ALU.mult), [sgqr, e1r], [qm2r[b]])
                        P.op("dve", lambda e: e.tensor_tensor(out=km2[b], in0=kk[:, :, t0:t0 + 128], in1=e1n,
                                                              op=ALU.mult), [kkr, e1nr], [km2r[b]])
                        P.op("dve", lambda e: e.tensor_tensor(out=Kh2[b], in0=kk[:, :, t0:t0 + 128], in1=e2,
                                                              op=ALU.mult), [kkr, e2r], [Kh2r[b]])

                    def F2(c):
                        b = c % 2
                        bS, bSr = bank()

                        def fn(e):
                            ins = None
                            for j in range(2):
                                ins = e.matmul(bS[:, j * 128:(j + 1) * 128], lhsT=km2[b][:, j, :], rhs=qm2[b][:, j, :],
                                               start=True, stop=True)
                            return ins
                        P.op("pe", fn, [km2r[b], qm2r[b]], [bSr])
                        bK, bKr = bank()
                        pk = bfv(bK, 2, 128)
                        tr_group([pk[:, j, :] for j in range(2)], [Kh2[b][:, j, :] for j in range(2)], ident[:, :],
                                 [Kh2r[b], CONST], bKr)
                        hst["F2", b] = (bS, bSr, pk, bKr)

                    def F2_ew(c):
                        b = c % 2
                        bS, bSr, pk, bKr = hst["F2", b]
                        P.op("dve", lambda e: e.tensor_scalar(
                            out=sq, in0=bS[:, 0:256], scalar1=1e30, scalar2=-1e30, op0=ALU.min, op1=ALU.max),
                            [bSr, sqr], [sqr])
                        P.op("dve", lambda e: e.tensor_tensor(
                            out=scm2[b], in0=sq.rearrange("p (a b) -> p a b", b=128),
                            in1=mask01[:, :].unsqueeze(1).broadcast_to([128, 2, 128]), op=ALU.mult),
                            [sqr, CONST], [scm2r[b]])
                        P.op("act", lambda e: e.activation(out=KhT2[b], in_=pk, func=AF.Copy), [bKr], [KhT2r[b]])

                    def U(c):
                        b = c % 2
                        bD, bDr = bank()

                        def fn(e):
                            ins = None
                            for j in range(2):
                                ins = e.matmul(bD[:, j * 128:(j + 1) * 128], lhsT=KhT2[b][:, j, :],
                                               rhs=V2[:, c, j * 128:(j + 1) * 128], start=True, stop=True)
                            return ins
                        P.op("pe", fn, [KhT2r[b], V2r], [bDr])
                        hst["U", b] = (bD, bDr)

                    def U_ew(c):
                        b = c % 2
                        bD, bDr = hst["U", b]
                        P.op("dve", lambda e: e.tensor_tensor(
                            out=S, in0=S, in1=eae2[c % 3].unsqueeze(2).broadcast_to([128, 2, 128]), op=ALU.mult),
                            [Sr, eae2r[c % 3]], [Sr])
                        P.op("dve", lambda e: e.tensor_tensor(
                            out=S, in0=S, in1=bD[:, 0:256].rearrange("p (a b) -> p a b", b=128), op=ALU.add),
                            [Sr, bDr], [Sr])
                        P.op("act", lambda e: e.activation(out=Sbf2[b], in_=S, func=AF.Copy), [Sr], [Sbf2r[b]])

                    def M(c):
                        b = c % 2
                        pb = (c - 1) % 2
                        bO, bOr = bank()

                        def fn(e):
                            ins = None
                            for j in range(2):
                                e.matmul(bO[:, j * 128:(j + 1) * 128], lhsT=scm2[b][:, j, :],
                                         rhs=V2[:, c, j * 128:(j + 1) * 128], start=True, stop=False)
                                ins = e.matmul(bO[:, j * 128:(j + 1) * 128], lhsT=q02[c % 3][:, j, :], rhs=Sbf2[pb][:, j, :],
                                               start=False, stop=True)
                            return ins
                        P.op("pe", fn, [scm2r[b], V2r, q02r[c % 3], Sbf2r[pb]], [bOr])
                        hst["M", b] = (bO, bOr)

                    def M_ew1(c):
                        b = c % 2
                        bO, bOr = hst["M", b]
                        smb = smB[b]; smbr = smBr[b]
                        P.op("act", lambda e: e.activation(out=sqM, in_=bO[:, 0:256], func=AF.Square), [bOr], [sqMr])
                        P.op("dve", lambda e: e.tensor_reduce(
                            out=smb[:, 2:4], in_=sqM.rearrange("p (a b) -> p a b", b=128), axis=AX.X, op=ALU.add),
                            [sqMr, smbr], [smbr])
                        P.op("pool", lambda e: e.tensor_scalar(out=smb[:, 4:6], in0=smb[:, 2:4], scalar1=1.0 / 128.0,
                                                               scalar2=4.0 * HN_EPS, op0=ALU.mult, op1=ALU.add),
                             [smbr], [smbr])
                        P.op("pool", lambda e: e.tensor_tensor(out=smb[:, 6:8], in0=smb[:, 4:6], in1=mhalf[:, 0:2],
                                                               op=ALU.pow), [smbr, CONST], [smbr])

                    def M_ew2(c):
                        b = c % 2
                        bO, bOr = hst["M", b]
                        smb = smB[b]; smbr = smBr[b]
                        P.op("dve", lambda e: e.tensor_tensor(
                            out=hn2[b], in0=bO[:, 0:256].rearrange("p (a b) -> p a b", b=128),
                            in1=smb[:, 6:8].unsqueeze(2).broadcast_to([128, 2, 128]), op=ALU.mult),
                            [bOr, smbr], [hn2r[b]])

                    def G1(c):
                        b = c % 2
                        P.op("dve", lambda e: e.scalar_tensor_tensor(
                            out=hgt2[b], in0=sG[:, c, :], scalar=1.0, in1=hn2[b].rearrange("p a b -> p (a b)"),
                            op0=ALU.add, op1=ALU.mult), [hn2r[b], sGr], [hgt2r[b]])

                    def G2(c):
                        b = c % 2
                        t0 = c * 128
                        hgt = hgt2[b]; hgtr = hgt2r[b]
                        bT, bTr = bank()
                        pv = bfv(bT, 2, 128)
                        tr_group([pv[:, j, :] for j in range(2)], [hgt[:, j * 128:(j + 1) * 128] for j in range(2)],
                                 ident[:, :], [hgtr, CONST], bTr)
                        hst["G2", b] = (pv, bTr)

                    def G2_ew(c):
                        b = c % 2
                        t0 = c * 128
                        pv, bTr = hst["G2", b]
                        P.op("dve", lambda e: e.tensor_tensor(
                            out=hBT[:, 2 * g:2 * g + 2, t0:t0 + 128], in0=pv,
                            in1=hgn[:, l, 2 * g:2 * g + 2].unsqueeze(2).broadcast_to([128, 2, 128]), op=ALU.mult),
                            [bTr, CONST], [hBTr[c // 4]])

                    hst = {}
                    F1(0); F1b(0)
                    F1(1); F1b(1)
                    F2(0); F2_ew(0)
                    yield
                    for i in range(NT):
                        if i + 1 < NT:
                            F2(i + 1)
                        U(i)
                        M(i)
                        if i >= 2:
                            G2(i - 2)
                        if i + 2 < NT:
                            F1(i + 2)
                        if i + 1 < NT:
                            F2_ew(i + 1)
                        U_ew(i)
                        M_ew1(i)
                        if i + 2 < NT:
                            F1b(i + 2)
                        if i >= 1:
                            G1(i - 1)
                        if i >= 2:
                            G2_ew(i - 2)
                        M_ew2(i)
                        yield
                    G1(NT - 1)
                    G2(NT - 2); G2_ew(NT - 2)
                    yield
                    G2(NT - 1); G2_ew(NT - 1)
                    P.dma("sp", lambda e: e.dma_start(out=sst[l, g].rearrange("p (a b) -> p a b", b=128), in_=S),
                          [Sr], [], "ss")
                    if l == 0 and g == 0:
                        dump("V20", V2[:, 0, :], V2r); dump("hBT0", hBT[:, 0, :], hBTr); dump("S0", S[:, 0, :], Sr)
                jobs.append((lambda si: None, bch_comp, 1.0))
            run_jobs()
            del jobs[:]

            P.barrier()
            WA.reset()
            tmpA = WA.alloc([128, NB, 512], F32); tmpAr = Reg()
            sga = WA.alloc([128, 512], F32); sgar = Reg()
            sgb = WA.alloc([128, 512], F32); sgbr = Reg()
            for fc in range(8):
                def c1_load(si, fc=fc):
                    wdma(si, sview(si, 16, 128), wsrc(dr["w_proj_a"][l], 0, 16, fc * 128, 128))

                def c1_comp(si, fc=fc):
                    wv = sview(si, 16, 128)
                    for tb in range(NB):
                        bk, br = bank()
                        mm_group(bk[:, :], [(wv[:, kc, :], hAT[:, kc, tb * 512:(tb + 1) * 512]) for kc in range(16)],
                                 [wreg[si], hATr[tb]], br)
                        P.op("act", lambda e, bk=bk, tb=tb: e.activation(out=tmpA[:, tb, :], in_=bk[:, :], func=AF.Copy),
                             [br], [tmpAr])
                jobs.append((c1_load, c1_comp))

                def c2_load(si, fc=fc):
                    wdma(si, sview(si, 24, 128)[:, 0:8, :], wsrc(dr["w_proj_b"][l], 0, 8, fc * 128, 128))
                    wdma(si, sview(si, 24, 128)[:, 8:16, :], wsrc(win, 0, 8, O_GA + fc * 128, 128))
                    wdma(si, sview(si, 24, 128)[:, 16:24, :], wsrc(win, 0, 8, O_GB + fc * 128, 128))

                def c2_comp(si, fc=fc):
                    wv = sview(si, 24, 128)
                    for tb in range(NB):
                        xs = lambda k: XT[:, k, tb * 512:(tb + 1) * 512]
                        bk, br = bank()
                        mm_group(bk[:, :], [(wv[:, 8 + k, :], xs(k)) for k in range(8)], [wreg[si]] + XTB(tb), br)
                        P.op("act", lambda e, bk=bk: e.activation(out=sga, in_=bk[:, :], func=AF.Tanh, scale=0.5,
                                                                  bias=bcolh[:, l, 48 + fc:49 + fc]), [br, CONST], [sgar])
                        bk, br = bank()
                        mm_group(bk[:, :], [(wv[:, 16 + k, :], xs(k)) for k in range(8)], [wreg[si]] + XTB(tb), br)
                        P.op("act", lambda e, bk=bk: e.activation(out=sgb, in_=bk[:, :], func=AF.Tanh, scale=0.5,
                                                                  bias=bcolh[:, l, 56 + fc:57 + fc]), [br, CONST], [sgbr])
                        bk, br = bank()
                        mm_group(bk[:, :], [(wv[:, k, :], hBT[:, k, tb * 512:(tb + 1) * 512]) for k in range(8)],
                                 [wreg[si], hBTr[tb]], br)
                        P.op("dve", lambda e, tb=tb: e.scalar_tensor_tensor(
                            out=sga, in0=sga, scalar=1.0, in1=tmpA[:, tb, :], op0=ALU.add, op1=ALU.mult),
                            [sgar, tmpAr], [sgar])
                        P.op("dve", lambda e, bk=bk: e.scalar_tensor_tensor(
                            out=sgb, in0=sgb, scalar=1.0, in1=bk[:, :], op0=ALU.add, op1=ALU.mult),
                            [br, sgbr], [sgbr])
                        P.op("dve", lambda e, tb=tb: e.tensor_tensor(
                            out=MIXT[:, fc, tb * 512:(tb + 1) * 512], in0=sga, in1=sgb, op=ALU.add),
                            [sgar, sgbr], [MIXr[tb]])
                jobs.append((c2_load, c2_comp))
            run_jobs()
            del jobs[:]

            if l == 0:
                dump("MIXT0", MIXT[:, 0, :], MIXr)
            P.barrier()
            WA.reset()
            YT = WA.alloc([128, 8, T], F32); YTr = [Reg() for _ in range(NT)]
            lnb = WA.alloc([128, 2, D], F32); lnbr = Reg()
            xres = [WA.alloc([128, D], F32) for _ in range(2)]; xresr = [Reg(), Reg()]
            rb2 = [WA.alloc([128, D], F32) for _ in range(2)]; rb2r = [Reg(), Reg()]
            xb22 = [WA.alloc([128, D], BF16) for _ in range(2)]; xb22r = [Reg(), Reg()]
            sm32 = [WA.alloc([128, 32], F32) for _ in range(2)]; sm32r = [Reg(), Reg()]
            esg = [WA.alloc([128, 512], F32) for _ in range(2)]; esgr = [Reg(), Reg()]

            def ln_stage(gname, bname, res_src, res_srcr, res_dst, res_dstr, make_xt):
                P.dma("sp", lambda e: e.dma_start(out=lnb[:, 0, :], in_=dr[gname][l, :].partition_broadcast(128)),
                      [], [lnbr], "lnb")
                P.dma("sp", lambda e: e.dma_start(out=lnb[:, 1, :], in_=dr[bname][l, :].partition_broadcast(128)),
                      [], [lnbr], "lnb")

                def LA(tt):
                    i = tt % 2
                    rb = rb2[i]; rbr = rb2r[i]; sm3 = sm32[i]; sm3r = sm32r[i]
                    P.dma("sp", lambda e: e.dma_start(out=xres[i], in_=res_src(tt)),
                          [res_srcr[tt]] if res_srcr else [], [xresr[i]], "xr%d" % i)
                    for hf in range(2):
                        bk, br = bank()
                        tr_group([bk[:, j * 128:(j + 1) * 128] for j in range(4)],
                                 [YT[:, hf * 4 + j, tt * 128:(tt + 1) * 128] for j in range(4)], identf[:, :],
                                 [YTr[tt], CONST], br)
                        P.op("dve", lambda e, bk=bk, hf=hf: e.scalar_tensor_tensor(
                            out=rb[:, hf * 512:(hf + 1) * 512], in0=xres[i][:, hf * 512:(hf + 1) * 512], scalar=ALPHA,
                            in1=bk[:, :], op0=ALU.mult, op1=ALU.add), [br, xresr[i], rbr], [rbr])
                    for hf in range(2):
                        P.op("dve", lambda e, hf=hf: e.bn_stats(out=sm3[:, hf * 6:(hf + 1) * 6],
                                                                in_=rb[:, hf * 512:(hf + 1) * 512]), [rbr, sm3r], [sm3r])
                    P.op("dve", lambda e: e.bn_aggr(out=sm3[:, 12:14], in_=sm3[:, 0:12]), [sm3r], [sm3r])
                    P.op("pool", lambda e: e.tensor_scalar(out=sm3[:, 14:15], in0=sm3[:, 13:14], scalar1=1.0, scalar2=LN_EPS, op0=ALU.mult, op1=ALU.add), [sm3r], [sm3r])
                    P.op("pool", lambda e: e.tensor_tensor(out=sm3[:, 15:16], in0=sm3[:, 14:15], in1=mhalf[:, 0:1],
                                                           op=ALU.pow), [sm3r, CONST], [sm3r])

                def LA2(tt):
                    i = tt % 2
                    sm3 = sm32[i]; sm3r = sm32r[i]
                    P.op("dve", lambda e: e.tensor_scalar(out=sm3[:, 16:17], in0=sm3[:, 12:13], scalar1=sm3[:, 15:16],
                                                          scalar2=-1.0, op0=ALU.mult, op1=ALU.mult), [sm3r], [sm3r])

                def LB(tt):
                    i = tt % 2
                    rb = rb2[i]; rbr = rb2r[i]; sm3 = sm32[i]; sm3r = sm32r[i]
                    P.op("act", lambda e: e.activation(out=rb, in_=rb, func=AF.Identity, scale=sm3[:, 15:16],
                                                       bias=sm3[:, 16:17]), [rbr, sm3r], [rbr])
                    P.op("dve", lambda e: e.tensor_tensor(out=rb, in0=rb, in1=lnb[:, 0, :], op=ALU.mult),
                         [rbr, lnbr], [rbr])
                    P.op("dve", lambda e: e.tensor_tensor(out=rb, in0=rb, in1=lnb[:, 1, :], op=ALU.add),
                         [rbr, lnbr], [rbr])
                    if l == 0 and tt == 0:
                        dump(gname, rb, rbr)
                    P.dma("sp", lambda e: e.dma_start(out=res_dst(tt), in_=rb), [rbr], [res_dstr[tt]], "ro%d" % i)
                    if make_xt:
                        to_xt(rb, rbr, tt, xb22[i], xb22r[i])
                LA(0)
                LA2(0)
                for tt in range(NT):
                    if tt + 1 < NT:
                        LA(tt + 1)
                    LB(tt)
                    if tt + 1 < NT:
                        LA2(tt + 1)

            for fc in range(8):
                def d_load(si, fc=fc):
                    wdma(si, sview(si, 8, 128), wsrc(dr["w_out"][l], 0, 8, fc * 128, 128))

                def d_comp(si, fc=fc):
                    wv = sview(si, 8, 128)
                    for tb in range(NB):
                        bk, br = bank()
                        mm_group(bk[:, :], [(wv[:, k, :], MIXT[:, k, tb * 512:(tb + 1) * 512]) for k in range(8)],
                                 [wreg[si], MIXr[tb]], br)
                        P.op("act", lambda e, bk=bk, tb=tb: e.activation(
                            out=YT[:, fc, tb * 512:(tb + 1) * 512], in_=bk[:, :], func=AF.Identity, scale=0.5),
                            [br], [YTr[4 * tb + i] for i in range(4)])
                jobs.append((d_load, d_comp))
            jobs.append((lambda si: None, lambda si: ln_stage(
                "ln1_g", "ln1_b", res_in, (R2 if l > 0 else None), lambda tt: res1[tt * 128:(tt + 1) * 128, :], R1, True)))

            for jb in range(NHC // 2):
                def e_load(si, jb=jb):
                    wdma(si, sview(si, 8, 256, 0, 512), wsrc(dr["w_ffn_gate"][l], 0, 8, jb * 256, 256))
                    wdma(si, sview(si, 8, 256, 256, 512), wsrc(dr["w_ffn_up"][l], 0, 8, jb * 256, 256))

                def e_comp(si, jb=jb):
                    wv = sview(si, 8, 512)
                    for j in range(2):
                        hc = 2 * jb + j
                        for tb in range(NB):
                            bg, bgr = bank()
                            mm_group(bg[:, :], [(wv[:, k, j * 128:(j + 1) * 128], XT[:, k, tb * 512:(tb + 1) * 512])
                                                for k in range(8)], [wreg[si]] + XTB(tb), bgr)
                            bu, bur = bank()
                            mm_group(bu[:, :], [(wv[:, k, 256 + j * 128:256 + (j + 1) * 128],
                                                 XT[:, k, tb * 512:(tb + 1) * 512]) for k in range(8)],
                                     [wreg[si]] + XTB(tb), bur)
                            sg = esg[(2 * j + tb) % 2]
                            sgr = esgr[(2 * j + tb) % 2]
                            P.op("act", lambda e, bg=bg, sg=sg: e.activation(out=sg, in_=bg[:, :], func=AF.Tanh,
                                                                             scale=0.5), [bgr], [sgr])
                            P.op("dve", lambda e, bg=bg, sg=sg: e.scalar_tensor_tensor(
                                out=sg, in0=sg, scalar=1.0, in1=bg[:, :], op0=ALU.add, op1=ALU.mult), [bgr, sgr], [sgr])
                            P.op("dve", lambda e, bu=bu, sg=sg, hc=hc, tb=tb: e.tensor_tensor(
                                out=HIDT[:, hc, tb * 512:(tb + 1) * 512], in0=bu[:, :], in1=sg, op=ALU.mult),
                                [bur, sgr], [HIDr[tb]])
                jobs.append((e_load, e_comp))
            for fc in range(8):
                def f_load(si, fc=fc):
                    wdma(si, sview(si, NHC, 128), wsrc(dr["w_ffn_down"][l], 0, NHC, fc * 128, 128))

                def f_comp(si, fc=fc):
                    wv = sview(si, NHC, 128)
                    for tb in range(NB):
                        bk, br = bank()
                        mm_group(bk[:, :], [(wv[:, hc, :], HIDT[:, hc, tb * 512:(tb + 1) * 512]) for hc in range(NHC)],
                                 [wreg[si], HIDr[tb]], br)
                        P.op("act", lambda e, bk=bk, tb=tb: e.activation(
                            out=YT[:, fc, tb * 512:(tb + 1) * 512], in_=bk[:, :], func=AF.Identity, scale=0.5),
                            [br], [YTr[4 * tb + i] for i in range(4)])
                jobs.append((f_load, f_comp))
            jobs.append((lambda si: None, lambda si: ln_stage(
                "ln2_g", "ln2_b", lambda tt: res1[tt * 128:(tt + 1) * 128, :], R1, res_out, R2, not last)))
            run_jobs()
            del jobs[:]

        setup()
        for seq in range(n_seq):
            for half in range(SEQ // T):
                do_pass(seq, half)
        P.barrier(final=True)
        P.emit()
    nc._dbg_names = dstate["names"]
    return nc


_CACHE = {}


def kernel(**inputs):
    n = 8
    x = np.ascontiguousarray(inputs["x"], dtype=np.float32)
    nseq = x.shape[0] // n
    key = (nseq, DEPTH)
    if key not in _CACHE:
        _CACHE[key] = build(nseq, DEPTH)
    nc = _CACHE[key]
    shared = {k: np.ascontiguousarray(v, dtype=np.float32) for k, v in inputs.items() if k != "x"}
    in_maps = []
    for i in range(n):
        m = dict(shared)
        m["x"] = x[i * nseq:(i + 1) * nseq]
        in_maps.append(m)
    res = run_bass_kernel_spmd(nc, in_maps, core_ids=list(range(n)))
    return np.concatenate([r["out"] for r in res.results], axis=0)
```
